# Optimizing a Trainium2 kernel written in Bass

```python
import jax, jax.numpy as jnp
from jax import lax
import numpy as np

D_MODEL = 4096
BATCH = 4
SEQ = 4096
DEPTH = 4

MEM_LEN = 256
GRID_W = 64
N_BRANCH = 4
BRANCH_W = D_MODEL // N_BRANCH

RWKV_HEAD = 64
RWKV_HEADS = BRANCH_W // RWKV_HEAD
LORA_W = 64
LORA_A = 64
SHIFT_TAPS = 3
LNX_EPS = 64e-5

CHUNK = 128
SG_GROUPS = 8
SG_CH = BRANCH_W // SG_GROUPS

NA_HEAD = 64
NA_HEADS = BRANCH_W // NA_HEAD
MAX_WIN_ROWS = 8
WIN_COLS = 16

MEM_HEADS = 4
MEM_HEAD = BRANCH_W // MEM_HEADS

A_SHIFT_W = 3 * BRANCH_W + 2 * LORA_W + 2 * LORA_A
A_W = A_SHIFT_W + BRANCH_W
B_W = 3 * BRANCH_W
C_W = 4 * BRANCH_W
M_W = 2 * BRANCH_W
GATE_W = N_BRANCH * D_MODEL
IN_W = A_W + B_W + C_W + M_W + GATE_W

NEG_INF = -1e30

kernel_name = 'hybrid_bidir_rwkv7_gmlp_natten_memxattn'


def _rms_norm(x, g, eps=1e-6):
    xf = x.astype(jnp.float32)
    y = xf * lax.rsqrt(jnp.mean(xf * xf, axis=-1, keepdims=True) + eps)
    return y.astype(x.dtype) * g


def _layer_norm(x, g, b, eps=1e-5):
    xf = x.astype(jnp.float32)
    mu = jnp.mean(xf, axis=-1, keepdims=True)
    var = jnp.mean(jnp.square(xf - mu), axis=-1, keepdims=True)
    return ((xf - mu) * lax.rsqrt(var + eps)).astype(x.dtype) * g + b


def _centred_dwconv(x, w):
    K = w.shape[0]
    pad = K // 2
    T = x.shape[1]
    xp = jnp.pad(x, ((0, 0), (pad, pad), (0, 0)))
    out = xp[:, 0:T] * w[0]
    for j in range(1, K):
        out = out + xp[:, j:j + T] * w[j]
    return out


def _wkv7_bidirectional(r, decay, k_t, v, kk, a):
    f32 = jnp.float32
    r, decay, k_t, v, kk, a = (t.astype(f32) for t in (r, decay, k_t, v, kk, a))
    B, T, H, N = r.shape
    z = -kk
    b = kk[:, :, None] * a

    def dirs(x_f, x_b):
        return jnp.moveaxis(jnp.stack([x_f, jnp.flip(x_b, axis=1)], axis=0), 2, 0)

    seq = (dirs(r, r), dirs(decay[:, :, 0], decay[:, :, 1]), dirs(k_t[:, :, 0], k_t[:, :, 1]),
           dirs(v, v), dirs(z, z), dirs(b[:, :, 0], b[:, :, 1]))

    def step(S, inp):
        r_t, w_t, k_tt, v_t, z_t, b_t = inp
        Sz = jnp.einsum('dbhij,dbhj->dbhi', S, z_t)
        S = S * w_t[..., None, :] + Sz[..., :, None] * b_t[..., None, :] + v_t[..., :, None] * k_tt[..., None, :]
        return S, jnp.einsum('dbhij,dbhj->dbhi', S, r_t)

    S0 = jnp.zeros((2, B, H, N, N), f32)
    _, ys = lax.scan(step, S0, seq)
    y = ys[:, 0] + jnp.flip(ys[:, 1], axis=0)
    return jnp.moveaxis(y, 0, 1)


def _rwkv7_branch(xn, w_a, conv, w_up, w0, a_up, a0, k_k, k_a, r_k, lnx_w, lnx_b):
    B, T, _ = xn.shape
    BW = BRANCH_W
    h = xn @ w_a
    hs = _centred_dwconv(h[..., :A_SHIFT_W], conv)
    g = h[..., A_SHIFT_W:]
    r, k, v, wd, ad = jnp.split(hs, [BW, 2 * BW, 3 * BW, 3 * BW + 2 * LORA_W], axis=-1)
    wd = wd.reshape(B, T, 2, LORA_W)
    ad = ad.reshape(B, T, 2, LORA_A)
    w_raw = (w0 + jnp.einsum('btzr,zrc->btzc', jnp.tanh(wd), w_up)).astype(jnp.float32)
    decay = jnp.exp(-jnp.exp(-jax.nn.softplus(-w_raw) - 0.5))
    a = jax.nn.sigmoid(a0 + jnp.einsum('btzr,zrc->btzc', ad, a_up))
    heads = lambda t: t.reshape(*t.shape[:-1], RWKV_HEADS, RWKV_HEAD)
    kk = heads(k * k_k).astype(jnp.float32)
    kk = kk / jnp.maximum(jnp.sqrt(jnp.sum(kk * kk, axis=-1, keepdims=True)), 1e-12)
    k_t = k[:, :, None, :] * (1.0 + (a - 1.0) * k_a)
    wkv = _wkv7_bidirectional(heads(r), heads(decay), heads(k_t), heads(v), kk, heads(a))
    mu = jnp.mean(wkv, axis=-1, keepdims=True)
    var = jnp.mean(jnp.square(wkv - mu), axis=-1, keepdims=True)
    gn = ((wkv - mu) * lax.rsqrt(var + LNX_EPS)).reshape(B, T, BW).astype(xn.dtype) * lnx_w + lnx_b
    bonus = jnp.einsum('bthn,btzhn,hn->bth', heads(r), heads(k_t), r_k)[..., None] * heads(v)
    y = gn + bonus.reshape(B, T, BW)
    return y * jax.nn.silu(g)


def _spatial_gating_branch(xn, w_b, ln_g, ln_b, w_s, b_s):
    B, T, _ = xn.shape
    u, v, g = jnp.split(xn @ w_b, 3, axis=-1)
    u = jax.nn.gelu(u)
    v = _layer_norm(jax.nn.gelu(v), ln_g, ln_b)
    vc = v.reshape(B, T // CHUNK, CHUNK, SG_GROUPS, SG_CH)
    sv = jnp.einsum('gpq,bnqgc->bnpgc', w_s, vc) + b_s.T[:, :, None]
    return u * sv.reshape(B, T, BRANCH_W) * jax.nn.silu(g)


def _natten_col_tables():
    n_cb = GRID_W // WIN_COLS
    span = 2 * WIN_COLS
    cs = np.clip(WIN_COLS * np.arange(n_cb) - WIN_COLS // 2, 0, GRID_W - span)
    col_idx = cs[:, None] + np.arange(span)[None, :]
    qcol = (WIN_COLS * np.arange(n_cb))[:, None] + np.arange(WIN_COLS)[None, :]
    sj = np.clip(qcol - WIN_COLS // 2, 0, GRID_W - WIN_COLS)
    kc = col_idx[:, None, :]
    valid = (kc >= sj[..., None]) & (kc < sj[..., None] + WIN_COLS)
    dc = np.clip(kc - qcol[..., None], -(WIN_COLS - 1), WIN_COLS - 1) + WIN_COLS - 1
    return col_idx, valid, dc


def _neighbourhood_attention(q, k, v, rpb):
    B, T, H, d = q.shape
    rows = T // GRID_W
    kh = min(MAX_WIN_ROWS, rows)
    col_idx, valid, dc = _natten_col_tables()
    n_cb, span = col_idx.shape
    to_grid = lambda t: t.reshape(B, rows, GRID_W, H, d).transpose(0, 3, 1, 2, 4)
    qg, kg, vg = to_grid(q), to_grid(k), to_grid(v)
    scale = d ** -0.5

    def row_fn(i):
        si = jnp.clip(i - kh // 2, 0, rows - kh)
        k_rows = lax.dynamic_slice_in_dim(kg, si, kh, axis=2)
        v_rows = lax.dynamic_slice_in_dim(vg, si, kh, axis=2)
        k_blk = k_rows[:, :, :, col_idx]
        v_blk = v_rows[:, :, :, col_idx]
        q_i = lax.dynamic_index_in_dim(qg, i, axis=2, keepdims=False).reshape(B, H, n_cb, WIN_COLS, d)
        s = jnp.einsum('bhcpd,bhrcmd->bhcprm', q_i, k_blk).astype(jnp.float32) * scale
        rel_r = si + jnp.arange(kh) - i + (MAX_WIN_ROWS - 1)
        bias = rpb[:, rel_r][:, :, dc].transpose(0, 2, 3, 1, 4)
        s = jnp.where(valid[:, :, None, :], s + bias.astype(jnp.float32), NEG_INF)
        p = jax.nn.softmax(s.reshape(B, H, n_cb, WIN_COLS, kh * span), axis=-1)
        p = p.reshape(s.shape).astype(v.dtype)
        o = jnp.einsum('bhcprm,bhrcmd->bhcpd', p, v_blk)
        return o.reshape(B, H, GRID_W, d)

    out = lax.map(row_fn, jnp.arange(rows))
    return out.transpose(1, 0, 3, 2, 4).reshape(B, T, H * d)


def _neighbourhood_branch(xn, w_c, q_norm, k_norm, rpb):
    B, T, _ = xn.shape
    q, k, v, g = jnp.split(xn @ w_c, 4, axis=-1)
    heads = lambda t: t.reshape(B, T, NA_HEADS, NA_HEAD)
    q = _rms_norm(heads(q), q_norm)
    k = _rms_norm(heads(k), k_norm)
    o = _neighbourhood_attention(q, k, heads(v), rpb)
    return o * jax.nn.silu(g)


def _memory_branch(xn, mem, w_m, m_norm_g, w_kv, q_norm, k_norm):
    B, T, _ = xn.shape
    M = mem.shape[1]
    q, g = jnp.split(xn @ w_m, 2, axis=-1)
    q = _rms_norm(q.reshape(B, T, MEM_HEADS, MEM_HEAD), q_norm)
    k, v = jnp.split(_rms_norm(mem, m_norm_g) @ w_kv, 2, axis=-1)
    k = _rms_norm(k.reshape(B, M, MEM_HEADS, MEM_HEAD), k_norm)
    v = v.reshape(B, M, MEM_HEADS, MEM_HEAD)
    s = jnp.einsum('bthd,bmhd->bhtm', q, k).astype(jnp.float32) * (MEM_HEAD ** -0.5)
    p = jax.nn.softmax(s, axis=-1).astype(v.dtype)
    o = jnp.einsum('bhtm,bmhd->bthd', p, v).reshape(B, T, BRANCH_W)
    return o * jax.nn.silu(g)


def setup_inputs(seed: int = 0) -> dict:
    key = jax.random.key(seed)
    ks = jax.random.split(key, 27)
    f32 = jnp.float32
    L, BW = DEPTH, BRANCH_W
    nrm = lambda k, shape, s: jax.random.normal(k, shape, f32) * s
    return {
        'x': nrm(ks[0], (BATCH, SEQ, D_MODEL), 1.0),
        'mem': nrm(ks[1], (BATCH, MEM_LEN, D_MODEL), 1.0),
        'norm_g': 1.0 + nrm(ks[2], (L, D_MODEL), 0.02),
        'w_in': nrm(ks[3], (L, D_MODEL, IN_W), D_MODEL ** -0.5),
        'a_conv': jnp.array([0.25, 0.5, 0.25], f32)[None, :, None] + nrm(ks[4], (L, SHIFT_TAPS, A_SHIFT_W), 0.05),
        'a_w_up': nrm(ks[5], (L, 2, LORA_W, BW), 0.1),
        'a_w0': jax.random.uniform(ks[6], (L, 2, BW), f32, minval=-6.0, maxval=0.0),
        'a_a_up': nrm(ks[7], (L, 2, LORA_A, BW), LORA_A ** -0.5),
        'a_a0': nrm(ks[8], (L, 2, BW), 0.1),
        'a_k_k': 0.85 + nrm(ks[9], (L, BW), 0.05),
        'a_k_a': 1.0 + nrm(ks[10], (L, BW), 0.05),
        'a_r_k': nrm(ks[11], (L, RWKV_HEADS, RWKV_HEAD), 0.1),
        'a_lnx_w': 1.0 + nrm(ks[12], (L, BW), 0.02),
        'a_lnx_b': nrm(ks[13], (L, BW), 0.02),
        'b_ln_g': 1.0 + nrm(ks[14], (L, BW), 0.02),
        'b_ln_b': nrm(ks[15], (L, BW), 0.02),
        'b_w_s': nrm(ks[16], (L, SG_GROUPS, CHUNK, CHUNK), CHUNK ** -0.5),
        'b_b_s': 1.0 + nrm(ks[17], (L, SG_GROUPS, CHUNK), 0.1),
        'c_q_norm': 1.0 + nrm(ks[18], (L, NA_HEAD), 0.02),
        'c_k_norm': 1.0 + nrm(ks[19], (L, NA_HEAD), 0.02),
        'c_rpb': nrm(ks[20], (L, NA_HEADS, 2 * MAX_WIN_ROWS - 1, 2 * WIN_COLS - 1), 0.1),
        'm_norm_g': 1.0 + nrm(ks[21], (L, D_MODEL), 0.02),
        'm_w_kv': nrm(ks[22], (L, D_MODEL, 2 * BW), D_MODEL ** -0.5),
        'm_q_norm': 1.0 + nrm(ks[23], (L, MEM_HEAD), 0.02),
        'm_k_norm': 1.0 + nrm(ks[24], (L, MEM_HEAD), 0.02),
        'w_branch': nrm(ks[25], (L, N_BRANCH, BW, D_MODEL), BW ** -0.5),
        'w_out': nrm(ks[26], (L, D_MODEL, D_MODEL), (2 * D_MODEL) ** -0.5),
    }


def reference(x, mem, norm_g, w_in, a_conv, a_w_up, a_w0, a_a_up, a_a0, a_k_k, a_k_a, a_r_k,
              a_lnx_w, a_lnx_b, b_ln_g, b_ln_b, b_w_s, b_b_s, c_q_norm, c_k_norm, c_rpb,
              m_norm_g, m_w_kv, m_q_norm, m_k_norm, w_branch, w_out):
    o1 = A_W
    o2 = o1 + B_W
    o3 = o2 + C_W
    o4 = o3 + M_W
    for l in range(DEPTH):
        xn = _rms_norm(x, norm_g[l])
        w = w_in[l]
        ys = (
            _rwkv7_branch(xn, w[:, :o1], a_conv[l], a_w_up[l], a_w0[l], a_a_up[l], a_a0[l],
                          a_k_k[l], a_k_a[l], a_r_k[l], a_lnx_w[l], a_lnx_b[l]),
            _spatial_gating_branch(xn, w[:, o1:o2], b_ln_g[l], b_ln_b[l], b_w_s[l], b_b_s[l]),
            _neighbourhood_branch(xn, w[:, o2:o3], c_q_norm[l], c_k_norm[l], c_rpb[l]),
            _memory_branch(xn, mem, w[:, o3:o4], m_norm_g[l], m_w_kv[l], m_q_norm[l], m_k_norm[l]),
        )
        merged = jnp.zeros_like(x)
        for n in range(N_BRANCH):
            gate = jax.nn.sigmoid(xn @ w[:, o4 + n * D_MODEL:o4 + (n + 1) * D_MODEL])
            merged = merged + gate * (ys[n] @ w_branch[l, n])
        x = x + merged @ w_out[l]
    return x
```

```python
import numpy as np
from contextlib import ExitStack

import concourse.bass as bass
import concourse.mybir as mybir
from concourse.bass_utils import run_bass_kernel_spmd

F32 = mybir.dt.float32
BF16 = mybir.dt.bfloat16
AF = mybir.ActivationFunctionType
ALU = mybir.AluOpType

D = 4096
SEQ = 4096
DEPTH = 4
BW = 1024
MEM = 256
KC = D // 128
A_SHIFT = 3 * BW + 256
A_W = A_SHIFT + BW
O1 = A_W
O2 = O1 + 3 * BW
O3 = O2 + 4 * BW
O4 = O3 + 2 * BW
IN_W = O4 + 4 * D
TB = 1024
ENGS = ('pe', 'act', 'dve', 'pool', 'sp')


class Buf:
    __slots__ = ('w', 'r')

    def __init__(self):
        self.w = {}
        self.r = {}


class Tl:
    def __init__(self, t, nbuf=1):
        self.t = t
        self.bufs = [Buf() for _ in range(nbuf)]

    @property
    def buf(self):
        return self.bufs[0]


class Prog:
    EPOCH = 60000
    NSLOT = 8

    def __init__(self, nc, es):
        self.nc = nc
        self.es = es
        self.sems = []
        self.ops = {e: [] for e in ENGS}
        self.cnt = {e: 0 for e in ENGS}
        self.csem = {e: self._newsem() for e in ENGS}
        self.waited = {e: {} for e in ENGS}
        self.dma_n = {e: 0 for e in ENGS}
        self.dma_sems = {e: None for e in ENGS}
        self.last = {}
        self.n_ops = 0
        self.chain_sem = None
        self.chain_n = 0

    def _newsem(self):
        s = self.es.enter_context(self.nc.semaphore("sem%d" % len(self.sems)))
        self.sems.append(s)
        return len(self.sems) - 1

    def emit(self, eng, fn, reads=(), writes=(), dma=False, awrites=(), chain=False):
        deps = {}

        def add(s, v):
            if deps.get(s, 0) < v:
                deps[s] = v

        for b in reads:
            for s, v in b.w.items():
                add(s, v)
        for b in writes:
            for s, v in b.w.items():
                add(s, v)
            for s, v in b.r.items():
                add(s, v)
        for b in awrites:
            for s, v in b.r.items():
                add(s, v)
        if dma and chain:
            if self.chain_sem is None:
                self.chain_sem = self._newsem()
            if self.chain_n > 0:
                add(self.chain_sem, 16 * self.chain_n)
            self.chain_n += 1
            ev = (self.chain_sem, 16 * self.chain_n)
            inc = 16
        elif dma:
            if self.dma_sems[eng] is None:
                self.dma_sems[eng] = [self._newsem() for _ in range(self.NSLOT)]
            n = self.dma_n[eng]
            self.dma_n[eng] += 1
            slot, rnd = n % self.NSLOT, n // self.NSLOT
            sem = self.dma_sems[eng][slot]
            if rnd > 0:
                add(sem, 16 * rnd)
            ev = (sem, 16 * (rnd + 1))
            inc = 16
        else:
            if self.cnt[eng] >= self.EPOCH:
                self.csem[eng] = self._newsem()
                self.cnt[eng] = 0
            self.cnt[eng] += 1
            ev = (self.csem[eng], self.cnt[eng])
            inc = 1
        wd = self.waited[eng]
        waits = []
        for s, v in deps.items():
            if (not dma) and eng == 'pe' and s == ev[0]:
                continue
            if wd.get(s, 0) >= v:
                continue
            wd[s] = v
            waits.append((s, v))
        self.ops[eng].append((waits, fn, ev[0], inc))
        for b in reads:
            if b.r.get(ev[0], 0) < ev[1]:
                b.r[ev[0]] = ev[1]
        for b in writes:
            b.w = {ev[0]: ev[1]}
            b.r = {}
        for b in awrites:
            if b.w.get(ev[0], 0) < ev[1]:
                b.w[ev[0]] = ev[1]
        if self.last.get(ev[0], 0) < ev[1]:
            self.last[ev[0]] = ev[1]
        self.n_ops += 1
        return ev

    def barrier(self):
        for eng in ENGS:
            wd = self.waited[eng]
            waits = []
            for s, v in self.last.items():
                if wd.get(s, 0) >= v:
                    continue
                wd[s] = v
                waits.append((s, v))
            if waits:
                self.ops[eng].append((waits, None, None, 0))

    def _replay(self, eng, e):
        sems = self.sems
        for waits, fn, sem, inc in self.ops[eng]:
            for s, v in waits:
                e.wait_ge(sems[s], v)
            if fn is not None:
                ins = fn(e)
                ins.then_inc(sems[sem], inc)

    def build(self):
        with self.nc.Block() as block:
            @block.tensor
            def _(e):
                self._replay('pe', e)

            @block.scalar
            def _(e):
                self._replay('act', e)

            @block.vector
            def _(e):
                self._replay('dve', e)

            @block.gpsimd
            def _(e):
                self._replay('pool', e)

            @block.sync
            def _(e):
                self._replay('sp', e)


class Ctx:
    pass


def sb(C, es, name, shape, dt, nbuf=1):
    C.uid += 1
    return Tl(es.enter_context(C.nc.sbuf_tensor("%s_%d" % (name, C.uid), shape, dt)), nbuf)


def ps(C, es, name, shape, dt=F32):
    C.uid += 1
    return Tl(es.enter_context(C.nc.psum_tensor("%s_%d" % (name, C.uid), shape, dt)))


def dma(C, q, out, in_, reads=(), writes=(), awrites=()):
    C.P.emit(q, lambda e, out=out, in_=in_: e.dma_start(out=out, in_=in_), reads=reads, writes=writes, dma=True,
             awrites=awrites)


def load_w(C, slot, src, kc_n, cw):
    g = 0
    for q in range(0, kc_n, 8):
        n = min(8, kc_n - q)
        dma(C, C.wq, slot.t[:, q:q + n, 0:cw],
            src[q * 128:(q + n) * 128, :].rearrange("(kc p) n -> p kc n", p=128),
            writes=[slot.bufs[g]])
        g += 1


class WView:
    def __init__(self, groups):
        self.groups = groups

    def __getitem__(self, key):
        rs, cs = key
        for lo, hi, ap in self.groups:
            if lo <= cs.start and cs.stop <= hi:
                return ap[rs, cs.start - lo:cs.stop - lo]
        raise KeyError(key)


def load_xT(C, dst, src3, t0, tb, kc_n=KC):
    g = 0
    for q in range(0, kc_n, 8):
        n = min(8, kc_n - q)
        dma(C, 'sp', dst.t[:, q:q + n, 0:tb], src3[q:q + n, :, t0:t0 + tb].rearrange("kc p t -> p kc t"),
            writes=[dst.bufs[g]])
        g += 1


def phase_rmsnorm(C, x_src, gbc_src, xnT_dst, T, eps=1e-6):
    P = C.P
    with ExitStack() as es:
        gb = sb(C, es, "gb", [128, D], F32)
        xt = [sb(C, es, "xt", [128, D], F32) for _ in range(2)]
        xn = [sb(C, es, "xn", [128, D], BF16) for _ in range(2)]
        junk = sb(C, es, "junk", [128, D], BF16)
        st = [sb(C, es, "st", [128, 2], F32) for _ in range(2)]
        xs = [sb(C, es, "xs", [128, KC, 512], BF16) for _ in range(2)]
        pt = [ps(C, es, "pt", [128, 8, 128], BF16) for _ in range(4)]
        dma(C, 'sp', gb.t[:], gbc_src, writes=[gb.buf])
        ntt = T // 128
        for tt in range(ntt):
            s = tt % 2
            x_, n_, st_ = xt[s], xn[s], st[s]
            xsb = xs[(tt // 4) % 2]
            dma(C, 'sp', x_.t[:], x_src[tt * 128:(tt + 1) * 128, :], writes=[x_.buf])
            P.emit('act', lambda e, x_=x_, st_=st_: e.activation(out=junk.t[:], in_=x_.t[:], func=AF.Square,
                                                                   accum_out=st_.t[:, 0:1]),
                   reads=[x_.buf], writes=[junk.buf, st_.buf])
            P.emit('act', lambda e, st_=st_: e.activation(out=st_.t[:, 1:2], in_=st_.t[:, 0:1], func=AF.Sqrt,
                                                           bias=eps, scale=1.0 / D),
                   reads=[st_.buf], writes=[st_.buf])
            P.emit('dve', lambda e, st_=st_: e.reciprocal(out=st_.t[:, 1:2], in_=st_.t[:, 1:2]),
                   reads=[st_.buf], writes=[st_.buf])
            P.emit('dve', lambda e, x_=x_, n_=n_, st_=st_: e.scalar_tensor_tensor(
                out=n_.t[:], in0=x_.t[:], scalar=st_.t[:, 1:2], in1=gb.t[:], op0=ALU.mult, op1=ALU.mult),
                reads=[x_.buf, st_.buf, gb.buf], writes=[n_.buf])
            for q in range(4):
                p_ = pt[q]

                def tr(e, n_=n_, p_=p_, q=q):
                    for j in range(8):
                        kc = q * 8 + j
                        ins = e.transpose(out=p_.t[:, j, :], in_=n_.t[:, kc * 128:(kc + 1) * 128], identity=C.ident.t[:])
                    return ins
                P.emit('pe', tr, reads=[n_.buf, C.ident.buf], writes=[p_.buf])
                eng = 'act' if q % 2 == 0 else 'dve'
                if eng == 'act':
                    P.emit('act', lambda e, p_=p_, xsb=xsb, q=q, tt=tt: e.activation(
                        out=xsb.t[:, q * 8:(q + 1) * 8, (tt % 4) * 128:(tt % 4 + 1) * 128], in_=p_.t[:], func=AF.Copy),
                        reads=[p_.buf], writes=[xsb.buf])
                else:
                    P.emit('dve', lambda e, p_=p_, xsb=xsb, q=q, tt=tt: e.tensor_copy(
                        out=xsb.t[:, q * 8:(q + 1) * 8, (tt % 4) * 128:(tt % 4 + 1) * 128], in_=p_.t[:]),
                        reads=[p_.buf], writes=[xsb.buf])
            if tt % 4 == 3:
                t0 = (tt // 4) * 512
                for q in range(0, KC, 8):
                    dma(C, 'sp', xnT_dst[q:q + 8, :, t0:t0 + 512].rearrange("kc p t -> p kc t"),
                        xsb.t[:, q:q + 8, :], reads=[xsb.buf], awrites=[C.dbuf('xnT', t0 // TB)])
    P.barrier()


def gemm_fm(C, wslot, kc_n, xT, ct, ntg, pss, extra_reads=()):
    def mm(e):
        ins = None
        for kc in range(kc_n):
            for tg in range(ntg):
                ins = e.matmul(pss[tg].t[:], wslot.t[:, kc, ct * 128:(ct + 1) * 128],
                               xT.t[:, kc, tg * 512:(tg + 1) * 512], start=(kc == 0), stop=(kc == kc_n - 1))
        return ins
    C.P.emit('pe', mm, reads=list(wslot.bufs) + list(xT.bufs) + list(extra_reads), writes=[p.buf for p in pss[:ntg]])


def gemm_tm(C, wslot, kc_n, xT, tt, cw, pst):
    def mm(e):
        ins = None
        for kc in range(kc_n):
            ins = e.matmul(pst.t[:, 0:cw], xT.t[:, kc, tt * 128:(tt + 1) * 128], wslot.t[:, kc, 0:cw],
                           start=(kc == 0), stop=(kc == kc_n - 1))
        return ins
    C.P.emit('pe', mm, reads=list(wslot.bufs) + list(xT.bufs), writes=[pst.buf])


class WStream:
    def __init__(self, C, es, kc_max, cw_max):
        self.C = C
        self.slots = [sb(C, es, "wsl", [128, kc_max, cw_max], BF16, nbuf=(kc_max + 7) // 8) for _ in range(2)]
        self.jobs = []

    def run(self, jobs):
        C = self.C
        n = len(jobs)
        if n == 0:
            return
        load_w(C, self.slots[0], jobs[0][0], jobs[0][1], jobs[0][2])
        for i in range(n):
            if i + 1 < n:
                load_w(C, self.slots[(i + 1) % 2], jobs[i + 1][0], jobs[i + 1][1], jobs[i + 1][2])
            jobs[i][3](self.slots[i % 2])


def evac_store_fm(C, stage, pst, dst_ap, dbuf, eng):
    if eng == 'act':
        C.P.emit('act', lambda e: e.activation(out=stage.t[:], in_=pst.t[:], func=AF.Copy),
                 reads=[pst.buf], writes=[stage.buf])
    else:
        C.P.emit('dve', lambda e: e.tensor_copy(out=stage.t[:], in_=pst.t[:]), reads=[pst.buf], writes=[stage.buf])
    dma(C, 'sp', dst_ap, stage.t[:], reads=[stage.buf], awrites=[dbuf])


def phase_B(C, l, T):
    P = C.P
    w = C.w_in[l]
    with ExitStack() as es:
        xT = sb(C, es, "xT", [128, KC, TB], BF16, nbuf=4)
        ws = WStream(C, es, KC, 512)
        vg = sb(C, es, "vg", [128, 8, BW], BF16, nbuf=8)
        lng = sb(C, es, "lng", [128, BW], F32)
        lnb = sb(C, es, "lnb", [128, BW], F32)
        wsT = sb(C, es, "wsT", [128, 8, 128], BF16)
        bsb = sb(C, es, "bsb", [128, 8, 128], F32)
        sv = sb(C, es, "sv", [128, 8, TB], BF16, nbuf=8)
        st6 = [sb(C, es, "st6", [128, 12], F32) for _ in range(2)]
        mv = [sb(C, es, "mv", [128, 4], F32) for _ in range(2)]
        vtmp = [sb(C, es, "vtmp", [128, BW], F32) for _ in range(2)]
        vln = [sb(C, es, "vln", [128, BW], BF16) for _ in range(2)]
        tmp = [sb(C, es, "tmpb", [128, 512], BF16) for _ in range(4)]
        ystage = [sb(C, es, "ystage", [128, 512], BF16) for _ in range(4)]
        pm = [ps(C, es, "pm", [128, 512]) for _ in range(4)]
        psv = [ps(C, es, "psv", [128, 512]) for _ in range(2)]
        dma(C, 'sp', lng.t[:], C.prm['lng_bc'][l], writes=[lng.buf])
        dma(C, 'sp', lnb.t[:], C.prm['lnb_bc'][l], writes=[lnb.buf])
        dma(C, 'sp', bsb.t[:], C.prm['bs_bc'][l], writes=[bsb.buf])
        dma(C, 'pool', wsT.t[:], C.prm['wsT'][l], writes=[wsT.buf])
        cnt = [0]
        jobs = []
        for tb in range(T // TB):
            t0 = tb * TB

            def job_v(slot, j, tb=tb, t0=t0):
                if j == 0:
                    load_xT(C, xT, C.xnT, t0, TB)
                for tt in range(8):
                    p_ = pm[cnt[0] % 4]
                    cnt[0] += 1
                    gemm_tm(C, slot, KC, xT, tt, 512, p_)
                    P.emit('act', lambda e, p_=p_, tt=tt: e.activation(out=vg.t[:, tt, j * 512:(j + 1) * 512], in_=p_.t[:],
                                                                      func=AF.Gelu_apprx_tanh),
                           reads=[p_.buf], writes=[vg.bufs[tt]])
                if j == 1:
                    for tt in range(8):
                        s = tt % 2
                        P.emit('dve', lambda e, s=s, tt=tt: e.bn_stats(out=st6[s].t[:, 0:6], in_=vg.t[:, tt, 0:512]),
                               reads=[vg.bufs[tt]], writes=[st6[s].buf])
                        P.emit('dve', lambda e, s=s, tt=tt: e.bn_stats(out=st6[s].t[:, 6:12], in_=vg.t[:, tt, 512:1024]),
                               reads=[vg.bufs[tt], st6[s].buf], writes=[st6[s].buf])
                        P.emit('dve', lambda e, s=s: e.bn_aggr(out=mv[s].t[:, 0:2], in_=st6[s].t[:]),
                               reads=[st6[s].buf], writes=[mv[s].buf])
                        P.emit('act', lambda e, s=s: e.activation(out=mv[s].t[:, 2:3], in_=mv[s].t[:, 1:2], func=AF.Sqrt,
                                                                    bias=1e-5, scale=1.0),
                               reads=[mv[s].buf], writes=[mv[s].buf])
                        P.emit('dve', lambda e, s=s: e.reciprocal(out=mv[s].t[:, 2:3], in_=mv[s].t[:, 2:3]),
                               reads=[mv[s].buf], writes=[mv[s].buf])
                        P.emit('dve', lambda e, s=s, tt=tt: e.tensor_scalar(
                            out=vtmp[s].t[:], in0=vg.t[:, tt, :], scalar1=mv[s].t[:, 0:1], scalar2=mv[s].t[:, 2:3],
                            op0=ALU.subtract, op1=ALU.mult),
                            reads=[vg.bufs[tt], mv[s].buf], writes=[vtmp[s].buf])
                        P.emit('pool', lambda e, s=s: e.tensor_tensor(out=vtmp[s].t[:], in0=vtmp[s].t[:], in1=lng.t[:], op=ALU.mult),
                               reads=[vtmp[s].buf, lng.buf], writes=[vtmp[s].buf])
                        P.emit('pool', lambda e, s=s: e.tensor_tensor(out=vln[s].t[:], in0=vtmp[s].t[:], in1=lnb.t[:], op=ALU.add),
                               reads=[vtmp[s].buf, lnb.buf], writes=[vln[s].buf])

                        def spm(e, s=s):
                            ins = None
                            for g in range(8):
                                ins = e.matmul(psv[g // 4].t[:, (g % 4) * 128:(g % 4 + 1) * 128],
                                               vln[s].t[:, g * 128:(g + 1) * 128], wsT.t[:, g, :], start=True, stop=True)
                            return ins
                        P.emit('pe', spm, reads=[vln[s].buf, wsT.buf], writes=[psv[0].buf, psv[1].buf])
                        for hh in range(2):
                            P.emit('dve', lambda e, hh=hh, tt=tt: e.tensor_tensor(
                                out=sv.t[:, hh * 4:(hh + 1) * 4, tt * 128:(tt + 1) * 128],
                                in0=psv[hh].t[:].rearrange("p (g q) -> p g q", g=4),
                                in1=bsb.t[:, hh * 4:(hh + 1) * 4, :], op=ALU.add),
                                reads=[psv[hh].buf, bsb.buf], writes=[sv.bufs[g_] for g_ in range(hh * 4, hh * 4 + 4)])

            def job_ug(slot, j, kind, tb=tb, t0=t0):
                for ct in range(4):
                    g = j * 4 + ct
                    pp = [pm[(cnt[0] % 2) * 2], pm[(cnt[0] % 2) * 2 + 1]]
                    cnt[0] += 1
                    gemm_fm(C, slot, KC, xT, ct, 2, pp)
                    for tg in range(2):
                        t_ = tmp[(cnt[0] * 2 + tg) % 4]
                        func = AF.Gelu_apprx_tanh if kind == 'u' else AF.Silu
                        P.emit('act', lambda e, t_=t_, p_=pp[tg], func=func: e.activation(out=t_.t[:], in_=p_.t[:], func=func),
                               reads=[pp[tg].buf], writes=[t_.buf])
                        if kind == 'u':
                            P.emit('dve', lambda e, t_=t_, g=g, tg=tg: e.tensor_tensor(
                                out=sv.t[:, g, tg * 512:(tg + 1) * 512], in0=t_.t[:], in1=sv.t[:, g, tg * 512:(tg + 1) * 512],
                                op=ALU.mult), reads=[t_.buf, sv.bufs[g]], writes=[sv.bufs[g]])
                        else:
                            y_ = ystage[(cnt[0] * 2 + tg) % 4]
                            P.emit('dve', lambda e, t_=t_, y_=y_, g=g, tg=tg: e.tensor_tensor(
                                out=y_.t[:], in0=t_.t[:], in1=sv.t[:, g, tg * 512:(tg + 1) * 512], op=ALU.mult),
                                reads=[t_.buf, sv.bufs[g]], writes=[y_.buf])
                            dma(C, 'sp', C.yT[1][g, :, t0 + tg * 512:t0 + (tg + 1) * 512], y_.t[:],
                                reads=[y_.buf], awrites=[C.dbuf('yT1', tb)])

            for j in range(2):
                jobs.append((w[:, O1 + BW + j * 512:O1 + BW + (j + 1) * 512], KC, 512, lambda s, j=j, f=job_v: f(s, j)))
            for j in range(2):
                jobs.append((w[:, O1 + j * 512:O1 + (j + 1) * 512], KC, 512, lambda s, j=j, f=job_ug: f(s, j, 'u')))
            for j in range(2):
                jobs.append((w[:, O1 + 2 * BW + j * 512:O1 + 2 * BW + (j + 1) * 512], KC, 512,
                             lambda s, j=j, f=job_ug: f(s, j, 'g')))
        ws.run(jobs)
    P.barrier()


def headnorm_fm(C, pq, q_sb, sq, pss, rs, gain_cols, out_tile, out_idx, nfeat, eps, tokn, ones=None):
    P = C.P
    ndc = len(pq)
    ones = ones or C.ones
    for dc in range(ndc):
        P.emit('act', lambda e, dc=dc: e.activation(out=q_sb.t[:, dc, 0:tokn], in_=pq[dc].t[:, 0:tokn], func=AF.Copy),
               reads=[pq[dc].buf], writes=[q_sb.bufs[dc]])
        P.emit('act', lambda e, dc=dc: e.activation(out=sq.t[:, dc, 0:tokn], in_=pq[dc].t[:, 0:tokn], func=AF.Square),
               reads=[pq[dc].buf], writes=[sq.bufs[dc]])

    def mm(e):
        ins = None
        for dc in range(ndc):
            ins = e.matmul(pss.t[:, 0:tokn], ones.t[:], sq.t[:, dc, 0:tokn], start=(dc == 0), stop=(dc == ndc - 1))
        return ins
    P.emit('pe', mm, reads=[ones.buf] + [sq.bufs[dc] for dc in range(ndc)], writes=[pss.buf])
    P.emit('act', lambda e: e.activation(out=rs.t[:, 0:tokn], in_=pss.t[:, 0:tokn], func=AF.Sqrt, bias=eps, scale=1.0 / nfeat),
           reads=[pss.buf], writes=[rs.buf])
    P.emit('dve', lambda e: e.reciprocal(out=rs.t[:, 0:tokn], in_=rs.t[:, 0:tokn]), reads=[rs.buf], writes=[rs.buf])
    for dc in range(ndc):
        P.emit('dve', lambda e, dc=dc: e.scalar_tensor_tensor(
            out=out_tile.t[:, out_idx[dc], 0:tokn], in0=q_sb.t[:, dc, 0:tokn], scalar=gain_cols[dc], in1=rs.t[:, 0:tokn],
            op0=ALU.mult, op1=ALU.mult),
            reads=[q_sb.bufs[dc], rs.buf, C.cols.buf], writes=[out_tile.bufs[out_idx[dc]]])


def phase_D(C, l, T):
    P = C.P
    w = C.w_in[l]
    wkv = C.w_kv[l]
    with ExitStack() as es:
        ws = WStream(C, es, KC, 512)
        kT = sb(C, es, "kT", [128, 8, MEM], BF16, nbuf=8)
        vmem = sb(C, es, "vmem", [128, 2, BW], BF16, nbuf=2)
        q_sb = sb(C, es, "q_sb", [128, 2, 512], F32, nbuf=2)
        sq = sb(C, es, "sq", [128, 2, 512], BF16, nbuf=2)
        rs = sb(C, es, "rs", [128, 512], F32)
        qn = sb(C, es, "qn", [128, 2, 512], BF16, nbuf=2)
        E = sb(C, es, "E", [128, 2, 512], BF16, nbuf=2)
        rd = sb(C, es, "rd", [128, 512], F32)
        tmp = [sb(C, es, "tmpd", [128, 512], BF16) for _ in range(2)]
        ystage = [sb(C, es, "ystd", [128, 512], BF16) for _ in range(4)]
        pm = [ps(C, es, "pm", [128, 512]) for _ in range(4)]
        pso = [ps(C, es, "pso", [128, 512]) for _ in range(2)]
        paux = ps(C, es, "paux", [128, 512])
        with ExitStack() as es2:
            mT = sb(C, es2, "mT", [128, KC, MEM], BF16, nbuf=1)
            gb = sb(C, es2, "mgb", [128, D], F32)
            xt = sb(C, es2, "mxt", [128, D], F32)
            xn = sb(C, es2, "mxn", [128, D], BF16)
            junk = sb(C, es2, "mjunk", [128, D], BF16)
            st = sb(C, es2, "mst", [128, 2], F32)
            pt = ps(C, es2, "mpt", [128, 8, 128], BF16)
            dma(C, 'sp', gb.t[:], C.prm['mg_bc'][l], writes=[gb.buf])
            for m in range(2):
                dma(C, 'sp', xt.t[:], C.mem[m * 128:(m + 1) * 128, :], writes=[xt.buf])
                P.emit('act', lambda e: e.activation(out=junk.t[:], in_=xt.t[:], func=AF.Square, accum_out=st.t[:, 0:1]),
                       reads=[xt.buf], writes=[junk.buf, st.buf])
                P.emit('act', lambda e: e.activation(out=st.t[:, 1:2], in_=st.t[:, 0:1], func=AF.Sqrt, bias=1e-6, scale=1.0 / D),
                       reads=[st.buf], writes=[st.buf])
                P.emit('dve', lambda e: e.reciprocal(out=st.t[:, 1:2], in_=st.t[:, 1:2]), reads=[st.buf], writes=[st.buf])
                P.emit('dve', lambda e: e.scalar_tensor_tensor(out=xn.t[:], in0=xt.t[:], scalar=st.t[:, 1:2], in1=gb.t[:],
                                                               op0=ALU.mult, op1=ALU.mult),
                       reads=[xt.buf, st.buf, gb.buf], writes=[xn.buf])
                for q in range(4):
                    def tr(e, q=q):
                        ins = None
                        for j in range(8):
                            kc = q * 8 + j
                            ins = e.transpose(out=pt.t[:, j, :], in_=xn.t[:, kc * 128:(kc + 1) * 128], identity=C.ident.t[:])
                        return ins
                    P.emit('pe', tr, reads=[xn.buf, C.ident.buf], writes=[pt.buf])
                    P.emit('dve', lambda e, q=q, m=m: e.tensor_copy(out=mT.t[:, q * 8:(q + 1) * 8, m * 128:(m + 1) * 128], in_=pt.t[:]),
                           reads=[pt.buf], writes=[mT.buf])
            jobs = []

            def job_k(slot, j):
                for h2 in range(2):
                    h = j * 2 + h2
                    for dc in range(2):
                        ct = h2 * 2 + dc

                        def mm(e, ct=ct, dc=dc):
                            ins = None
                            for kc in range(KC):
                                ins = e.matmul(pm[dc].t[:, 0:MEM], slot.t[:, kc, ct * 128:(ct + 1) * 128], mT.t[:, kc, :],
                                               start=(kc == 0), stop=(kc == KC - 1))
                            return ins
                        P.emit('pe', mm, reads=list(slot.bufs) + [mT.buf], writes=[pm[dc].buf])
                    headnorm_fm(C, [pm[0], pm[1]], q_sb, sq, paux, rs,
                                [C.col('mk', l, 0), C.col('mk', l, 1)], kT, [h * 2, h * 2 + 1], 256, 1e-6, MEM)

            def job_v(slot, j):
                for m in range(2):
                    gemm_tm(C, slot, KC, mT, m, 512, pm[2 + m])
                    P.emit('act', lambda e, m=m: e.activation(out=vmem.t[:, m, j * 512:(j + 1) * 512], in_=pm[2 + m].t[:], func=AF.Copy),
                           reads=[pm[2 + m].buf], writes=[vmem.bufs[m]])
            for j in range(2):
                jobs.append((wkv[:, j * 512:(j + 1) * 512], KC, 512, lambda s, j=j: job_k(s, j)))
            for j in range(2):
                jobs.append((wkv[:, BW + j * 512:BW + (j + 1) * 512], KC, 512, lambda s, j=j: job_v(s, j)))
            ws.run(jobs)
        P.barrier()
        xT = sb(C, es, "xT", [128, KC, TB], BF16, nbuf=4)
        oD = sb(C, es, "oD", [128, 8, TB], BF16, nbuf=8)
        jobs = []
        cnt = [0]
        for tb in range(T // TB):
            t0 = tb * TB

            def job_q(slot, j, tb=tb, t0=t0):
                if j == 0:
                    load_xT(C, xT, C.xnT, t0, TB)
                for h2 in range(2):
                    h = j * 2 + h2
                    for dc in range(2):
                        gemm_fm(C, slot, KC, xT, h2 * 2 + dc, 2, [pm[dc * 2], pm[dc * 2 + 1]])
                    for tg in range(2):
                        headnorm_fm(C, [pm[tg], pm[2 + tg]], q_sb, sq, paux, rs,
                                    [C.col('mq', l, 0), C.col('mq', l, 1)], qn, [0, 1], 256, 1e-6, 512)
                        for m in range(2):
                            def mm(e, m=m, h=h):
                                ins = None
                                for dc in range(2):
                                    ins = e.matmul(pso[m].t[:], kT.t[:, h * 2 + dc, m * 128:(m + 1) * 128], qn.t[:, dc, :],
                                                   start=(dc == 0), stop=(dc == 1))
                                return ins
                            P.emit('pe', mm, reads=[kT.bufs[h * 2], kT.bufs[h * 2 + 1], qn.bufs[0], qn.bufs[1]], writes=[pso[m].buf])
                            P.emit('act', lambda e, m=m: e.activation(out=E.t[:, m, :], in_=pso[m].t[:], func=AF.Exp, scale=1.0 / 16.0),
                                   reads=[pso[m].buf], writes=[E.bufs[m]])

                        def mmd(e):
                            ins = None
                            for m in range(2):
                                ins = e.matmul(paux.t[:], C.ones.t[:], E.t[:, m, :], start=(m == 0), stop=(m == 1))
                            return ins
                        P.emit('pe', mmd, reads=[C.ones.buf, E.bufs[0], E.bufs[1]], writes=[paux.buf])
                        P.emit('dve', lambda e: e.reciprocal(out=rd.t[:], in_=paux.t[:]), reads=[paux.buf], writes=[rd.buf])
                        for dc in range(2):
                            def mmo(e, dc=dc, h=h):
                                ins = None
                                for m in range(2):
                                    c0 = h * 256 + dc * 128
                                    ins = e.matmul(pso[dc].t[:], vmem.t[:, m, c0:c0 + 128], E.t[:, m, :], start=(m == 0), stop=(m == 1))
                                return ins
                            P.emit('pe', mmo, reads=[vmem.bufs[0], vmem.bufs[1], E.bufs[0], E.bufs[1]], writes=[pso[dc].buf])
                            P.emit('dve', lambda e, dc=dc, h=h, tg=tg: e.tensor_tensor(
                                out=oD.t[:, h * 2 + dc, tg * 512:(tg + 1) * 512], in0=pso[dc].t[:], in1=rd.t[:], op=ALU.mult),
                                reads=[pso[dc].buf, rd.buf], writes=[oD.bufs[h * 2 + dc]])

            def job_g(slot, j, tb=tb, t0=t0):
                for ct in range(4):
                    g = j * 4 + ct
                    pp = [pm[(cnt[0] % 2) * 2], pm[(cnt[0] % 2) * 2 + 1]]
                    cnt[0] += 1
                    gemm_fm(C, slot, KC, xT, ct, 2, pp)
                    for tg in range(2):
                        t_ = tmp[tg]
                        y_ = ystage[(cnt[0] * 2 + tg) % 4]
                        P.emit('act', lambda e, t_=t_, p_=pp[tg]: e.activation(out=t_.t[:], in_=p_.t[:], func=AF.Silu),
                               reads=[pp[tg].buf], writes=[t_.buf])
                        P.emit('dve', lambda e, t_=t_, y_=y_, g=g, tg=tg: e.tensor_tensor(
                            out=y_.t[:], in0=t_.t[:], in1=oD.t[:, g, tg * 512:(tg + 1) * 512], op=ALU.mult),
                            reads=[t_.buf, oD.bufs[g]], writes=[y_.buf])
                        dma(C, 'sp', C.yT[3][g, :, t0 + tg * 512:t0 + (tg + 1) * 512], y_.t[:],
                            reads=[y_.buf], awrites=[C.dbuf('yT3', tb)])
            for j in range(2):
                jobs.append((w[:, O3 + j * 512:O3 + (j + 1) * 512], KC, 512, lambda s, j=j, f=job_q: f(s, j)))
            for j in range(2):
                jobs.append((w[:, O3 + BW + j * 512:O3 + BW + (j + 1) * 512], KC, 512, lambda s, j=j, f=job_g: f(s, j)))
        ws.run(jobs)
    P.barrier()


CA = {'conv': 0, 'convl': 144, 'w0': 156, 'a0': 188, 'kk': 220, 'ka': 236, 'rk': 252, 'lnxw': 268, 'lnxb': 284}
NCA = 300
CH = 64
NCG = 8


def phase_A(C, l, T):
    P = C.P
    w = C.w_in[l]
    NTG = T // 512
    NCHK = T // CH

    def E(eng, fn, r=(), w=(), aw=()):
        P.emit(eng, fn, reads=r, writes=w, awrites=aw)

    with ExitStack() as es:
        xT = sb(C, es, "xT", [128, KC, TB], BF16, nbuf=4)
        ws = WStream(C, es, KC, 512)
        stage = [sb(C, es, "astage", [128, 512], F32) for _ in range(4)]
        pm = [ps(C, es, "pma", [128, 512]) for _ in range(4)]
        cnt = [0]
        jobs = []
        for tb in range(T // TB):
            t0 = tb * TB

            def job(slot, c0, cw, tb=tb, t0=t0):
                if c0 == 0:
                    load_xT(C, xT, C.xnT, t0, TB)
                for ct in range(cw // 128):
                    tile = c0 // 128 + ct
                    pp = [pm[(cnt[0] % 2) * 2], pm[(cnt[0] % 2) * 2 + 1]]
                    cnt[0] += 1
                    gemm_fm(C, slot, KC, xT, ct, 2, pp)
                    for tg in range(2):
                        k = (cnt[0] * 2 + tg) % 4
                        func = AF.Silu if tile >= 26 else AF.Copy
                        E('act', lambda e, k=k, p_=pp[tg], func=func: e.activation(out=stage[k].t[:], in_=p_.t[:], func=func),
                          r=[pp[tg].buf], w=[stage[k].buf])
                        dma(C, 'sp', C.hA[tile, :, t0 + tg * 512:t0 + (tg + 1) * 512], stage[k].t[:], reads=[stage[k].buf],
                            awrites=[C.dbuf('hA', tb)])
            for c0, cw in [(0, 512), (512, 512), (1024, 512), (1536, 512), (2048, 512), (2560, 512), (3072, 256), (3328, 512), (3840, 512)]:
                jobs.append((w[:, c0:c0 + cw], KC, cw, lambda s, c0=c0, cw=cw, f=job: f(s, c0, cw)))
        ws.run(jobs)
    P.barrier()

    with ExitStack() as es:
        F = lambda name, shape=(64, 512), dt=F32: sb(C, es, name, list(shape), dt)
        ca = F("ca", (64, NCA))
        mskS = F("mskS")
        m2 = [F("m2f", (64, NCG, 128)), F("m2b", (64, NCG, 128))]
        I8 = F("I8", (64, NCG, 64))
        id64 = F("id64", (64, 64), BF16)
        on64 = F("on64", (64, 64), BF16)
        wup = F("wup", (64, 2, BW), BF16)
        aup = F("aup", (64, 2, BW), BF16)
        cst = C.prm['cstA']
        dma(C, 'sp', ca.t[:], C.prm['colsA'][l], writes=[ca.buf])
        dma(C, 'sp', mskS.t[:], cst[:, 0:512], writes=[mskS.buf])
        dma(C, 'sp', m2[0].t[:], cst[:, 512:1536].rearrange("p (c f) -> p c f", c=NCG), writes=[m2[0].buf])
        dma(C, 'sp', m2[1].t[:], cst[:, 1536:2560].rearrange("p (c f) -> p c f", c=NCG), writes=[m2[1].buf])
        dma(C, 'sp', I8.t[:], cst[:, 2560:3072].rearrange("p (c f) -> p c f", c=NCG), writes=[I8.buf])
        dma(C, 'pool', id64.t[:], cst[:, 2560:2624], writes=[id64.buf])
        dma(C, 'pool', on64.t[:], cst[:, 3072:3136], writes=[on64.buf])
        dma(C, 'pool', wup.t[:], C.a_w_up[l].rearrange("z r c -> r z c"), writes=[wup.buf])
        dma(C, 'pool', aup.t[:], C.a_a_up[l].rearrange("z r c -> r z c"), writes=[aup.buf])
        col = lambda name, i=0: ca.t[:, CA[name] + i:CA[name] + i + 1]
        raw = [F("raw%d" % i, (64, 514)) for i in range(4)]
        tw = [[F("tw%d%d" % (i, z), (64, 512), BF16) for z in range(2)] for i in range(2)]
        r_, k_, v_ = F("r_"), F("k_"), F("v_")
        kk_, t1, t2, t3 = F("kk_"), F("t1"), F("t2"), F("t3")
        a_, kt_ = [F("a0_"), F("a1_")], [F("kt0"), F("kt1")]
        lw_, Pc, Qc, Tt, CLb = F("lw_"), F("Pc"), F("Qc"), F("Tt"), F("CLb")
        E1, E1x, Em, Er = F("E1"), F("E1x"), F("Em"), F("Er")
        WC = F("WC", (64, NCG))
        b_ = F("b_")
        vb = F("vb", (64, 512), BF16)
        sqk = F("sqk", (64, 512), BF16)
        ZR = F("ZR", (64, NCG, 128), BF16)
        Bt, Kt, Bh, Kh = [F(n, (64, NCG, 64), BF16) for n in ("Bt", "Kt", "Bh", "Kh")]
        SP = F("SP", (64, NCG, 128), BF16)
        PT = F("PT", (64, NCG, 64), BF16)
        Abr = F("Abr", (64, NCG, 64), BF16)
        MK = F("MK", (64, NCG, 128), BF16)
        VT, ZT, BhT, KhT, X0T, U0T, ZpT = [F(n, (64, NCG, 64), BF16) for n in ("VT", "ZT", "BhT", "KhT", "X0T", "U0T", "ZpT")]
        GT = F("GT", (64, 2, NCHK, 64), BF16)
        Hh = F("Hh", (64, 2, NCHK, 64), BF16)
        Rp = F("Rp", (64, 2, NCHK, 64), BF16)
        Y0 = F("Y0", (64, 2, NCHK, 64), BF16)
        Sall = F("Sall", (64, 2, NCHK, 64), BF16)
        rkk = F("rkk", (64, T), BF16)
        vfull = F("vfull", (64, T), BF16)
        sgfull = F("sgfull", (64, T), BF16)
        yst = [F("yst%d" % i, (64, 512), BF16) for i in range(2)]
        pA = ps(C, es, "pA", [64, NCG, 128])
        pB = ps(C, es, "pB", [64, NCG, 128])
        pC = ps(C, es, "pC", [64, NCG, 64])
        pD = ps(C, es, "pD", [64, NCG, 64])
        pTr = ps(C, es, "pTr", [64, NCG, 64], BF16)
        pTr2 = ps(C, es, "pTr2", [64, NCG, 64], BF16)

        def v3(t, lo=0, hi=None):
            return t.t[:].rearrange("p (c f) -> p c f", f=CH)

        def conv(dst, src, base):
            E('dve', lambda e: e.tensor_scalar(out=dst.t[:], in0=src.t[:, 0:512], scalar1=col(*base(0)), scalar2=None, op0=ALU.mult),
              r=[src.buf, ca.buf], w=[dst.buf])
            E('dve', lambda e: e.scalar_tensor_tensor(out=dst.t[:], in0=src.t[:, 1:513], scalar=col(*base(1)), in1=dst.t[:],
                                                      op0=ALU.mult, op1=ALU.add), r=[src.buf, ca.buf, dst.buf], w=[dst.buf])
            E('dve', lambda e: e.scalar_tensor_tensor(out=dst.t[:], in0=src.t[:, 2:514], scalar=col(*base(2)), in1=dst.t[:],
                                                      op0=ALU.mult, op1=ALU.add), r=[src.buf, ca.buf, dst.buf], w=[dst.buf])

        def load_halo(dst, tile, row0, c0):
            lo, hi = max(c0 - 1, 0), min(c0 + 513, T)
            if c0 == 0:
                E('pool', lambda e: e.memset(dst.t[:, 0:1], 0.0), w=[dst.buf])
            if c0 + 512 == T:
                E('pool', lambda e: e.memset(dst.t[:, 513:514], 0.0), w=[dst.buf])
            dma(C, 'sp', dst.t[:, lo - (c0 - 1):hi - (c0 - 1)], C.hA[tile, row0:row0 + 64, lo:hi], writes=[dst.buf])

        def chunk_mm(pt, osl, lhs_fn, rhs_fn, reads, n2=1, lhs2=None, rhs2=None):
            def mm(e):
                ins = None
                for c in range(NCG):
                    ins = e.matmul(pt.t[:, c, osl], lhs_fn(c), rhs_fn(c), start=True, stop=(lhs2 is None))
                    if lhs2 is not None:
                        ins = e.matmul(pt.t[:, c, osl], lhs2(c), rhs2(c), start=False, stop=True)
                return ins
            E('pe', mm, r=reads, w=[pt.buf])

        def chunk_tr(pt, src3, reads):
            def tr(e):
                ins = None
                for c in range(NCG):
                    ins = e.transpose(out=pt.t[:, c, :], in_=src3(c), identity=id64.t[:])
                return ins
            E('pe', tr, r=list(reads) + [id64.buf], w=[pt.buf])

        A0 = slice(0, 64)
        A1 = slice(64, 128)
        def do_head(h):
            hp, hr = h // 2, (h % 2) * 64
            def do_cg(cg):
                c0 = cg * 512
                cb = cg * NCG
                if True:
                    for gi, (tile, row0) in enumerate([(24, 0), (24, 64), (25, 0), (25, 64)]):
                        load_halo(raw[gi], tile, row0, c0)
                        conv(t1, raw[gi], lambda tap, gi=gi: ('convl', gi * 3 + tap))
                        kind, z = gi // 2, gi % 2
                        if kind == 0:
                            E('act', lambda e, z=z: e.activation(out=tw[0][z].t[:], in_=t1.t[:], func=AF.Tanh), r=[t1.buf], w=[tw[0][z].buf])
                        else:
                            E('act', lambda e, z=z: e.activation(out=tw[1][z].t[:], in_=t1.t[:], func=AF.Copy), r=[t1.buf], w=[tw[1][z].buf])
                for wi, dst in enumerate((r_, k_, v_)):
                    load_halo(raw[wi], wi * 8 + hp, hr, c0)
                    conv(dst, raw[wi], lambda tap, wi=wi: ('conv', (wi * 16 + h) * 3 + tap))
                E('dve', lambda e: e.tensor_scalar(out=t1.t[:], in0=k_.t[:], scalar1=col('kk', h), scalar2=None, op0=ALU.mult),
                  r=[k_.buf, ca.buf], w=[t1.buf])
                E('act', lambda e: e.activation(out=sqk.t[:], in_=t1.t[:], func=AF.Square), r=[t1.buf], w=[sqk.buf])
                E('pe', lambda e: e.matmul(pC.t[:].rearrange("p c f -> p (c f)"), on64.t[:], sqk.t[:], start=True, stop=True),
                  r=[on64.buf, sqk.buf], w=[pC.buf])
                E('act', lambda e: e.activation(out=t2.t[:], in_=pC.t[:].rearrange("p c f -> p (c f)"), func=AF.Sqrt), r=[pC.buf], w=[t2.buf])
                E('dve', lambda e: e.tensor_scalar(out=t2.t[:], in0=t2.t[:], scalar1=1e-12, scalar2=None, op0=ALU.max), r=[t2.buf], w=[t2.buf])
                E('dve', lambda e: e.reciprocal(out=t2.t[:], in_=t2.t[:]), r=[t2.buf], w=[t2.buf])
                E('dve', lambda e: e.tensor_tensor(out=kk_.t[:], in0=t1.t[:], in1=t2.t[:], op=ALU.mult), r=[t1.buf, t2.buf], w=[kk_.buf])
                E('act', lambda e: e.activation(out=vb.t[:], in_=v_.t[:], func=AF.Copy), r=[v_.buf], w=[vb.buf])
                E('pool', lambda e, c0=c0: e.tensor_copy(out=vfull.t[:, c0:c0 + 512], in_=v_.t[:]), r=[v_.buf], aw=[vfull.buf])
                chunk_tr(pTr, lambda c: vb.t[:, c * CH:(c + 1) * CH], [vb.buf])
                E('dve', lambda e: e.tensor_copy(out=VT.t[:], in_=pTr.t[:]), r=[pTr.buf], w=[VT.buf])
                def do_z(z):
                    E('pe', lambda e, z=z: e.matmul(pC.t[:].rearrange("p c f -> p (c f)"), wup.t[:, z, h * 64:(h + 1) * 64], tw[0][z].t[:], start=True, stop=True),
                      r=[wup.buf, tw[0][z].buf], w=[pC.buf])
                    E('act', lambda e, z=z: e.activation(out=lw_.t[:], in_=pC.t[:].rearrange("p c f -> p (c f)"), func=AF.Sigmoid,
                                                         bias=col('w0', z * 16 + h), scale=1.0), r=[pC.buf, ca.buf], w=[lw_.buf])
                    E('pool', lambda e: e.tensor_scalar(out=lw_.t[:], in0=lw_.t[:], scalar1=-0.6065306597126334, scalar2=None, op0=ALU.mult),
                      r=[lw_.buf], w=[lw_.buf])
                    E('pe', lambda e, z=z: e.matmul(pD.t[:].rearrange("p c f -> p (c f)"), aup.t[:, z, h * 64:(h + 1) * 64], tw[1][z].t[:], start=True, stop=True),
                      r=[aup.buf, tw[1][z].buf], w=[pD.buf])
                    E('act', lambda e, z=z: e.activation(out=a_[z].t[:], in_=pD.t[:].rearrange("p c f -> p (c f)"), func=AF.Sigmoid,
                                                         bias=col('a0', z * 16 + h), scale=1.0), r=[pD.buf, ca.buf], w=[a_[z].buf])
                    E('dve', lambda e, z=z: e.tensor_scalar(out=t3.t[:], in0=a_[z].t[:], scalar1=-1.0, scalar2=col('ka', h), op0=ALU.add, op1=ALU.mult),
                      r=[a_[z].buf, ca.buf], w=[t3.buf])
                    E('dve', lambda e, z=z: e.scalar_tensor_tensor(out=kt_[z].t[:], in0=t3.t[:], scalar=1.0, in1=k_.t[:], op0=ALU.add, op1=ALU.mult),
                      r=[t3.buf, k_.buf], w=[kt_[z].buf])
                    E('pool', lambda e, z=z: e.tensor_tensor(out=b_.t[:], in0=kk_.t[:], in1=a_[z].t[:], op=ALU.mult), r=[kk_.buf, a_[z].buf], w=[b_.buf])
                    E('dve', lambda e: e.tensor_tensor_scan(out=Pc.t[:], data0=mskS.t[:], data1=lw_.t[:], initial=0.0, op0=ALU.mult, op1=ALU.add),
                      r=[mskS.buf, lw_.buf], w=[Pc.buf])
                    E('pool', lambda e: e.tensor_tensor(out=Qc.t[:], in0=Pc.t[:], in1=lw_.t[:], op=ALU.subtract), r=[Pc.buf, lw_.buf], w=[Qc.buf])
                    for c in range(NCG):
                        E('dve', lambda e, c=c: e.tensor_scalar(out=Tt.t[:, c * CH:(c + 1) * CH], in0=Pc.t[:, c * CH:(c + 1) * CH], scalar1=-1.0,
                                                                scalar2=Pc.t[:, c * CH + CH - 1:c * CH + CH], op0=ALU.mult, op1=ALU.add),
                          r=[Pc.buf], aw=[Tt.buf])
                    E('act', lambda e: e.activation(out=WC.t[:], in_=v3(Pc)[:, :, CH - 1], func=AF.Exp), r=[Pc.buf], w=[WC.buf])
                    if z == 0:
                        cl, clx, cr = Pc, Qc, Tt
                    else:
                        E('pool', lambda e: e.tensor_tensor(out=CLb.t[:], in0=Tt.t[:], in1=lw_.t[:], op=ALU.add), r=[Tt.buf, lw_.buf], w=[CLb.buf])
                        cl, clx, cr = CLb, Tt, Qc
                    E('act', lambda e, cl=cl: e.activation(out=E1.t[:], in_=cl.t[:], func=AF.Exp), r=[cl.buf], w=[E1.buf])
                    E('act', lambda e, clx=clx: e.activation(out=E1x.t[:], in_=clx.t[:], func=AF.Exp), r=[clx.buf], w=[E1x.buf])
                    E('act', lambda e, cl=cl: e.activation(out=Em.t[:], in_=cl.t[:], func=AF.Exp, scale=-1.0), r=[cl.buf], w=[Em.buf])
                    E('act', lambda e, cr=cr: e.activation(out=Er.t[:], in_=cr.t[:], func=AF.Exp), r=[cr.buf], w=[Er.buf])
                    E('dve', lambda e: e.scalar_tensor_tensor(out=ZR.t[:, :, A0], in0=v3(kk_), scalar=-1.0, in1=v3(E1x), op0=ALU.mult, op1=ALU.mult),
                      r=[kk_.buf, E1x.buf], aw=[ZR.buf])
                    E('pool', lambda e: e.tensor_tensor(out=ZR.t[:, :, A1], in0=v3(r_), in1=v3(E1), op=ALU.mult), r=[r_.buf, E1.buf], aw=[ZR.buf])
                    E('dve', lambda e: e.tensor_tensor(out=Bt.t[:], in0=v3(b_), in1=v3(Em), op=ALU.mult), r=[b_.buf, Em.buf], w=[Bt.buf])
                    E('pool', lambda e, z=z: e.tensor_tensor(out=Kt.t[:], in0=v3(kt_[z]), in1=v3(Em), op=ALU.mult), r=[kt_[z].buf, Em.buf], w=[Kt.buf])
                    E('dve', lambda e: e.tensor_tensor(out=Bh.t[:], in0=v3(b_), in1=v3(Er), op=ALU.mult), r=[b_.buf, Er.buf], w=[Bh.buf])
                    E('pool', lambda e, z=z: e.tensor_tensor(out=Kh.t[:], in0=v3(kt_[z]), in1=v3(Er), op=ALU.mult), r=[kt_[z].buf, Er.buf], w=[Kh.buf])
                    mz = m2[z]
                    chunk_mm(pA, slice(0, 128), lambda c: Bt.t[:, c, :], lambda c: ZR.t[:, c, :], [Bt.buf, ZR.buf])
                    E('dve', lambda e, mz=mz: e.tensor_tensor(out=SP.t[:, :, A1], in0=pA.t[:, :, A0], in1=mz.t[:, :, A0], op=ALU.mult),
                      r=[pA.buf, mz.buf], aw=[SP.buf])
                    E('dve', lambda e, mz=mz: e.tensor_tensor(out=Abr.t[:], in0=pA.t[:, :, A1], in1=mz.t[:, :, A1], op=ALU.mult),
                      r=[pA.buf, mz.buf], w=[Abr.buf])
                    E('pool', lambda e: e.tensor_tensor(out=SP.t[:, :, A0], in0=SP.t[:, :, A1], in1=I8.t[:], op=ALU.add), r=[SP.buf, I8.buf], aw=[SP.buf])
                    chunk_mm(pB, slice(0, 128), lambda c: Kt.t[:, c, :], lambda c: ZR.t[:, c, :], [Kt.buf, ZR.buf])
                    E('dve', lambda e, mz=mz: e.tensor_tensor(out=MK.t[:], in0=pB.t[:], in1=mz.t[:], op=ALU.mult), r=[pB.buf, mz.buf], w=[MK.buf])
                    chunk_mm(pC, slice(0, 64), lambda c: ZR.t[:, c, A0], lambda c: Bt.t[:, c, :], [Bt.buf, ZR.buf])
                    mT_ = m2[1 - z]
                    E('dve', lambda e, mT_=mT_: e.tensor_tensor(out=PT.t[:], in0=pC.t[:], in1=mT_.t[:, :, A0], op=ALU.mult), r=[pC.buf, mT_.buf], w=[PT.buf])
                    for kstep in range(5):
                        chunk_mm(pA, slice(0, 64), lambda c: PT.t[:, c, :], lambda c: SP.t[:, c, A1], [PT.buf, SP.buf])
                        chunk_mm(pD, slice(0, 64), lambda c: SP.t[:, c, A1], lambda c: PT.t[:, c, :], [PT.buf, SP.buf])
                        E('act', lambda e: e.activation(out=SP.t[:, :, A1], in_=pA.t[:, :, A0], func=AF.Copy), r=[pA.buf], w=[SP.buf])
                        E('dve', lambda e: e.tensor_copy(out=PT.t[:], in_=pD.t[:]), r=[pD.buf], w=[PT.buf])
                        chunk_mm(pC, slice(0, 64), lambda c: PT.t[:, c, :], lambda c: SP.t[:, c, A0], [PT.buf, SP.buf])
                        E('dve', lambda e: e.tensor_tensor(out=SP.t[:, :, A0], in0=pC.t[:], in1=SP.t[:, :, A0], op=ALU.add), r=[pC.buf, SP.buf], w=[SP.buf])
                    Tm = lambda c: SP.t[:, c, A0]
                    chunk_tr(pTr, lambda c: ZR.t[:, c, A0], [ZR.buf])
                    E('dve', lambda e: e.tensor_copy(out=ZT.t[:], in_=pTr.t[:]), r=[pTr.buf], w=[ZT.buf])
                    chunk_tr(pTr2, lambda c: Bh.t[:, c, :], [Bh.buf])
                    E('act', lambda e: e.activation(out=BhT.t[:], in_=pTr2.t[:], func=AF.Copy), r=[pTr2.buf], w=[BhT.buf])
                    chunk_tr(pTr, lambda c: Kh.t[:, c, :], [Kh.buf])
                    E('dve', lambda e: e.tensor_copy(out=KhT.t[:], in_=pTr.t[:]), r=[pTr.buf], w=[KhT.buf])
                    chunk_mm(pD, slice(0, 64), lambda c: MK.t[:, c, A0], lambda c: VT.t[:, c, :], [MK.buf, VT.buf])
                    E('act', lambda e: e.activation(out=X0T.t[:], in_=pD.t[:], func=AF.Copy), r=[pD.buf], w=[X0T.buf])
                    chunk_mm(pC, slice(0, 64), Tm, lambda c: X0T.t[:, c, :], [SP.buf, X0T.buf])
                    E('dve', lambda e: e.tensor_copy(out=U0T.t[:], in_=pC.t[:]), r=[pC.buf], w=[U0T.buf])
                    chunk_mm(pD, slice(0, 64), Tm, lambda c: ZT.t[:, c, :], [SP.buf, ZT.buf])
                    E('act', lambda e: e.activation(out=ZpT.t[:], in_=pD.t[:], func=AF.Copy), r=[pD.buf], w=[ZpT.buf])
                    chunk_mm(pC, slice(0, 64), lambda c: ZpT.t[:, c, :], lambda c: BhT.t[:, c, :], [ZpT.buf, BhT.buf])
                    for c in range(NCG):
                        E('dve', lambda e, c=c, z=z, cb=cb: e.scalar_tensor_tensor(out=GT.t[:, z, cb + c, :], in0=I8.t[:, 0, :], scalar=WC.t[:, c:c + 1],
                                                                                 in1=pC.t[:, c, :], op0=ALU.mult, op1=ALU.add),
                          r=[I8.buf, WC.buf, pC.buf], aw=[GT.buf])
                    chunk_mm(pD, slice(0, 64), lambda c: BhT.t[:, c, :], lambda c: U0T.t[:, c, :], [BhT.buf, U0T.buf, KhT.buf, VT.buf],
                             lhs2=lambda c: KhT.t[:, c, :], rhs2=lambda c: VT.t[:, c, :])
                    E('act', lambda e, z=z, cb=cb: e.activation(out=Hh.t[:, z, cb:cb + NCG, :], in_=pD.t[:], func=AF.Copy), r=[pD.buf], aw=[Hh.buf])
                    chunk_mm(pC, slice(0, 64), lambda c: ZpT.t[:, c, :], lambda c: Abr.t[:, c, :], [ZpT.buf, Abr.buf])
                    E('dve', lambda e, z=z, cb=cb: e.tensor_tensor(out=Rp.t[:, z, cb:cb + NCG, :], in0=pC.t[:], in1=ZR.t[:, :, A1], op=ALU.add),
                      r=[pC.buf, ZR.buf], aw=[Rp.buf])
                    chunk_mm(pD, slice(0, 64), lambda c: U0T.t[:, c, :], lambda c: Abr.t[:, c, :], [U0T.buf, Abr.buf, VT.buf, MK.buf],
                             lhs2=lambda c: VT.t[:, c, :], rhs2=lambda c: MK.t[:, c, A1])
                    E('act', lambda e, z=z, cb=cb: e.activation(out=Y0.t[:, z, cb:cb + NCG, :], in_=pD.t[:], func=AF.Copy), r=[pD.buf], aw=[Y0.buf])
                for z in range(2):
                    do_z(z)
                E('pool', lambda e: e.tensor_tensor(out=t3.t[:], in0=kt_[0].t[:], in1=kt_[1].t[:], op=ALU.add), r=[kt_[0].buf, kt_[1].buf], w=[t3.buf])
                E('dve', lambda e, c0=c0: e.scalar_tensor_tensor(out=rkk.t[:, c0:c0 + 512], in0=r_.t[:], scalar=col('rk', h), in1=t3.t[:], op0=ALU.mult, op1=ALU.mult),
                  r=[r_.buf, ca.buf, t3.buf], aw=[rkk.buf])
                dma(C, 'pool', sgfull.t[:, c0:c0 + 512], C.hA[26 + hp, hr:hr + 64, c0:c0 + 512], awrites=[sgfull.buf])
            for cg in range(NTG):
                do_cg(cg)
            _scan_chain(C, E, GT, Hh, Sall, NCHK, pA, pB)
            def do_out(cg):
                cb = cg * NCG
                c0 = cg * 512
                for z in range(2):
                    pz = pC if z == 0 else pD
                    chunk_mm(pz, slice(0, 64), lambda c, z=z, cb=cb: Sall.t[:, z, cb + c, :], lambda c, z=z, cb=cb: Rp.t[:, z, cb + c, :], [Sall.buf, Rp.buf])
                E('dve', lambda e, cb=cb: e.tensor_tensor(out=v3(t1), in0=pC.t[:], in1=Y0.t[:, 0, cb:cb + NCG, :], op=ALU.add), r=[pC.buf, Y0.buf], w=[t1.buf])
                E('dve', lambda e, cb=cb: e.tensor_tensor(out=v3(t2), in0=pD.t[:], in1=Y0.t[:, 1, cb:cb + NCG, :], op=ALU.add), r=[pD.buf, Y0.buf], w=[t2.buf])
                E('pool', lambda e: e.tensor_tensor(out=t1.t[:], in0=t1.t[:], in1=t2.t[:], op=ALU.add), r=[t1.buf, t2.buf], w=[t1.buf])
                E('act', lambda e: e.activation(out=vb.t[:], in_=t1.t[:], func=AF.Copy), r=[t1.buf], w=[vb.buf])
                E('act', lambda e: e.activation(out=sqk.t[:], in_=t1.t[:], func=AF.Square), r=[t1.buf], w=[sqk.buf])
                E('pe', lambda e: e.matmul(pA.t[:, 0:4, :].rearrange("p c f -> p (c f)"), on64.t[:], vb.t[:], start=True, stop=True), r=[on64.buf, vb.buf], w=[pA.buf])
                E('pe', lambda e: e.matmul(pB.t[:, 0:4, :].rearrange("p c f -> p (c f)"), on64.t[:], sqk.t[:], start=True, stop=True), r=[on64.buf, sqk.buf], w=[pB.buf])
                mean_ps = lambda: pA.t[:, 0:4, :].rearrange("p c f -> p (c f)")
                sq_ps = lambda: pB.t[:, 0:4, :].rearrange("p c f -> p (c f)")
                E('act', lambda e: e.activation(out=t2.t[:], in_=mean_ps(), func=AF.Copy, scale=1.0 / 64), r=[pA.buf], w=[t2.buf])
                E('dve', lambda e: e.tensor_tensor(out=t3.t[:], in0=t2.t[:], in1=t2.t[:], op=ALU.mult), r=[t2.buf], w=[t3.buf])
                E('dve', lambda e: e.scalar_tensor_tensor(out=t3.t[:], in0=sq_ps(), scalar=1.0 / 64, in1=t3.t[:], op0=ALU.mult, op1=ALU.subtract),
                  r=[pB.buf, t3.buf], w=[t3.buf])
                E('act', lambda e: e.activation(out=t3.t[:], in_=t3.t[:], func=AF.Sqrt, bias=64e-5, scale=1.0), r=[t3.buf], w=[t3.buf])
                E('dve', lambda e: e.reciprocal(out=t3.t[:], in_=t3.t[:]), r=[t3.buf], w=[t3.buf])
                E('pool', lambda e: e.tensor_tensor(out=t1.t[:], in0=t1.t[:], in1=t2.t[:], op=ALU.subtract), r=[t1.buf, t2.buf], w=[t1.buf])
                E('dve', lambda e: e.tensor_tensor(out=t1.t[:], in0=t1.t[:], in1=t3.t[:], op=ALU.mult), r=[t1.buf, t3.buf], w=[t1.buf])
                E('dve', lambda e, h=h: e.tensor_scalar(out=t1.t[:], in0=t1.t[:], scalar1=col('lnxw', h), scalar2=col('lnxb', h), op0=ALU.mult, op1=ALU.add),
                  r=[t1.buf, ca.buf], w=[t1.buf])
                E('pe', lambda e, c0=c0: e.matmul(pA.t[:, 4:8, :].rearrange("p c f -> p (c f)"), on64.t[:], rkk.t[:, c0:c0 + 512], start=True, stop=True),
                  r=[on64.buf, rkk.buf], w=[pA.buf])
                E('dve', lambda e, c0=c0: e.tensor_tensor(out=t2.t[:], in0=pA.t[:, 4:8, :].rearrange("p c f -> p (c f)"), in1=vfull.t[:, c0:c0 + 512], op=ALU.mult),
                  r=[pA.buf, vfull.buf], w=[t2.buf])
                E('pool', lambda e: e.tensor_tensor(out=t1.t[:], in0=t1.t[:], in1=t2.t[:], op=ALU.add), r=[t1.buf, t2.buf], w=[t1.buf])
                y_ = yst[cg % 2]
                E('pool', lambda e, c0=c0, y_=y_: e.tensor_tensor(out=y_.t[:], in0=t1.t[:], in1=sgfull.t[:, c0:c0 + 512], op=ALU.mult),
                  r=[t1.buf, sgfull.buf], w=[y_.buf])
                dma(C, 'sp', C.yT[0][hp, hr:hr + 64, c0:c0 + 512], y_.t[:], reads=[y_.buf], awrites=[C.dbuf('yT0', 0)])
            for cg in range(NTG):
                do_out(cg)
        for h in range(16):
            do_head(h)
    P.barrier()


def _scan_chain(C, E, GT, Hh, Sall, NCHK, pA, pB):
    ztile = [Buf(), Buf()]
    for z in range(2):
        first = 0 if z == 0 else NCHK - 1
        E('pool', lambda e, z=z, first=first: e.memset(Sall.t[:, z, first, :], 0.0), w=[ztile[z]], aw=[Sall.buf])
    for i in range(NCHK - 1):
        for z in range(2):
            c = i if z == 0 else NCHK - 1 - i
            nxt = c + 1 if z == 0 else c - 1
            pz = pA if z == 0 else pB
            E('pe', lambda e, z=z, c=c, pz=pz: e.matmul(pz.t[:, 0, 0:64], GT.t[:, z, c, :], Sall.t[:, z, c, :], start=True, stop=True),
              r=[GT.buf, ztile[z]], w=[pz.buf])
            E('dve', lambda e, z=z, c=c, nxt=nxt, pz=pz: e.tensor_tensor(out=Sall.t[:, z, nxt, :], in0=pz.t[:, 0, 0:64], in1=Hh.t[:, z, c, :], op=ALU.add),
              r=[pz.buf, Hh.buf], w=[ztile[z]], aw=[Sall.buf])


def phase_C(C, l, T):
    P = C.P
    w = C.w_in[l]
    R = T // 64
    with ExitStack() as es:
        xT = sb(C, es, "xT", [128, KC, TB], BF16, nbuf=4)
        ws = WStream(C, es, KC, 512)
        q_sb = sb(C, es, "cq_sb", [128, 1, 512], F32)
        sq = sb(C, es, "csq", [128, 1, 512], BF16)
        rs = sb(C, es, "crs", [128, 512], F32)
        qst = [sb(C, es, "cqst", [128, 1, 512], BF16) for _ in range(2)]
        stage = [sb(C, es, "cstage", [128, 512], BF16) for _ in range(4)]
        pm = [ps(C, es, "pmc", [128, 512]) for _ in range(4)]
        paux = ps(C, es, "pauxc", [128, 512])
        cnt = [0]
        jobs = []
        for tb in range(T // TB):
            t0 = tb * TB

            def job_qk(slot, j, which, tb=tb, t0=t0):
                if which == 'q' and j == 0:
                    load_xT(C, xT, C.xnT, t0, TB)
                dst = C.qT if which == 'q' else C.kT
                gcol = C.col('cq' if which == 'q' else 'ck', l)
                for ct in range(4):
                    pp = [pm[(cnt[0] % 2) * 2], pm[(cnt[0] % 2) * 2 + 1]]
                    cnt[0] += 1
                    gemm_fm(C, slot, KC, xT, ct, 2, pp)
                    for tg in range(2):
                        o_ = qst[tg]
                        headnorm_fm(C, [pp[tg]], q_sb, sq, paux, rs, [gcol], o_, [0], 64, 1e-6, 512, ones=C.bones)
                        dma(C, 'sp', dst[j * 4 + ct, :, t0 + tg * 512:t0 + (tg + 1) * 512], o_.t[:, 0, :], reads=[o_.buf],
                            awrites=[C.dbuf('cqk', tb)])

            def job_v(slot, j, tb=tb, t0=t0):
                for tt in range(8):
                    k = cnt[0] % 4
                    cnt[0] += 1
                    gemm_tm(C, slot, KC, xT, tt, 512, pm[k])
                    P.emit('act', lambda e, k=k: e.activation(out=stage[k].t[:], in_=pm[k].t[:], func=AF.Copy),
                           reads=[pm[k].buf], writes=[stage[k].buf])
                    r0 = t0 + tt * 128
                    dma(C, 'sp', C.vC[r0:r0 + 128, j * 512:(j + 1) * 512], stage[k].t[:], reads=[stage[k].buf],
                        awrites=[C.dbuf('cv', tb)])

            def job_g(slot, j, tb=tb, t0=t0):
                for ct in range(4):
                    pp = [pm[(cnt[0] % 2) * 2], pm[(cnt[0] % 2) * 2 + 1]]
                    cnt[0] += 1
                    gemm_fm(C, slot, KC, xT, ct, 2, pp)
                    for tg in range(2):
                        k = (cnt[0] * 2 + tg) % 4
                        P.emit('act', lambda e, k=k, p_=pp[tg]: e.activation(out=stage[k].t[:], in_=p_.t[:], func=AF.Silu),
                               reads=[pp[tg].buf], writes=[stage[k].buf])
                        dma(C, 'sp', C.sgC[j * 4 + ct, :, t0 + tg * 512:t0 + (tg + 1) * 512], stage[k].t[:], reads=[stage[k].buf],
                            awrites=[C.dbuf('cg', tb)])
            for j in range(2):
                jobs.append((w[:, O2 + j * 512:O2 + (j + 1) * 512], KC, 512, lambda s, j=j, f=job_qk: f(s, j, 'q')))
            for j in range(2):
                jobs.append((w[:, O2 + BW + j * 512:O2 + BW + (j + 1) * 512], KC, 512, lambda s, j=j, f=job_qk: f(s, j, 'k')))
            for j in range(2):
                jobs.append((w[:, O2 + 2 * BW + j * 512:O2 + 2 * BW + (j + 1) * 512], KC, 512, lambda s, j=j, f=job_v: f(s, j)))
            for j in range(2):
                jobs.append((w[:, O2 + 3 * BW + j * 512:O2 + 3 * BW + (j + 1) * 512], KC, 512, lambda s, j=j, f=job_g: f(s, j)))
        ws.run(jobs)
    P.barrier()
    NT = T // 128
    if C.dbg.get('c_noattn'):
        return
    with ExitStack() as es:
        bufs = []
        for i in range(2):
            bufs.append(dict(
                q0=sb(C, es, "aq0", [128, T], BF16), q1=sb(C, es, "aq1", [128, T], BF16),
                k=sb(C, es, "ak", [128, T], BF16), g=sb(C, es, "ag", [128, T], BF16),
                ve=sb(C, es, "ave", [128, NT, 128], BF16), vo=sb(C, es, "avo", [128, NT, 128], BF16),
                tb=sb(C, es, "atb", [128, 2, 15, 64], F32), y=sb(C, es, "ay", [128, T], BF16)))
        sc = [sb(C, es, "asc", [128, 512], F32) for _ in range(2)]
        E = [sb(C, es, "aE", [128, 2, 4, 64], BF16) for _ in range(2)]
        rden = [sb(C, es, "arden", [128, 64], F32) for _ in range(2)]
        o1 = [sb(C, es, "ao1", [128, 64], F32) for _ in range(2)]
        pS = [ps(C, es, "apS", [128, 2, 4, 64]) for _ in range(2)]
        pOD = [ps(C, es, "apOD", [128, 4, 64]) for _ in range(2)]
        for b in bufs:
            P.emit('pool', lambda e, b=b: e.memset(b['q0'].t[64:128, :], 0.0), writes=[b['q0'].buf])
            P.emit('pool', lambda e, b=b: e.memset(b['q1'].t[0:64, :], 0.0), writes=[b['q1'].buf])
        for ct in range(8):
            b = bufs[ct % 2]
            dma(C, 'sp', b['q0'].t[0:64, :], C.qT[ct, 0:64, :], writes=[b['q0'].buf])
            dma(C, 'sp', b['q1'].t[64:128, :], C.qT[ct, 64:128, :], writes=[b['q1'].buf])
            dma(C, 'sp', b['k'].t[:], C.kT[ct], writes=[b['k'].buf])
            dma(C, 'sp', b['g'].t[:], C.sgC[ct], writes=[b['g'].buf])
            dma(C, 'sp', b['ve'].t[:], C.vC[:, ct * 128:(ct + 1) * 128].rearrange("(n p) c -> p n c", p=128), writes=[b['ve'].buf])
            dma(C, 'sp', b['vo'].t[:, 0:NT - 1, :],
                C.vC[64:T - 64, ct * 128:(ct + 1) * 128].rearrange("(n p) c -> p n c", p=128), writes=[b['vo'].buf])
            dma(C, 'sp', b['tb'].t[:], C.prm['rpbT'][l, ct], writes=[b['tb'].buf])
            for i in range(R):
                k = i % 2
                si = min(max(i - 4, 0), R - 8)
                rel0 = si - i + 7

                def mms(e, b=b, i=i, si=si, k=k):
                    ins = None
                    for h2 in range(2):
                        for kt in range(4):
                            tk = (si + 2 * kt) * 64
                            ins = e.matmul(pS[k].t[:, h2, kt, :], b['k'].t[:, tk:tk + 128],
                                           b['q%d' % h2].t[:, i * 64:(i + 1) * 64], start=True, stop=True)
                    return ins
                lvl = C.dbg.get('c_lvl', 9)
                if lvl < 2:
                    continue
                P.emit('pe', mms, reads=[b['k'].buf, b['q0'].buf, b['q1'].buf], writes=[pS[k].buf])
                if lvl == 21:
                    continue
                for h2 in range(2):
                    P.emit('dve', lambda e, b=b, k=k, rel0=rel0, h2=h2: e.scalar_tensor_tensor(
                        out=sc[k].t[:, h2 * 256:(h2 + 1) * 256].rearrange("p (t q) -> p t q", t=4), in0=pS[k].t[:, h2, :, :], scalar=0.125,
                        in1=b['tb'].t[:, h2, rel0:rel0 + 8:2, :], op0=ALU.mult, op1=ALU.add),
                        reads=[pS[k].buf, b['tb'].buf], awrites=[sc[k].buf])
                if lvl == 22:
                    continue
                P.emit('act', lambda e, k=k: e.activation(out=E[k].t[:].rearrange("p h t q -> p (h t q)"), in_=sc[k].t[:], func=AF.Exp),
                       reads=[sc[k].buf], writes=[E[k].buf])

                if lvl < 3:
                    continue

                def mmo(e, b=b, si=si, k=k):
                    ins = None
                    for h2 in range(2):
                        for kt in range(4):
                            row = si + 2 * kt
                            vt = b['ve'].t[:, row // 2, :] if row % 2 == 0 else b['vo'].t[:, row // 2, :]
                            ins = e.matmul(pOD[k].t[:, h2, :], vt, E[k].t[:, h2, kt, :], start=(kt == 0), stop=(kt == 3))
                    for h2 in range(2):
                        for kt in range(4):
                            ins = e.matmul(pOD[k].t[:, 2 + h2, :], C.ones.t[:], E[k].t[:, h2, kt, :], start=(kt == 0), stop=(kt == 3))
                    return ins
                P.emit('pe', mmo, reads=[b['ve'].buf, b['vo'].buf, E[k].buf, C.ones.buf], writes=[pOD[k].buf])
                for h2 in range(2):
                    sl = slice(h2 * 64, (h2 + 1) * 64)
                    P.emit('dve', lambda e, k=k, sl=sl, h2=h2: e.reciprocal(out=rden[k].t[sl, :], in_=pOD[k].t[sl, 2 + h2, :]),
                           reads=[pOD[k].buf], awrites=[rden[k].buf])
                    P.emit('dve', lambda e, k=k, sl=sl, h2=h2: e.tensor_tensor(out=o1[k].t[sl, :], in0=pOD[k].t[sl, h2, :], in1=rden[k].t[sl, :], op=ALU.mult),
                           reads=[pOD[k].buf, rden[k].buf], awrites=[o1[k].buf])
                P.emit('pool', lambda e, k=k, b=b, i=i: e.tensor_tensor(out=b['y'].t[:, i * 64:(i + 1) * 64], in0=o1[k].t[:],
                                                                          in1=b['g'].t[:, i * 64:(i + 1) * 64], op=ALU.mult),
                       reads=[o1[k].buf, b['g'].buf], awrites=[b['y'].buf])
            dma(C, 'sp', C.yT[2][ct], b['y'].t[:], reads=[b['y'].buf], awrites=[C.dbuf('yT2', 0)])
    P.barrier()


def phase_merge(C, l, T):
    P = C.P
    w = C.w_in[l]
    CW = 256
    with ExitStack() as es:
        xT = sb(C, es, "xT", [128, KC, TB], BF16, nbuf=4)
        yTs = [sb(C, es, "yTs", [128, 8, TB], BF16, nbuf=1) for _ in range(2)]
        wg = WStream(C, es, KC, CW)
        wb = WStream(C, es, 8, CW)
        acc = [[sb(C, es, "acc", [128, 512], F32) for _ in range(2)] for _ in range(2)]
        sig = [sb(C, es, "sig", [128, 512], F32) for _ in range(2)]
        prod = [sb(C, es, "prod", [128, 512], F32) for _ in range(2)]
        ostage = [sb(C, es, "ostage", [128, 512], BF16) for _ in range(4)]
        psg = [ps(C, es, "psg", [128, 512]) for _ in range(4)]
        psp = [ps(C, es, "psp", [128, 512]) for _ in range(4)]
        cnt = [0]
        for tb in range(T // TB):
            t0 = tb * TB
            load_xT(C, xT, C.xnT, t0, TB)
            units = [(cb, n) for cb in range(D // CW) for n in range(4)]
            load_w(C, wg.slots[0], w[:, O4 + units[0][1] * D + units[0][0] * CW:O4 + units[0][1] * D + (units[0][0] + 1) * CW], KC, CW)
            load_w(C, wb.slots[0], C.w_br[l][units[0][1]][:, units[0][0] * CW:(units[0][0] + 1) * CW], 8, CW)
            for ui, (cb, n) in enumerate(units):
                if ui + 1 < len(units):
                    cb2, n2 = units[ui + 1]
                    load_w(C, wg.slots[(ui + 1) % 2], w[:, O4 + n2 * D + cb2 * CW:O4 + n2 * D + (cb2 + 1) * CW], KC, CW)
                    load_w(C, wb.slots[(ui + 1) % 2], C.w_br[l][n2][:, cb2 * CW:(cb2 + 1) * CW], 8, CW)
                gs, bs = wg.slots[ui % 2], wb.slots[ui % 2]
                ys = yTs[ui % 2]
                dma(C, 'sp', ys.t[:, :, :], C.yT[n][:, :, t0:t0 + TB].rearrange("kc p t -> p kc t"), writes=[ys.buf])
                for ct in range(CW // 128):
                    k = cnt[0] % 2
                    cnt[0] += 1
                    pg = [psg[k * 2], psg[k * 2 + 1]]
                    pp = [psp[k * 2], psp[k * 2 + 1]]
                    gemm_fm(C, gs, KC, xT, ct, 2, pg)
                    gemm_fm(C, bs, 8, ys, ct, 2, pp)
                    for tg in range(2):
                        a_ = acc[ct][tg]
                        P.emit('act', lambda e, tg=tg, pg=pg: e.activation(out=sig[tg].t[:], in_=pg[tg].t[:], func=AF.Sigmoid),
                               reads=[pg[tg].buf], writes=[sig[tg].buf])
                        if n == 0:
                            P.emit('dve', lambda e, tg=tg, pp=pp, a_=a_: e.tensor_tensor(out=a_.t[:], in0=sig[tg].t[:], in1=pp[tg].t[:], op=ALU.mult),
                                   reads=[sig[tg].buf, pp[tg].buf], writes=[a_.buf])
                        else:
                            P.emit('dve', lambda e, tg=tg, pp=pp: e.tensor_tensor(out=prod[tg].t[:], in0=sig[tg].t[:], in1=pp[tg].t[:], op=ALU.mult),
                                   reads=[sig[tg].buf, pp[tg].buf], writes=[prod[tg].buf])
                            if n < 3:
                                P.emit('pool', lambda e, tg=tg, a_=a_: e.tensor_tensor(out=a_.t[:], in0=a_.t[:], in1=prod[tg].t[:], op=ALU.add),
                                       reads=[a_.buf, prod[tg].buf], writes=[a_.buf])
                            else:
                                o_ = ostage[(cnt[0] * 2 + tg) % 4]
                                P.emit('pool', lambda e, tg=tg, a_=a_, o_=o_: e.tensor_tensor(out=o_.t[:], in0=a_.t[:], in1=prod[tg].t[:], op=ALU.add),
                                       reads=[a_.buf, prod[tg].buf], writes=[o_.buf])
                                kc = cb * (CW // 128) + ct
                                dma(C, 'sp', C.mT[kc, :, t0 + tg * 512:t0 + (tg + 1) * 512], o_.t[:], reads=[o_.buf],
                                    awrites=[C.dbuf('mT', tb)])
    P.barrier()


def phase_out(C, l, x_src, x_dst, T):
    P = C.P
    with ExitStack() as es:
        mTs = sb(C, es, "mTs", [128, KC, TB], BF16, nbuf=4)
        ws = WStream(C, es, KC, 512)
        xin = [sb(C, es, "xin", [128, 512], F32) for _ in range(4)]
        xo = [sb(C, es, "xo", [128, 512], F32) for _ in range(4)]
        pm = [ps(C, es, "pmo", [128, 512]) for _ in range(4)]
        cnt = [0]
        jobs = []
        for tb in range(T // TB):
            t0 = tb * TB

            def job(slot, cb, tb=tb, t0=t0):
                if cb == 0:
                    load_xT(C, mTs, C.mT, t0, TB)
                for tt in range(TB // 128):
                    k = cnt[0] % 4
                    cnt[0] += 1
                    r0 = t0 + tt * 128
                    dma(C, 'sp', xin[k].t[:], x_src[r0:r0 + 128, cb * 512:(cb + 1) * 512], writes=[xin[k].buf])
                    gemm_tm(C, slot, KC, mTs, tt, 512, pm[k])
                    P.emit('dve', lambda e, k=k: e.tensor_tensor(out=xo[k].t[:], in0=pm[k].t[:], in1=xin[k].t[:], op=ALU.add),
                           reads=[pm[k].buf, xin[k].buf], writes=[xo[k].buf])
                    dma(C, 'sp', x_dst[r0:r0 + 128, cb * 512:(cb + 1) * 512], xo[k].t[:], reads=[xo[k].buf],
                        awrites=[C.dbuf('xout', tb)])
            for cb in range(D // 512):
                jobs.append((C.w_out[l][:, cb * 512:(cb + 1) * 512], KC, 512, lambda s, cb=cb, f=job: f(s, cb)))
        ws.run(jobs)
    P.barrier()


WIN_GROUPS = [(0, O1), (O1, O2), (O2, O3), (O3, O4)] + [(O4 + n * D, O4 + (n + 1) * D) for n in range(4)]
COLS = {'mq': 0, 'mk': 2, 'cq': 4, 'ck': 5, 'conv': 6, 'w0': 84, 'a0': 100, 'kk': 116, 'ka': 124, 'rk': 132,
        'lnxw': 140, 'lnxb': 148}
NCOLS = 160
NCST = 3 * 128


def build_program(T=SEQ, n_layers=DEPTH, dbg=None, NB=1):
    dbg = dbg or {}
    L = n_layers
    nc = bass.Bass("TRN2", target_bir_lowering=False)
    C = Ctx()
    C.nc = nc
    C.uid = 0
    C.T = T
    C.dbg = dbg

    def din(name, shape, dt=F32):
        return nc.dram_tensor(name, list(shape), dt, kind="ExternalInput").ap()

    def dscr(name, shape, dt):
        return nc.dram_tensor(name, list(shape), dt, kind="Internal").ap()

    x_all = din("x", [NB, T, D])
    mem_all = din("mem", [NB, MEM, D])
    nsh = dbg.get('nshard', 0)
    C.wq = 'pool'
    gathers = []
    if nsh == 0:
        C.w_in = din("w_in", [L, D, IN_W])
        C.w_kv = din("w_kv", [L, D, 2 * BW])
        C.w_br = din("w_br", [L, 4, BW, D])
        C.w_out = din("w_out", [L, D, D])
    else:
        def sharded(name, rows, cols):
            src = din(name, [L, rows // nsh, cols])
            outl = []
            for l in range(L):
                bnc = dscr("%s_b%d" % (name, l), [rows // nsh, cols], BF16)
                full = dscr("%s_f%d" % (name, l), [rows, cols], BF16)
                gathers.append((src[l], bnc, full, rows // nsh))
                outl.append(full)
            return outl
        wg = [sharded("win%d" % g, D, hi - lo) for g, (lo, hi) in enumerate(WIN_GROUPS)]
        C.w_in = [WView([(lo, hi, wg[g][l]) for g, (lo, hi) in enumerate(WIN_GROUPS)]) for l in range(L)]
        C.w_kv = sharded("wkv", D, 2 * BW)
        wbr = [sharded("wbr%d" % n, BW, D) for n in range(4)]
        C.w_br = [[wbr[n][l] for n in range(4)] for l in range(L)]
        C.w_out = sharded("wout", D, D)
    C.prm = {
        'g_bc': din("g_bc", [L, 128, D]), 'mg_bc': din("mg_bc", [L, 128, D]),
        'lng_bc': din("lng_bc", [L, 128, BW]), 'lnb_bc': din("lnb_bc", [L, 128, BW]),
        'bs_bc': din("bs_bc", [L, 128, 8, 128]), 'wsT': din("wsT", [L, 128, 8, 128]),
        'cols': din("cols", [L, 128, NCOLS]), 'cst': din("cst", [128, NCST]),
        'rpbT': din("rpbT", [L, 8, 128, 2, 15, 64]),
        'colsA': din("colsA", [L, 64, NCA]), 'cstA': din("cstA", [64, 3136]),
    }
    y_all = nc.dram_tensor("y", [NB, T, D], F32, kind="ExternalOutput").ap()
    C.xnT = dscr("xnT", [KC, 128, T], BF16)
    C.yT = [dscr("yT%d" % n, [8, 128, T], BF16) for n in range(4)]
    C.mT = dscr("mT", [KC, 128, T], BF16)
    C.qT = dscr("qT", [8, 128, T], BF16)
    C.kT = dscr("kT", [8, 128, T], BF16)
    C.sgC = dscr("sgC", [8, 128, T], BF16)
    C.vC = dscr("vC", [T, BW], BF16)
    C.hA = dscr("hA", [34, 128, T], F32)
    C.a_w_up = din("a_w_up", [L, 2, 64, BW])
    C.a_a_up = din("a_a_up", [L, 2, 64, BW])
    xbuf = [dscr("xbuf%d" % i, [T, D], F32) for i in range(2)]
    dbg_in = {k: din("in_" + k, [8, 128, T]) for k in dbg.get('yT_in', [])}
    dbg_out = {k: nc.dram_tensor("dbg_" + k, [8, 128, T], F32, kind="ExternalOutput").ap() for k in dbg.get('yT_out', [])}
    if dbg.get('mT_out'):
        dbg_out['mT'] = nc.dram_tensor("dbg_mT", [KC, 128, T], F32, kind="ExternalOutput").ap()
    dbufs = {}

    def dbuf(name, idx):
        k = (name, idx)
        if k not in dbufs:
            dbufs[k] = Buf()
        return dbufs[k]
    C.dbuf = dbuf

    with ExitStack() as es:
        P = Prog(nc, es)
        C.P = P
        C.ident = sb(C, es, "ident", [128, 128], BF16)
        C.ones = sb(C, es, "ones", [128, 128], BF16)
        C.bones = sb(C, es, "bones", [128, 128], BF16)
        C.cols = sb(C, es, "cols", [128, NCOLS], F32)
        C.col = lambda name, l, i=0: C.cols.t[:, COLS[name] + i:COLS[name] + i + 1]
        cst = C.prm['cst']
        dma(C, 'pool', C.ident.t[:], cst[:, 0:128], writes=[C.ident.buf])
        dma(C, 'pool', C.ones.t[:], cst[:, 128:256], writes=[C.ones.buf])
        dma(C, 'pool', C.bones.t[:], cst[:, 256:384], writes=[C.bones.buf])
        for src, bnc, full, rows in gathers:
            for r0 in range(0, rows, 128):
                dma(C, 'pool', bnc[r0:r0 + 128, :], src[r0:r0 + 128, :], awrites=[dbuf('bnc', id(bnc))])
            P.emit('pool', lambda e, bnc=bnc, full=full: e.collective_compute(
                "AllGather", ALU.bypass, replica_groups=[list(range(nsh))], ins=[bnc[:, :]], outs=[full[:, :]]),
                reads=[dbuf('bnc', id(bnc))], dma=True, chain=True)
        for k, ap in dbg_in.items():
            n = int(k[-1])
            for kc in range(8):
                dma(C, 'pool', C.yT[n][kc], ap[kc], awrites=[dbuf('dbgin', 0)])
        P.barrier()
        for b in range(NB):
            x_in, y_out = x_all[b], y_all[b]
            C.mem = mem_all[b]
            for l in range(L):
                x_src = x_in if l == 0 else xbuf[(l - 1) % 2]
                x_dst = y_out if l == L - 1 else xbuf[l % 2]
                dma(C, 'sp', C.cols.t[:], C.prm['cols'][l], writes=[C.cols.buf])
                phase_rmsnorm(C, x_src, C.prm['g_bc'][l], C.xnT, T)
                if 'A' not in dbg.get('skip', ''):
                    phase_A(C, l, T)
                if 'B' not in dbg.get('skip', ''):
                    phase_B(C, l, T)
                if 'C' not in dbg.get('skip', ''):
                    phase_C(C, l, T)
                if 'D' not in dbg.get('skip', ''):
                    phase_D(C, l, T)
                phase_merge(C, l, T)
                phase_out(C, l, x_src, x_dst, T)
        for k, ap in dbg_out.items():
            src = C.mT if k == 'mT' else C.yT[int(k[-1])]
            for kc in range(src.shape[0]):
                dma(C, 'pool', ap[kc], src[kc], awrites=[dbuf('dbgout', 0)])
        P.barrier()
        P.build()
    C.n_ops = P.n_ops
    return nc, C


def host_params(p, L):
    f = np.float32
    rep = lambda v: np.ascontiguousarray(np.broadcast_to(np.asarray(v, f)[:, None, :], (v.shape[0], 128, v.shape[1])))
    out = {}
    out['g_bc'] = rep(p['norm_g'][:L])
    out['mg_bc'] = rep(p['m_norm_g'][:L])
    out['lng_bc'] = rep(p['b_ln_g'][:L])
    out['lnb_bc'] = rep(p['b_ln_b'][:L])
    bs = np.asarray(p['b_b_s'][:L], f)
    out['bs_bc'] = np.ascontiguousarray(np.broadcast_to(bs[:, None, :, :], (L, 128, 8, 128)))
    out['wsT'] = np.ascontiguousarray(np.transpose(np.asarray(p['b_w_s'][:L], f), (0, 3, 1, 2)))
    cols = np.zeros((L, 128, NCOLS), f)
    colv = lambda v, n: np.transpose(np.asarray(v, f).reshape(L, n, 128), (0, 2, 1))
    cols[:, :, COLS['mq']:COLS['mq'] + 2] = colv(p['m_q_norm'][:L], 2)
    cols[:, :, COLS['mk']:COLS['mk'] + 2] = colv(p['m_k_norm'][:L], 2)
    cols[:, :, COLS['cq']] = np.tile(np.asarray(p['c_q_norm'][:L], f), (1, 2))
    cols[:, :, COLS['ck']] = np.tile(np.asarray(p['c_k_norm'][:L], f), (1, 2))
    conv = np.asarray(p['a_conv'][:L], f)
    cols[:, :, COLS['conv']:COLS['conv'] + 78] = np.transpose(conv.reshape(L, 3, 26, 128), (0, 3, 2, 1)).reshape(L, 128, 78)
    cols[:, :, COLS['w0']:COLS['w0'] + 16] = np.transpose(np.asarray(p['a_w0'][:L], f).reshape(L, 16, 128), (0, 2, 1))
    cols[:, :, COLS['a0']:COLS['a0'] + 16] = np.transpose(np.asarray(p['a_a0'][:L], f).reshape(L, 16, 128), (0, 2, 1))
    cols[:, :, COLS['kk']:COLS['kk'] + 8] = colv(p['a_k_k'][:L], 8)
    cols[:, :, COLS['ka']:COLS['ka'] + 8] = colv(p['a_k_a'][:L], 8)
    cols[:, :, COLS['rk']:COLS['rk'] + 8] = colv(np.asarray(p['a_r_k'][:L]).reshape(L, BW), 8)
    cols[:, :, COLS['lnxw']:COLS['lnxw'] + 8] = colv(p['a_lnx_w'][:L], 8)
    cols[:, :, COLS['lnxb']:COLS['lnxb'] + 8] = colv(p['a_lnx_b'][:L], 8)
    out['cols'] = cols
    rpb = np.asarray(p['c_rpb'][:L], f)
    qv = np.arange(64)[None, :]
    kv = np.arange(64)[:, None]
    dcol = np.clip(kv - qv, -15, 15) + 15
    sj = np.clip(qv - 8, 0, 48)
    valid = (kv >= sj) & (kv < sj + 16)
    tbl = rpb[:, :, :, dcol]
    tbl = np.where(valid[None, None, None], tbl, f(-1e30))
    lo = np.transpose(tbl, (0, 1, 3, 2, 4))
    hi = np.full_like(lo, f(-1e30))
    hi[:, :, :, 0:14, :] = lo[:, :, :, 1:15, :]
    t2 = np.concatenate([lo, hi], axis=2)
    out['rpbT'] = np.ascontiguousarray(np.transpose(t2.reshape(L, 8, 2, 128, 15, 64), (0, 1, 3, 2, 4, 5)))
    ca = np.zeros((L, 64, NCA), f)
    hd = lambda v: np.transpose(np.asarray(v, f).reshape(L, -1, 64), (0, 2, 1))
    for wi in range(3):
        for tap in range(3):
            ca[:, :, CA['conv'] + (wi * 16) * 3 + tap:CA['conv'] + (wi * 16 + 16) * 3:3] = hd(conv[:, tap, wi * BW:(wi + 1) * BW])
    for gi in range(4):
        for tap in range(3):
            ca[:, :, CA['convl'] + gi * 3 + tap] = conv[:, tap, 3 * BW + gi * 64:3 * BW + (gi + 1) * 64]
    ca[:, :, CA['w0']:CA['w0'] + 32] = hd(np.asarray(p['a_w0'][:L], f).reshape(L, 2 * BW))
    ca[:, :, CA['a0']:CA['a0'] + 32] = hd(np.asarray(p['a_a0'][:L], f).reshape(L, 2 * BW))
    ca[:, :, CA['kk']:CA['kk'] + 16] = hd(p['a_k_k'][:L])
    ca[:, :, CA['ka']:CA['ka'] + 16] = hd(p['a_k_a'][:L])
    ca[:, :, CA['rk']:CA['rk'] + 16] = hd(np.asarray(p['a_r_k'][:L], f).reshape(L, BW))
    ca[:, :, CA['lnxw']:CA['lnxw'] + 16] = hd(p['a_lnx_w'][:L])
    ca[:, :, CA['lnxb']:CA['lnxb'] + 16] = hd(p['a_lnx_b'][:L])
    out['colsA'] = ca
    out['a_w_up'] = np.ascontiguousarray(np.asarray(p['a_w_up'][:L], f))
    out['a_a_up'] = np.ascontiguousarray(np.asarray(p['a_a_up'][:L], f))
    ka = np.zeros((64, 3136), f)
    ka[:, 0:512] = 1.0
    ka[:, 0:512:64] = 0.0
    pi = np.arange(64)[:, None]
    fi = np.arange(64)[None, :]
    m2f = np.concatenate([(pi < fi), (pi <= fi)], axis=1).astype(f)
    m2b = np.concatenate([(pi > fi), (pi >= fi)], axis=1).astype(f)
    ka[:, 512:1536] = np.tile(m2f, (1, 8))
    ka[:, 1536:2560] = np.tile(m2b, (1, 8))
    ka[:, 2560:3072] = np.tile(np.eye(64, dtype=f), (1, 8))
    ka[:, 3072:3136] = 1.0
    out['cstA'] = ka
    cst = np.zeros((128, NCST), f)
    cst[:, 0:128] = np.eye(128, dtype=f)
    cst[:, 128:256] = 1.0
    cst[0:64, 256:320] = 1.0
    cst[64:128, 320:384] = 1.0
    out['cst'] = cst
    return out


N_CORES = 4
NB_PER_CORE = 1


def kernel(**inputs):
    p = {k: np.asarray(v) for k, v in inputs.items()}
    L = DEPTH
    nc, C = build_program(T=SEQ, n_layers=L, dbg={}, NB=NB_PER_CORE)
    hp = host_params(p, L)
    shared = dict(hp)
    shared['w_in'] = np.ascontiguousarray(p['w_in'], dtype=np.float32)
    shared['w_kv'] = np.ascontiguousarray(p['m_w_kv'], dtype=np.float32)
    shared['w_br'] = np.ascontiguousarray(p['w_branch'], dtype=np.float32)
    shared['w_out'] = np.ascontiguousarray(p['w_out'], dtype=np.float32)
    in_maps = []
    for c in range(N_CORES):
        m = dict(shared)
        m['x'] = np.ascontiguousarray(p['x'][c * NB_PER_CORE:(c + 1) * NB_PER_CORE], dtype=np.float32)
        m['mem'] = np.ascontiguousarray(p['mem'][c * NB_PER_CORE:(c + 1) * NB_PER_CORE], dtype=np.float32)
        in_maps.append(m)
    res = run_bass_kernel_spmd(nc, in_maps, core_ids=list(range(N_CORES)))
    return np.concatenate([np.asarray(r['y'], dtype=np.float32) for r in res.results], axis=0)
```

```python
import numpy as np
from contextlib import ExitStack

import concourse.bass as bass
import concourse.mybir as mybir
from concourse.bass_utils import run_bass_kernel_spmd

F32 = mybir.dt.float32
BF16 = mybir.dt.bfloat16
AF = mybir.ActivationFunctionType
ALU = mybir.AluOpType

D = 4096
SEQ = 4096
DEPTH = 4
BW = 1024
MEM = 256
KC = D // 128
A_SHIFT = 3 * BW + 256
A_W = A_SHIFT + BW
O1 = A_W
O2 = O1 + 3 * BW
O3 = O2 + 4 * BW
O4 = O3 + 2 * BW
IN_W = O4 + 4 * D
TB = 1024
ENGS = ('pe', 'act', 'dve', 'pool', 'sp')


class Buf:
    __slots__ = ('w', 'r')

    def __init__(self):
        self.w = {}
        self.r = {}


class Tl:
    def __init__(self, t, nbuf=1):
        self.t = t
        self.bufs = [Buf() for _ in range(nbuf)]

    @property
    def buf(self):
        return self.bufs[0]


class Prog:
    EPOCH = 60000
    NSLOT = 8

    def __init__(self, nc, es):
        self.nc = nc
        self.es = es
        self.sems = []
        self.ops = {e: [] for e in ENGS}
        self.cnt = {e: 0 for e in ENGS}
        self.csem = {e: self._newsem() for e in ENGS}
        self.waited = {e: {} for e in ENGS}
        self.dma_n = {e: 0 for e in ENGS}
        self.dma_sems = {e: None for e in ENGS}
        self.last = {}
        self.n_ops = 0
        self.chain_sem = None
        self.chain_n = 0

    def _newsem(self):
        s = self.es.enter_context(self.nc.semaphore("sem%d" % len(self.sems)))
        self.sems.append(s)
        return len(self.sems) - 1

    def emit(self, eng, fn, reads=(), writes=(), dma=False, awrites=(), chain=False):
        deps = {}

        def add(s, v):
            if deps.get(s, 0) < v:
                deps[s] = v

        for b in reads:
            for s, v in b.w.items():
                add(s, v)
        for b in writes:
            for s, v in b.w.items():
                add(s, v)
            for s, v in b.r.items():
                add(s, v)
        for b in awrites:
            for s, v in b.r.items():
                add(s, v)
        if dma and chain:
            if self.chain_sem is None:
                self.chain_sem = self._newsem()
            if self.chain_n > 0:
                add(self.chain_sem, 16 * self.chain_n)
            self.chain_n += 1
            ev = (self.chain_sem, 16 * self.chain_n)
            inc = 16
        elif dma:
            if self.dma_sems[eng] is None:
                self.dma_sems[eng] = [self._newsem() for _ in range(self.NSLOT)]
            n = self.dma_n[eng]
            self.dma_n[eng] += 1
            slot, rnd = n % self.NSLOT, n // self.NSLOT
            sem = self.dma_sems[eng][slot]
            if rnd > 0:
                add(sem, 16 * rnd)
            ev = (sem, 16 * (rnd + 1))
            inc = 16
        else:
            if self.cnt[eng] >= self.EPOCH:
                self.csem[eng] = self._newsem()
                self.cnt[eng] = 0
            self.cnt[eng] += 1
            ev = (self.csem[eng], self.cnt[eng])
            inc = 1
        wd = self.waited[eng]
        waits = []
        for s, v in deps.items():
            if (not dma) and eng == 'pe' and s == ev[0]:
                continue
            if wd.get(s, 0) >= v:
                continue
            wd[s] = v
            waits.append((s, v))
        self.ops[eng].append((waits, fn, ev[0], inc))
        for b in reads:
            if b.r.get(ev[0], 0) < ev[1]:
                b.r[ev[0]] = ev[1]
        for b in writes:
            b.w = {ev[0]: ev[1]}
            b.r = {}
        for b in awrites:
            if b.w.get(ev[0], 0) < ev[1]:
                b.w[ev[0]] = ev[1]
        if self.last.get(ev[0], 0) < ev[1]:
            self.last[ev[0]] = ev[1]
        self.n_ops += 1
        return ev

    def barrier(self):
        for eng in ENGS:
            wd = self.waited[eng]
            waits = []
            for s, v in self.last.items():
                if wd.get(s, 0) >= v:
                    continue
                wd[s] = v
                waits.append((s, v))
            if waits:
                self.ops[eng].append((waits, None, None, 0))

    def _replay(self, eng, e):
        sems = self.sems
        for waits, fn, sem, inc in self.ops[eng]:
            for s, v in waits:
                e.wait_ge(sems[s], v)
            if fn is not None:
                ins = fn(e)
                ins.then_inc(sems[sem], inc)

    def build(self):
        with self.nc.Block() as block:
            @block.tensor
            def _(e):
                self._replay('pe', e)

            @block.scalar
            def _(e):
                self._replay('act', e)

            @block.vector
            def _(e):
                self._replay('dve', e)

            @block.gpsimd
            def _(e):
                self._replay('pool', e)

            @block.sync
            def _(e):
                self._replay('sp', e)


class Ctx:
    pass


def sb(C, es, name, shape, dt, nbuf=1):
    C.uid += 1
    return Tl(es.enter_context(C.nc.sbuf_tensor("%s_%d" % (name, C.uid), shape, dt)), nbuf)


def ps(C, es, name, shape, dt=F32):
    C.uid += 1
    return Tl(es.enter_context(C.nc.psum_tensor("%s_%d" % (name, C.uid), shape, dt)))


def dma(C, q, out, in_, reads=(), writes=(), awrites=()):
    C.P.emit(q, lambda e, out=out, in_=in_: e.dma_start(out=out, in_=in_), reads=reads, writes=writes, dma=True,
             awrites=awrites)


def load_w(C, slot, src, kc_n, cw):
    g = 0
    for q in range(0, kc_n, 8):
        n = min(8, kc_n - q)
        dma(C, C.wq, slot.t[:, q:q + n, 0:cw],
            src[q * 128:(q + n) * 128, :].rearrange("(kc p) n -> p kc n", p=128),
            writes=[slot.bufs[g]])
        g += 1


class WView:
    def __init__(self, groups):
        self.groups = groups

    def __getitem__(self, key):
        rs, cs = key
        for lo, hi, ap in self.groups:
            if lo <= cs.start and cs.stop <= hi:
                return ap[rs, cs.start - lo:cs.stop - lo]
        raise KeyError(key)


def load_xT(C, dst, src3, t0, tb, kc_n=KC):
    g = 0
    for q in range(0, kc_n, 8):
        n = min(8, kc_n - q)
        dma(C, 'sp', dst.t[:, q:q + n, 0:tb], src3[q:q + n, :, t0:t0 + tb].rearrange("kc p t -> p kc t"),
            writes=[dst.bufs[g]])
        g += 1


def phase_rmsnorm(C, x_src, gbc_src, xnT_dst, T, eps=1e-6):
    P = C.P
    with ExitStack() as es:
        gb = sb(C, es, "gb", [128, D], F32)
        xt = [sb(C, es, "xt", [128, D], F32) for _ in range(2)]
        xn = [sb(C, es, "xn", [128, D], BF16) for _ in range(2)]
        junk = sb(C, es, "junk", [128, D], BF16)
        st = [sb(C, es, "st", [128, 2], F32) for _ in range(2)]
        xs = [sb(C, es, "xs", [128, KC, 512], BF16) for _ in range(2)]
        pt = [ps(C, es, "pt", [128, 8, 128], BF16) for _ in range(4)]
        dma(C, 'sp', gb.t[:], gbc_src, writes=[gb.buf])
        ntt = T // 128
        for tt in range(ntt):
            s = tt % 2
            x_, n_, st_ = xt[s], xn[s], st[s]
            xsb = xs[(tt // 4) % 2]
            dma(C, 'sp', x_.t[:], x_src[tt * 128:(tt + 1) * 128, :], writes=[x_.buf])
            P.emit('act', lambda e, x_=x_, st_=st_: e.activation(out=junk.t[:], in_=x_.t[:], func=AF.Square,
                                                                   accum_out=st_.t[:, 0:1]),
                   reads=[x_.buf], writes=[junk.buf, st_.buf])
            P.emit('act', lambda e, st_=st_: e.activation(out=st_.t[:, 1:2], in_=st_.t[:, 0:1], func=AF.Sqrt,
                                                           bias=eps, scale=1.0 / D),
                   reads=[st_.buf], writes=[st_.buf])
            P.emit('dve', lambda e, st_=st_: e.reciprocal(out=st_.t[:, 1:2], in_=st_.t[:, 1:2]),
                   reads=[st_.buf], writes=[st_.buf])
            P.emit('dve', lambda e, x_=x_, n_=n_, st_=st_: e.scalar_tensor_tensor(
                out=n_.t[:], in0=x_.t[:], scalar=st_.t[:, 1:2], in1=gb.t[:], op0=ALU.mult, op1=ALU.mult),
                reads=[x_.buf, st_.buf, gb.buf], writes=[n_.buf])
            for q in range(4):
                p_ = pt[q]

                def tr(e, n_=n_, p_=p_, q=q):
                    for j in range(8):
                        kc = q * 8 + j
                        ins = e.transpose(out=p_.t[:, j, :], in_=n_.t[:, kc * 128:(kc + 1) * 128], identity=C.ident.t[:])
                    return ins
                P.emit('pe', tr, reads=[n_.buf, C.ident.buf], writes=[p_.buf])
                eng = 'act' if q % 2 == 0 else 'dve'
                if eng == 'act':
                    P.emit('act', lambda e, p_=p_, xsb=xsb, q=q, tt=tt: e.activation(
                        out=xsb.t[:, q * 8:(q + 1) * 8, (tt % 4) * 128:(tt % 4 + 1) * 128], in_=p_.t[:], func=AF.Copy),
                        reads=[p_.buf], writes=[xsb.buf])
                else:
                    P.emit('dve', lambda e, p_=p_, xsb=xsb, q=q, tt=tt: e.tensor_copy(
                        out=xsb.t[:, q * 8:(q + 1) * 8, (tt % 4) * 128:(tt % 4 + 1) * 128], in_=p_.t[:]),
                        reads=[p_.buf], writes=[xsb.buf])
            if tt % 4 == 3:
                t0 = (tt // 4) * 512
                for q in range(0, KC, 8):
                    dma(C, 'sp', xnT_dst[q:q + 8, :, t0:t0 + 512].rearrange("kc p t -> p kc t"),
                        xsb.t[:, q:q + 8, :], reads=[xsb.buf], awrites=[C.dbuf('xnT', t0 // TB)])
    P.barrier()


def gemm_fm(C, wslot, kc_n, xT, ct, ntg, pss, extra_reads=()):
    def mm(e):
        ins = None
        for kc in range(kc_n):
            for tg in range(ntg):
                ins = e.matmul(pss[tg].t[:], wslot.t[:, kc, ct * 128:(ct + 1) * 128],
                               xT.t[:, kc, tg * 512:(tg + 1) * 512], start=(kc == 0), stop=(kc == kc_n - 1))
        return ins
    C.P.emit('pe', mm, reads=list(wslot.bufs) + list(xT.bufs) + list(extra_reads), writes=[p.buf for p in pss[:ntg]])


def gemm_tm(C, wslot, kc_n, xT, tt, cw, pst):
    def mm(e):
        ins = None
        for kc in range(kc_n):
            ins = e.matmul(pst.t[:, 0:cw], xT.t[:, kc, tt * 128:(tt + 1) * 128], wslot.t[:, kc, 0:cw],
                           start=(kc == 0), stop=(kc == kc_n - 1))
        return ins
    C.P.emit('pe', mm, reads=list(wslot.bufs) + list(xT.bufs), writes=[pst.buf])


class WStream:
    def __init__(self, C, es, kc_max, cw_max):
        self.C = C
        self.slots = [sb(C, es, "wsl", [128, kc_max, cw_max], BF16, nbuf=(kc_max + 7) // 8) for _ in range(2)]
        self.jobs = []

    def run(self, jobs):
        C = self.C
        n = len(jobs)
        if n == 0:
            return
        load_w(C, self.slots[0], jobs[0][0], jobs[0][1], jobs[0][2])
        for i in range(n):
            if i + 1 < n:
                load_w(C, self.slots[(i + 1) % 2], jobs[i + 1][0], jobs[i + 1][1], jobs[i + 1][2])
            jobs[i][3](self.slots[i % 2])


def evac_store_fm(C, stage, pst, dst_ap, dbuf, eng):
    if eng == 'act':
        C.P.emit('act', lambda e: e.activation(out=stage.t[:], in_=pst.t[:], func=AF.Copy),
                 reads=[pst.buf], writes=[stage.buf])
    else:
        C.P.emit('dve', lambda e: e.tensor_copy(out=stage.t[:], in_=pst.t[:]), reads=[pst.buf], writes=[stage.buf])
    dma(C, 'sp', dst_ap, stage.t[:], reads=[stage.buf], awrites=[dbuf])


def phase_B(C, l, T):
    P = C.P
    w = C.w_in[l]
    with ExitStack() as es:
        xT = sb(C, es, "xT", [128, KC, TB], BF16, nbuf=4)
        ws = WStream(C, es, KC, 512)
        vg = sb(C, es, "vg", [128, 8, BW], BF16, nbuf=8)
        lng = sb(C, es, "lng", [128, BW], F32)
        lnb = sb(C, es, "lnb", [128, BW], F32)
        wsT = sb(C, es, "wsT", [128, 8, 128], BF16)
        bsb = sb(C, es, "bsb", [128, 8, 128], F32)
        sv = sb(C, es, "sv", [128, 8, TB], BF16, nbuf=8)
        st6 = [sb(C, es, "st6", [128, 12], F32) for _ in range(2)]
        mv = [sb(C, es, "mv", [128, 4], F32) for _ in range(2)]
        vtmp = [sb(C, es, "vtmp", [128, BW], F32) for _ in range(2)]
        vln = [sb(C, es, "vln", [128, BW], BF16) for _ in range(2)]
        tmp = [sb(C, es, "tmpb", [128, 512], BF16) for _ in range(4)]
        ystage = [sb(C, es, "ystage", [128, 512], BF16) for _ in range(4)]
        pm = [ps(C, es, "pm", [128, 512]) for _ in range(4)]
        psv = [ps(C, es, "psv", [128, 512]) for _ in range(2)]
        dma(C, 'sp', lng.t[:], C.prm['lng_bc'][l], writes=[lng.buf])
        dma(C, 'sp', lnb.t[:], C.prm['lnb_bc'][l], writes=[lnb.buf])
        dma(C, 'sp', bsb.t[:], C.prm['bs_bc'][l], writes=[bsb.buf])
        dma(C, 'pool', wsT.t[:], C.prm['wsT'][l], writes=[wsT.buf])
        cnt = [0]
        jobs = []
        for tb in range(T // TB):
            t0 = tb * TB

            def job_v(slot, j, tb=tb, t0=t0):
                if j == 0:
                    load_xT(C, xT, C.xnT, t0, TB)
                for tt in range(8):
                    p_ = pm[cnt[0] % 4]
                    cnt[0] += 1
                    gemm_tm(C, slot, KC, xT, tt, 512, p_)
                    P.emit('act', lambda e, p_=p_, tt=tt: e.activation(out=vg.t[:, tt, j * 512:(j + 1) * 512], in_=p_.t[:],
                                                                      func=AF.Gelu_apprx_tanh),
                           reads=[p_.buf], writes=[vg.bufs[tt]])
                if j == 1:
                    for tt in range(8):
                        s = tt % 2
                        P.emit('dve', lambda e, s=s, tt=tt: e.bn_stats(out=st6[s].t[:, 0:6], in_=vg.t[:, tt, 0:512]),
                               reads=[vg.bufs[tt]], writes=[st6[s].buf])
                        P.emit('dve', lambda e, s=s, tt=tt: e.bn_stats(out=st6[s].t[:, 6:12], in_=vg.t[:, tt, 512:1024]),
                               reads=[vg.bufs[tt], st6[s].buf], writes=[st6[s].buf])
                        P.emit('dve', lambda e, s=s: e.bn_aggr(out=mv[s].t[:, 0:2], in_=st6[s].t[:]),
                               reads=[st6[s].buf], writes=[mv[s].buf])
                        P.emit('act', lambda e, s=s: e.activation(out=mv[s].t[:, 2:3], in_=mv[s].t[:, 1:2], func=AF.Sqrt,
                                                                    bias=1e-5, scale=1.0),
                               reads=[mv[s].buf], writes=[mv[s].buf])
                        P.emit('dve', lambda e, s=s: e.reciprocal(out=mv[s].t[:, 2:3], in_=mv[s].t[:, 2:3]),
                               reads=[mv[s].buf], writes=[mv[s].buf])
                        P.emit('dve', lambda e, s=s, tt=tt: e.tensor_scalar(
                            out=vtmp[s].t[:], in0=vg.t[:, tt, :], scalar1=mv[s].t[:, 0:1], scalar2=mv[s].t[:, 2:3],
                            op0=ALU.subtract, op1=ALU.mult),
                            reads=[vg.bufs[tt], mv[s].buf], writes=[vtmp[s].buf])
                        P.emit('pool', lambda e, s=s: e.tensor_tensor(out=vtmp[s].t[:], in0=vtmp[s].t[:], in1=lng.t[:], op=ALU.mult),
                               reads=[vtmp[s].buf, lng.buf], writes=[vtmp[s].buf])
                        P.emit('pool', lambda e, s=s: e.tensor_tensor(out=vln[s].t[:], in0=vtmp[s].t[:], in1=lnb.t[:], op=ALU.add),
                               reads=[vtmp[s].buf, lnb.buf], writes=[vln[s].buf])

                        def spm(e, s=s):
                            ins = None
                            for g in range(8):
                                ins = e.matmul(psv[g // 4].t[:, (g % 4) * 128:(g % 4 + 1) * 128],
                                               vln[s].t[:, g * 128:(g + 1) * 128], wsT.t[:, g, :], start=True, stop=True)
                            return ins
                        P.emit('pe', spm, reads=[vln[s].buf, wsT.buf], writes=[psv[0].buf, psv[1].buf])
                        for hh in range(2):
                            P.emit('dve', lambda e, hh=hh, tt=tt: e.tensor_tensor(
                                out=sv.t[:, hh * 4:(hh + 1) * 4, tt * 128:(tt + 1) * 128],
                                in0=psv[hh].t[:].rearrange("p (g q) -> p g q", g=4),
                                in1=bsb.t[:, hh * 4:(hh + 1) * 4, :], op=ALU.add),
                                reads=[psv[hh].buf, bsb.buf], writes=[sv.bufs[g_] for g_ in range(hh * 4, hh * 4 + 4)])

            def job_ug(slot, j, kind, tb=tb, t0=t0):
                for ct in range(4):
                    g = j * 4 + ct
                    pp = [pm[(cnt[0] % 2) * 2], pm[(cnt[0] % 2) * 2 + 1]]
                    cnt[0] += 1
                    gemm_fm(C, slot, KC, xT, ct, 2, pp)
                    for tg in range(2):
                        t_ = tmp[(cnt[0] * 2 + tg) % 4]
                        func = AF.Gelu_apprx_tanh if kind == 'u' else AF.Silu
                        P.emit('act', lambda e, t_=t_, p_=pp[tg], func=func: e.activation(out=t_.t[:], in_=p_.t[:], func=func),
                               reads=[pp[tg].buf], writes=[t_.buf])
                        if kind == 'u':
                            P.emit('dve', lambda e, t_=t_, g=g, tg=tg: e.tensor_tensor(
                                out=sv.t[:, g, tg * 512:(tg + 1) * 512], in0=t_.t[:], in1=sv.t[:, g, tg * 512:(tg + 1) * 512],
                                op=ALU.mult), reads=[t_.buf, sv.bufs[g]], writes=[sv.bufs[g]])
                        else:
                            y_ = ystage[(cnt[0] * 2 + tg) % 4]
                            P.emit('dve', lambda e, t_=t_, y_=y_, g=g, tg=tg: e.tensor_tensor(
                                out=y_.t[:], in0=t_.t[:], in1=sv.t[:, g, tg * 512:(tg + 1) * 512], op=ALU.mult),
                                reads=[t_.buf, sv.bufs[g]], writes=[y_.buf])
                            dma(C, 'sp', C.yT[1][g, :, t0 + tg * 512:t0 + (tg + 1) * 512], y_.t[:],
                                reads=[y_.buf], awrites=[C.dbuf('yT1', tb)])

            for j in range(2):
                jobs.append((w[:, O1 + BW + j * 512:O1 + BW + (j + 1) * 512], KC, 512, lambda s, j=j, f=job_v: f(s, j)))
            for j in range(2):
                jobs.append((w[:, O1 + j * 512:O1 + (j + 1) * 512], KC, 512, lambda s, j=j, f=job_ug: f(s, j, 'u')))
            for j in range(2):
                jobs.append((w[:, O1 + 2 * BW + j * 512:O1 + 2 * BW + (j + 1) * 512], KC, 512,
                             lambda s, j=j, f=job_ug: f(s, j, 'g')))
        ws.run(jobs)
    P.barrier()


def headnorm_fm(C, pq, q_sb, sq, pss, rs, gain_cols, out_tile, out_idx, nfeat, eps, tokn, ones=None):
    P = C.P
    ndc = len(pq)
    ones = ones or C.ones
    for dc in range(ndc):
        P.emit('act', lambda e, dc=dc: e.activation(out=q_sb.t[:, dc, 0:tokn], in_=pq[dc].t[:, 0:tokn], func=AF.Copy),
               reads=[pq[dc].buf], writes=[q_sb.bufs[dc]])
        P.emit('act', lambda e, dc=dc: e.activation(out=sq.t[:, dc, 0:tokn], in_=pq[dc].t[:, 0:tokn], func=AF.Square),
               reads=[pq[dc].buf], writes=[sq.bufs[dc]])

    def mm(e):
        ins = None
        for dc in range(ndc):
            ins = e.matmul(pss.t[:, 0:tokn], ones.t[:], sq.t[:, dc, 0:tokn], start=(dc == 0), stop=(dc == ndc - 1))
        return ins
    P.emit('pe', mm, reads=[ones.buf] + [sq.bufs[dc] for dc in range(ndc)], writes=[pss.buf])
    P.emit('act', lambda e: e.activation(out=rs.t[:, 0:tokn], in_=pss.t[:, 0:tokn], func=AF.Sqrt, bias=eps, scale=1.0 / nfeat),
           reads=[pss.buf], writes=[rs.buf])
    P.emit('dve', lambda e: e.reciprocal(out=rs.t[:, 0:tokn], in_=rs.t[:, 0:tokn]), reads=[rs.buf], writes=[rs.buf])
    for dc in range(ndc):
        P.emit('dve', lambda e, dc=dc: e.scalar_tensor_tensor(
            out=out_tile.t[:, out_idx[dc], 0:tokn], in0=q_sb.t[:, dc, 0:tokn], scalar=gain_cols[dc], in1=rs.t[:, 0:tokn],
            op0=ALU.mult, op1=ALU.mult),
            reads=[q_sb.bufs[dc], rs.buf, C.cols.buf], writes=[out_tile.bufs[out_idx[dc]]])


def phase_D(C, l, T):
    P = C.P
    w = C.w_in[l]
    wkv = C.w_kv[l]
    with ExitStack() as es:
        ws = WStream(C, es, KC, 512)
        kT = sb(C, es, "kT", [128, 8, MEM], BF16, nbuf=8)
        vmem = sb(C, es, "vmem", [128, 2, BW], BF16, nbuf=2)
        q_sb = sb(C, es, "q_sb", [128, 2, 512], F32, nbuf=2)
        sq = sb(C, es, "sq", [128, 2, 512], BF16, nbuf=2)
        rs = sb(C, es, "rs", [128, 512], F32)
        qn = sb(C, es, "qn", [128, 2, 512], BF16, nbuf=2)
        E = sb(C, es, "E", [128, 2, 512], BF16, nbuf=2)
        rd = sb(C, es, "rd", [128, 512], F32)
        tmp = [sb(C, es, "tmpd", [128, 512], BF16) for _ in range(2)]
        ystage = [sb(C, es, "ystd", [128, 512], BF16) for _ in range(4)]
        pm = [ps(C, es, "pm", [128, 512]) for _ in range(4)]
        pso = [ps(C, es, "pso", [128, 512]) for _ in range(2)]
        paux = ps(C, es, "paux", [128, 512])
        with ExitStack() as es2:
            mT = sb(C, es2, "mT", [128, KC, MEM], BF16, nbuf=1)
            gb = sb(C, es2, "mgb", [128, D], F32)
            xt = sb(C, es2, "mxt", [128, D], F32)
            xn = sb(C, es2, "mxn", [128, D], BF16)
            junk = sb(C, es2, "mjunk", [128, D], BF16)
            st = sb(C, es2, "mst", [128, 2], F32)
            pt = ps(C, es2, "mpt", [128, 8, 128], BF16)
            dma(C, 'sp', gb.t[:], C.prm['mg_bc'][l], writes=[gb.buf])
            for m in range(2):
                dma(C, 'sp', xt.t[:], C.mem[m * 128:(m + 1) * 128, :], writes=[xt.buf])
                P.emit('act', lambda e: e.activation(out=junk.t[:], in_=xt.t[:], func=AF.Square, accum_out=st.t[:, 0:1]),
                       reads=[xt.buf], writes=[junk.buf, st.buf])
                P.emit('act', lambda e: e.activation(out=st.t[:, 1:2], in_=st.t[:, 0:1], func=AF.Sqrt, bias=1e-6, scale=1.0 / D),
                       reads=[st.buf], writes=[st.buf])
                P.emit('dve', lambda e: e.reciprocal(out=st.t[:, 1:2], in_=st.t[:, 1:2]), reads=[st.buf], writes=[st.buf])
                P.emit('dve', lambda e: e.scalar_tensor_tensor(out=xn.t[:], in0=xt.t[:], scalar=st.t[:, 1:2], in1=gb.t[:],
                                                               op0=ALU.mult, op1=ALU.mult),
                       reads=[xt.buf, st.buf, gb.buf], writes=[xn.buf])
                for q in range(4):
                    def tr(e, q=q):
                        ins = None
                        for j in range(8):
                            kc = q * 8 + j
                            ins = e.transpose(out=pt.t[:, j, :], in_=xn.t[:, kc * 128:(kc + 1) * 128], identity=C.ident.t[:])
                        return ins
                    P.emit('pe', tr, reads=[xn.buf, C.ident.buf], writes=[pt.buf])
                    P.emit('dve', lambda e, q=q, m=m: e.tensor_copy(out=mT.t[:, q * 8:(q + 1) * 8, m * 128:(m + 1) * 128], in_=pt.t[:]),
                           reads=[pt.buf], writes=[mT.buf])
            jobs = []

            def job_k(slot, j):
                for h2 in range(2):
                    h = j * 2 + h2
                    for dc in range(2):
                        ct = h2 * 2 + dc

                        def mm(e, ct=ct, dc=dc):
                            ins = None
                            for kc in range(KC):
                                ins = e.matmul(pm[dc].t[:, 0:MEM], slot.t[:, kc, ct * 128:(ct + 1) * 128], mT.t[:, kc, :],
                                               start=(kc == 0), stop=(kc == KC - 1))
                            return ins
                        P.emit('pe', mm, reads=list(slot.bufs) + [mT.buf], writes=[pm[dc].buf])
                    headnorm_fm(C, [pm[0], pm[1]], q_sb, sq, paux, rs,
                                [C.col('mk', l, 0), C.col('mk', l, 1)], kT, [h * 2, h * 2 + 1], 256, 1e-6, MEM)

            def job_v(slot, j):
                for m in range(2):
                    gemm_tm(C, slot, KC, mT, m, 512, pm[2 + m])
                    P.emit('act', lambda e, m=m: e.activation(out=vmem.t[:, m, j * 512:(j + 1) * 512], in_=pm[2 + m].t[:], func=AF.Copy),
                           reads=[pm[2 + m].buf], writes=[vmem.bufs[m]])
            for j in range(2):
                jobs.append((wkv[:, j * 512:(j + 1) * 512], KC, 512, lambda s, j=j: job_k(s, j)))
            for j in range(2):
                jobs.append((wkv[:, BW + j * 512:BW + (j + 1) * 512], KC, 512, lambda s, j=j: job_v(s, j)))
            ws.run(jobs)
        P.barrier()
        xT = sb(C, es, "xT", [128, KC, TB], BF16, nbuf=4)
        oD = sb(C, es, "oD", [128, 8, TB], BF16, nbuf=8)
        jobs = []
        cnt = [0]
        for tb in range(T // TB):
            t0 = tb * TB

            def job_q(slot, j, tb=tb, t0=t0):
                if j == 0:
                    load_xT(C, xT, C.xnT, t0, TB)
                for h2 in range(2):
                    h = j * 2 + h2
                    for dc in range(2):
                        gemm_fm(C, slot, KC, xT, h2 * 2 + dc, 2, [pm[dc * 2], pm[dc * 2 + 1]])
                    for tg in range(2):
                        headnorm_fm(C, [pm[tg], pm[2 + tg]], q_sb, sq, paux, rs,
                                    [C.col('mq', l, 0), C.col('mq', l, 1)], qn, [0, 1], 256, 1e-6, 512)
                        for m in range(2):
                            def mm(e, m=m, h=h):
                                ins = None
                                for dc in range(2):
                                    ins = e.matmul(pso[m].t[:], kT.t[:, h * 2 + dc, m * 128:(m + 1) * 128], qn.t[:, dc, :],
                                                   start=(dc == 0), stop=(dc == 1))
                                return ins
                            P.emit('pe', mm, reads=[kT.bufs[h * 2], kT.bufs[h * 2 + 1], qn.bufs[0], qn.bufs[1]], writes=[pso[m].buf])
                            P.emit('act', lambda e, m=m: e.activation(out=E.t[:, m, :], in_=pso[m].t[:], func=AF.Exp, scale=1.0 / 16.0),
                                   reads=[pso[m].buf], writes=[E.bufs[m]])

                        def mmd(e):
                            ins = None
                            for m in range(2):
                                ins = e.matmul(paux.t[:], C.ones.t[:], E.t[:, m, :], start=(m == 0), stop=(m == 1))
                            return ins
                        P.emit('pe', mmd, reads=[C.ones.buf, E.bufs[0], E.bufs[1]], writes=[paux.buf])
                        P.emit('dve', lambda e: e.reciprocal(out=rd.t[:], in_=paux.t[:]), reads=[paux.buf], writes=[rd.buf])
                        for dc in range(2):
                            def mmo(e, dc=dc, h=h):
                                ins = None
                                for m in range(2):
                                    c0 = h * 256 + dc * 128
                                    ins = e.matmul(pso[dc].t[:], vmem.t[:, m, c0:c0 + 128], E.t[:, m, :], start=(m == 0), stop=(m == 1))
                                return ins
                            P.emit('pe', mmo, reads=[vmem.bufs[0], vmem.bufs[1], E.bufs[0], E.bufs[1]], writes=[pso[dc].buf])
                            P.emit('dve', lambda e, dc=dc, h=h, tg=tg: e.tensor_tensor(
                                out=oD.t[:, h * 2 + dc, tg * 512:(tg + 1) * 512], in0=pso[dc].t[:], in1=rd.t[:], op=ALU.mult),
                                reads=[pso[dc].buf, rd.buf], writes=[oD.bufs[h * 2 + dc]])

            def job_g(slot, j, tb=tb, t0=t0):
                for ct in range(4):
                    g = j * 4 + ct
                    pp = [pm[(cnt[0] % 2) * 2], pm[(cnt[0] % 2) * 2 + 1]]
                    cnt[0] += 1
                    gemm_fm(C, slot, KC, xT, ct, 2, pp)
                    for tg in range(2):
                        t_ = tmp[tg]
                        y_ = ystage[(cnt[0] * 2 + tg) % 4]
                        P.emit('act', lambda e, t_=t_, p_=pp[tg]: e.activation(out=t_.t[:], in_=p_.t[:], func=AF.Silu),
                               reads=[pp[tg].buf], writes=[t_.buf])
                        P.emit('dve', lambda e, t_=t_, y_=y_, g=g, tg=tg: e.tensor_tensor(
                            out=y_.t[:], in0=t_.t[:], in1=oD.t[:, g, tg * 512:(tg + 1) * 512], op=ALU.mult),
                            reads=[t_.buf, oD.bufs[g]], writes=[y_.buf])
                        dma(C, 'sp', C.yT[3][g, :, t0 + tg * 512:t0 + (tg + 1) * 512], y_.t[:],
                            reads=[y_.buf], awrites=[C.dbuf('yT3', tb)])
            for j in range(2):
                jobs.append((w[:, O3 + j * 512:O3 + (j + 1) * 512], KC, 512, lambda s, j=j, f=job_q: f(s, j)))
            for j in range(2):
                jobs.append((w[:, O3 + BW + j * 512:O3 + BW + (j + 1) * 512], KC, 512, lambda s, j=j, f=job_g: f(s, j)))
        ws.run(jobs)
    P.barrier()


CA = {'conv': 0, 'convl': 144, 'w0': 156, 'a0': 188, 'kk': 220, 'ka': 236, 'rk': 252, 'lnxw': 268, 'lnxb': 284}
NCA = 300
CH = 64
NCG = 8


def phase_A(C, l, T):
    P = C.P
    w = C.w_in[l]
    NTG = T // 512
    NCHK = T // CH

    def E(eng, fn, r=(), w=(), aw=()):
        P.emit(eng, fn, reads=r, writes=w, awrites=aw)

    with ExitStack() as es:
        xT = sb(C, es, "xT", [128, KC, TB], BF16, nbuf=4)
        ws = WStream(C, es, KC, 512)
        stage = [sb(C, es, "astage", [128, 512], F32) for _ in range(4)]
        pm = [ps(C, es, "pma", [128, 512]) for _ in range(4)]
        cnt = [0]
        jobs = []
        for tb in range(T // TB):
            t0 = tb * TB

            def job(slot, c0, cw, tb=tb, t0=t0):
                if c0 == 0:
                    load_xT(C, xT, C.xnT, t0, TB)
                for ct in range(cw // 128):
                    tile = c0 // 128 + ct
                    pp = [pm[(cnt[0] % 2) * 2], pm[(cnt[0] % 2) * 2 + 1]]
                    cnt[0] += 1
                    gemm_fm(C, slot, KC, xT, ct, 2, pp)
                    for tg in range(2):
                        k = (cnt[0] * 2 + tg) % 4
                        func = AF.Silu if tile >= 26 else AF.Copy
                        E('act', lambda e, k=k, p_=pp[tg], func=func: e.activation(out=stage[k].t[:], in_=p_.t[:], func=func),
                          r=[pp[tg].buf], w=[stage[k].buf])
                        dma(C, 'sp', C.hA[tile, :, t0 + tg * 512:t0 + (tg + 1) * 512], stage[k].t[:], reads=[stage[k].buf],
                            awrites=[C.dbuf('hA', tb)])
            for c0, cw in [(0, 512), (512, 512), (1024, 512), (1536, 512), (2048, 512), (2560, 512), (3072, 256), (3328, 512), (3840, 512)]:
                jobs.append((w[:, c0:c0 + cw], KC, cw, lambda s, c0=c0, cw=cw, f=job: f(s, c0, cw)))
        ws.run(jobs)
    P.barrier()

    with ExitStack() as es:
        F = lambda name, shape=(64, 512), dt=F32: sb(C, es, name, list(shape), dt)
        ca = F("ca", (64, NCA))
        mskS = F("mskS")
        m2 = [F("m2f", (64, NCG, 128)), F("m2b", (64, NCG, 128))]
        I8 = F("I8", (64, NCG, 64))
        id64 = F("id64", (64, 64), BF16)
        on64 = F("on64", (64, 64), BF16)
        wup = F("wup", (64, 2, BW), BF16)
        aup = F("aup", (64, 2, BW), BF16)
        cst = C.prm['cstA']
        dma(C, 'sp', ca.t[:], C.prm['colsA'][l], writes=[ca.buf])
        dma(C, 'sp', mskS.t[:], cst[:, 0:512], writes=[mskS.buf])
        dma(C, 'sp', m2[0].t[:], cst[:, 512:1536].rearrange("p (c f) -> p c f", c=NCG), writes=[m2[0].buf])
        dma(C, 'sp', m2[1].t[:], cst[:, 1536:2560].rearrange("p (c f) -> p c f", c=NCG), writes=[m2[1].buf])
        dma(C, 'sp', I8.t[:], cst[:, 2560:3072].rearrange("p (c f) -> p c f", c=NCG), writes=[I8.buf])
        dma(C, 'pool', id64.t[:], cst[:, 2560:2624], writes=[id64.buf])
        dma(C, 'pool', on64.t[:], cst[:, 3072:3136], writes=[on64.buf])
        dma(C, 'pool', wup.t[:], C.a_w_up[l].rearrange("z r c -> r z c"), writes=[wup.buf])
        dma(C, 'pool', aup.t[:], C.a_a_up[l].rearrange("z r c -> r z c"), writes=[aup.buf])
        col = lambda name, i=0: ca.t[:, CA[name] + i:CA[name] + i + 1]
        raw = [F("raw%d" % i, (64, 514)) for i in range(4)]
        tw = [[F("tw%d%d" % (i, z), (64, 512), BF16) for z in range(2)] for i in range(2)]
        r_, k_, v_ = F("r_"), F("k_"), F("v_")
        kk_, t1, t2, t3 = F("kk_"), F("t1"), F("t2"), F("t3")
        a_, kt_ = [F("a0_"), F("a1_")], [F("kt0"), F("kt1")]
        lw_, Pc, Qc, Tt, CLb = F("lw_"), F("Pc"), F("Qc"), F("Tt"), F("CLb")
        E1, E1x, Em, Er = F("E1"), F("E1x"), F("Em"), F("Er")
        WC = F("WC", (64, NCG))
        b_ = F("b_")
        vb = F("vb", (64, 512), BF16)
        sqk = F("sqk", (64, 512), BF16)
        ZR = F("ZR", (64, NCG, 128), BF16)
        Bt, Kt, Bh, Kh = [F(n, (64, NCG, 64), BF16) for n in ("Bt", "Kt", "Bh", "Kh")]
        SP = F("SP", (64, NCG, 128), BF16)
        PT = F("PT", (64, NCG, 64), BF16)
        Abr = F("Abr", (64, NCG, 64), BF16)
        MK = F("MK", (64, NCG, 128), BF16)
        VT, ZT, BhT, KhT, X0T, U0T, ZpT = [F(n, (64, NCG, 64), BF16) for n in ("VT", "ZT", "BhT", "KhT", "X0T", "U0T", "ZpT")]
        GT = F("GT", (64, 2, NCHK, 64), BF16)
        Hh = F("Hh", (64, 2, NCHK, 64), BF16)
        Rp = F("Rp", (64, 2, NCHK, 64), BF16)
        Y0 = F("Y0", (64, 2, NCHK, 64), BF16)
        Sall = F("Sall", (64, 2, NCHK, 64), BF16)
        rkk = F("rkk", (64, T), BF16)
        vfull = F("vfull", (64, T), BF16)
        sgfull = F("sgfull", (64, T), BF16)
        yst = [F("yst%d" % i, (64, 512), BF16) for i in range(2)]
        pA = ps(C, es, "pA", [64, NCG, 128])
        pB = ps(C, es, "pB", [64, NCG, 128])
        pC = ps(C, es, "pC", [64, NCG, 64])
        pD = ps(C, es, "pD", [64, NCG, 64])
        pTr = ps(C, es, "pTr", [64, NCG, 64], BF16)
        pTr2 = ps(C, es, "pTr2", [64, NCG, 64], BF16)

        def v3(t, lo=0, hi=None):
            return t.t[:].rearrange("p (c f) -> p c f", f=CH)

        def conv(dst, src, base):
            E('dve', lambda e: e.tensor_scalar(out=dst.t[:], in0=src.t[:, 0:512], scalar1=col(*base(0)), scalar2=None, op0=ALU.mult),
              r=[src.buf, ca.buf], w=[dst.buf])
            E('dve', lambda e: e.scalar_tensor_tensor(out=dst.t[:], in0=src.t[:, 1:513], scalar=col(*base(1)), in1=dst.t[:],
                                                      op0=ALU.mult, op1=ALU.add), r=[src.buf, ca.buf, dst.buf], w=[dst.buf])
            E('dve', lambda e: e.scalar_tensor_tensor(out=dst.t[:], in0=src.t[:, 2:514], scalar=col(*base(2)), in1=dst.t[:],
                                                      op0=ALU.mult, op1=ALU.add), r=[src.buf, ca.buf, dst.buf], w=[dst.buf])

        def load_halo(dst, tile, row0, c0):
            lo, hi = max(c0 - 1, 0), min(c0 + 513, T)
            if c0 == 0:
                E('pool', lambda e: e.memset(dst.t[:, 0:1], 0.0), w=[dst.buf])
            if c0 + 512 == T:
                E('pool', lambda e: e.memset(dst.t[:, 513:514], 0.0), w=[dst.buf])
            dma(C, 'sp', dst.t[:, lo - (c0 - 1):hi - (c0 - 1)], C.hA[tile, row0:row0 + 64, lo:hi], writes=[dst.buf])

        def chunk_mm(pt, osl, lhs_fn, rhs_fn, reads, n2=1, lhs2=None, rhs2=None):
            def mm(e):
                ins = None
                for c in range(NCG):
                    ins = e.matmul(pt.t[:, c, osl], lhs_fn(c), rhs_fn(c), start=True, stop=(lhs2 is None))
                    if lhs2 is not None:
                        ins = e.matmul(pt.t[:, c, osl], lhs2(c), rhs2(c), start=False, stop=True)
                return ins
            E('pe', mm, r=reads, w=[pt.buf])

        def chunk_tr(pt, src3, reads):
            def tr(e):
                ins = None
                for c in range(NCG):
                    ins = e.transpose(out=pt.t[:, c, :], in_=src3(c), identity=id64.t[:])
                return ins
            E('pe', tr, r=list(reads) + [id64.buf], w=[pt.buf])

        A0 = slice(0, 64)
        A1 = slice(64, 128)
        def do_head(h):
            hp, hr = h // 2, (h % 2) * 64
            def do_cg(cg):
                c0 = cg * 512
                cb = cg * NCG
                for _ in range(getattr(C, 'bg_per', 1)):
                    if C.bg:
                        C.bg.pop(0)()
                if True:
                    for gi, (tile, row0) in enumerate([(24, 0), (24, 64), (25, 0), (25, 64)]):
                        load_halo(raw[gi], tile, row0, c0)
                        conv(t1, raw[gi], lambda tap, gi=gi: ('convl', gi * 3 + tap))
                        kind, z = gi // 2, gi % 2
                        if kind == 0:
                            E('act', lambda e, z=z: e.activation(out=tw[0][z].t[:], in_=t1.t[:], func=AF.Tanh), r=[t1.buf], w=[tw[0][z].buf])
                        else:
                            E('act', lambda e, z=z: e.activation(out=tw[1][z].t[:], in_=t1.t[:], func=AF.Copy), r=[t1.buf], w=[tw[1][z].buf])
                for wi, dst in enumerate((r_, k_, v_)):
                    load_halo(raw[wi], wi * 8 + hp, hr, c0)
                    conv(dst, raw[wi], lambda tap, wi=wi: ('conv', (wi * 16 + h) * 3 + tap))
                E('dve', lambda e: e.tensor_scalar(out=t1.t[:], in0=k_.t[:], scalar1=col('kk', h), scalar2=None, op0=ALU.mult),
                  r=[k_.buf, ca.buf], w=[t1.buf])
                E('act', lambda e: e.activation(out=sqk.t[:], in_=t1.t[:], func=AF.Square), r=[t1.buf], w=[sqk.buf])
                E('pe', lambda e: e.matmul(pC.t[:].rearrange("p c f -> p (c f)"), on64.t[:], sqk.t[:], start=True, stop=True),
                  r=[on64.buf, sqk.buf], w=[pC.buf])
                E('act', lambda e: e.activation(out=t2.t[:], in_=pC.t[:].rearrange("p c f -> p (c f)"), func=AF.Sqrt), r=[pC.buf], w=[t2.buf])
                E('dve', lambda e: e.tensor_scalar(out=t2.t[:], in0=t2.t[:], scalar1=1e-12, scalar2=None, op0=ALU.max), r=[t2.buf], w=[t2.buf])
                E('dve', lambda e: e.reciprocal(out=t2.t[:], in_=t2.t[:]), r=[t2.buf], w=[t2.buf])
                E('dve', lambda e: e.tensor_tensor(out=kk_.t[:], in0=t1.t[:], in1=t2.t[:], op=ALU.mult), r=[t1.buf, t2.buf], w=[kk_.buf])
                E('act', lambda e: e.activation(out=vb.t[:], in_=v_.t[:], func=AF.Copy), r=[v_.buf], w=[vb.buf])
                E('pool', lambda e, c0=c0: e.tensor_copy(out=vfull.t[:, c0:c0 + 512], in_=v_.t[:]), r=[v_.buf], aw=[vfull.buf])
                chunk_tr(pTr, lambda c: vb.t[:, c * CH:(c + 1) * CH], [vb.buf])
                E('dve', lambda e: e.tensor_copy(out=VT.t[:], in_=pTr.t[:]), r=[pTr.buf], w=[VT.buf])
                def do_z(z):
                    E('pe', lambda e, z=z: e.matmul(pC.t[:].rearrange("p c f -> p (c f)"), wup.t[:, z, h * 64:(h + 1) * 64], tw[0][z].t[:], start=True, stop=True),
                      r=[wup.buf, tw[0][z].buf], w=[pC.buf])
                    E('act', lambda e, z=z: e.activation(out=lw_.t[:], in_=pC.t[:].rearrange("p c f -> p (c f)"), func=AF.Sigmoid,
                                                         bias=col('w0', z * 16 + h), scale=1.0), r=[pC.buf, ca.buf], w=[lw_.buf])
                    E('pool', lambda e: e.tensor_scalar(out=lw_.t[:], in0=lw_.t[:], scalar1=-0.6065306597126334, scalar2=None, op0=ALU.mult),
                      r=[lw_.buf], w=[lw_.buf])
                    E('pe', lambda e, z=z: e.matmul(pD.t[:].rearrange("p c f -> p (c f)"), aup.t[:, z, h * 64:(h + 1) * 64], tw[1][z].t[:], start=True, stop=True),
                      r=[aup.buf, tw[1][z].buf], w=[pD.buf])
                    E('act', lambda e, z=z: e.activation(out=a_[z].t[:], in_=pD.t[:].rearrange("p c f -> p (c f)"), func=AF.Sigmoid,
                                                         bias=col('a0', z * 16 + h), scale=1.0), r=[pD.buf, ca.buf], w=[a_[z].buf])
                    E('dve', lambda e, z=z: e.tensor_scalar(out=t3.t[:], in0=a_[z].t[:], scalar1=-1.0, scalar2=col('ka', h), op0=ALU.add, op1=ALU.mult),
                      r=[a_[z].buf, ca.buf], w=[t3.buf])
                    E('dve', lambda e, z=z: e.scalar_tensor_tensor(out=kt_[z].t[:], in0=t3.t[:], scalar=1.0, in1=k_.t[:], op0=ALU.add, op1=ALU.mult),
                      r=[t3.buf, k_.buf], w=[kt_[z].buf])
                    E('pool', lambda e, z=z: e.tensor_tensor(out=b_.t[:], in0=kk_.t[:], in1=a_[z].t[:], op=ALU.mult), r=[kk_.buf, a_[z].buf], w=[b_.buf])
                    E('dve', lambda e: e.tensor_tensor_scan(out=Pc.t[:], data0=mskS.t[:], data1=lw_.t[:], initial=0.0, op0=ALU.mult, op1=ALU.add),
                      r=[mskS.buf, lw_.buf], w=[Pc.buf])
                    E('pool', lambda e: e.tensor_tensor(out=Qc.t[:], in0=Pc.t[:], in1=lw_.t[:], op=ALU.subtract), r=[Pc.buf, lw_.buf], w=[Qc.buf])
                    if C.dbg.get('a_bcast', True):
                        E('dve', lambda e: e.tensor_tensor(out=v3(Tt), in0=v3(Pc)[:, :, CH - 1:CH].to_broadcast([64, NCG, CH]), in1=v3(Pc), op=ALU.subtract),
                          r=[Pc.buf], w=[Tt.buf])
                    else:
                        for c in range(NCG):
                            E('dve', lambda e, c=c: e.tensor_scalar(out=Tt.t[:, c * CH:(c + 1) * CH], in0=Pc.t[:, c * CH:(c + 1) * CH], scalar1=-1.0,
                                                                    scalar2=Pc.t[:, c * CH + CH - 1:c * CH + CH], op0=ALU.mult, op1=ALU.add),
                              r=[Pc.buf], aw=[Tt.buf])
                    E('act', lambda e: e.activation(out=WC.t[:], in_=v3(Pc)[:, :, CH - 1], func=AF.Exp), r=[Pc.buf], w=[WC.buf])
                    if z == 0:
                        cl, clx, cr = Pc, Qc, Tt
                    else:
                        E('pool', lambda e: e.tensor_tensor(out=CLb.t[:], in0=Tt.t[:], in1=lw_.t[:], op=ALU.add), r=[Tt.buf, lw_.buf], w=[CLb.buf])
                        cl, clx, cr = CLb, Tt, Qc
                    E('act', lambda e, cl=cl: e.activation(out=E1.t[:], in_=cl.t[:], func=AF.Exp), r=[cl.buf], w=[E1.buf])
                    E('act', lambda e, clx=clx: e.activation(out=E1x.t[:], in_=clx.t[:], func=AF.Exp), r=[clx.buf], w=[E1x.buf])
                    E('act', lambda e, cl=cl: e.activation(out=Em.t[:], in_=cl.t[:], func=AF.Exp, scale=-1.0), r=[cl.buf], w=[Em.buf])
                    E('act', lambda e, cr=cr: e.activation(out=Er.t[:], in_=cr.t[:], func=AF.Exp), r=[cr.buf], w=[Er.buf])
                    E('dve', lambda e: e.scalar_tensor_tensor(out=ZR.t[:, :, A0], in0=v3(kk_), scalar=-1.0, in1=v3(E1x), op0=ALU.mult, op1=ALU.mult),
                      r=[kk_.buf, E1x.buf], aw=[ZR.buf])
                    E('pool', lambda e: e.tensor_tensor(out=ZR.t[:, :, A1], in0=v3(r_), in1=v3(E1), op=ALU.mult), r=[r_.buf, E1.buf], aw=[ZR.buf])
                    E('dve', lambda e: e.tensor_tensor(out=Bt.t[:], in0=v3(b_), in1=v3(Em), op=ALU.mult), r=[b_.buf, Em.buf], w=[Bt.buf])
                    E('pool', lambda e, z=z: e.tensor_tensor(out=Kt.t[:], in0=v3(kt_[z]), in1=v3(Em), op=ALU.mult), r=[kt_[z].buf, Em.buf], w=[Kt.buf])
                    E('dve', lambda e: e.tensor_tensor(out=Bh.t[:], in0=v3(b_), in1=v3(Er), op=ALU.mult), r=[b_.buf, Er.buf], w=[Bh.buf])
                    E('pool', lambda e, z=z: e.tensor_tensor(out=Kh.t[:], in0=v3(kt_[z]), in1=v3(Er), op=ALU.mult), r=[kt_[z].buf, Er.buf], w=[Kh.buf])
                    mz = m2[z]
                    chunk_mm(pA, slice(0, 128), lambda c: Bt.t[:, c, :], lambda c: ZR.t[:, c, :], [Bt.buf, ZR.buf])
                    E('dve', lambda e, mz=mz: e.tensor_tensor(out=SP.t[:, :, A1], in0=pA.t[:, :, A0], in1=mz.t[:, :, A0], op=ALU.mult),
                      r=[pA.buf, mz.buf], aw=[SP.buf])
                    E('dve', lambda e, mz=mz: e.tensor_tensor(out=Abr.t[:], in0=pA.t[:, :, A1], in1=mz.t[:, :, A1], op=ALU.mult),
                      r=[pA.buf, mz.buf], w=[Abr.buf])
                    E('pool', lambda e: e.tensor_tensor(out=SP.t[:, :, A0], in0=SP.t[:, :, A1], in1=I8.t[:], op=ALU.add), r=[SP.buf, I8.buf], aw=[SP.buf])
                    chunk_mm(pB, slice(0, 128), lambda c: Kt.t[:, c, :], lambda c: ZR.t[:, c, :], [Kt.buf, ZR.buf])
                    E('dve', lambda e, mz=mz: e.tensor_tensor(out=MK.t[:], in0=pB.t[:], in1=mz.t[:], op=ALU.mult), r=[pB.buf, mz.buf], w=[MK.buf])
                    chunk_mm(pC, slice(0, 64), lambda c: ZR.t[:, c, A0], lambda c: Bt.t[:, c, :], [Bt.buf, ZR.buf])
                    mT_ = m2[1 - z]
                    E('dve', lambda e, mT_=mT_: e.tensor_tensor(out=PT.t[:], in0=pC.t[:], in1=mT_.t[:, :, A0], op=ALU.mult), r=[pC.buf, mT_.buf], w=[PT.buf])
                    for kstep in range(5):
                        chunk_mm(pA, slice(0, 64), lambda c: PT.t[:, c, :], lambda c: SP.t[:, c, A1], [PT.buf, SP.buf])
                        chunk_mm(pD, slice(0, 64), lambda c: SP.t[:, c, A1], lambda c: PT.t[:, c, :], [PT.buf, SP.buf])
                        E('act', lambda e: e.activation(out=SP.t[:, :, A1], in_=pA.t[:, :, A0], func=AF.Copy), r=[pA.buf], w=[SP.buf])
                        E('dve', lambda e: e.tensor_copy(out=PT.t[:], in_=pD.t[:]), r=[pD.buf], w=[PT.buf])
                        chunk_mm(pC, slice(0, 64), lambda c: PT.t[:, c, :], lambda c: SP.t[:, c, A0], [PT.buf, SP.buf])
                        E('dve', lambda e: e.tensor_tensor(out=SP.t[:, :, A0], in0=pC.t[:], in1=SP.t[:, :, A0], op=ALU.add), r=[pC.buf, SP.buf], w=[SP.buf])
                    Tm = lambda c: SP.t[:, c, A0]
                    chunk_tr(pTr, lambda c: ZR.t[:, c, A0], [ZR.buf])
                    E('dve', lambda e: e.tensor_copy(out=ZT.t[:], in_=pTr.t[:]), r=[pTr.buf], w=[ZT.buf])
                    chunk_tr(pTr2, lambda c: Bh.t[:, c, :], [Bh.buf])
                    E('act', lambda e: e.activation(out=BhT.t[:], in_=pTr2.t[:], func=AF.Copy), r=[pTr2.buf], w=[BhT.buf])
                    chunk_tr(pTr, lambda c: Kh.t[:, c, :], [Kh.buf])
                    E('dve', lambda e: e.tensor_copy(out=KhT.t[:], in_=pTr.t[:]), r=[pTr.buf], w=[KhT.buf])
                    chunk_mm(pD, slice(0, 64), lambda c: MK.t[:, c, A0], lambda c: VT.t[:, c, :], [MK.buf, VT.buf])
                    E('act', lambda e: e.activation(out=X0T.t[:], in_=pD.t[:], func=AF.Copy), r=[pD.buf], w=[X0T.buf])
                    chunk_mm(pC, slice(0, 64), Tm, lambda c: X0T.t[:, c, :], [SP.buf, X0T.buf])
                    E('dve', lambda e: e.tensor_copy(out=U0T.t[:], in_=pC.t[:]), r=[pC.buf], w=[U0T.buf])
                    chunk_mm(pD, slice(0, 64), Tm, lambda c: ZT.t[:, c, :], [SP.buf, ZT.buf])
                    E('act', lambda e: e.activation(out=ZpT.t[:], in_=pD.t[:], func=AF.Copy), r=[pD.buf], w=[ZpT.buf])
                    chunk_mm(pC, slice(0, 64), lambda c: ZpT.t[:, c, :], lambda c: BhT.t[:, c, :], [ZpT.buf, BhT.buf])
                    if C.dbg.get('a_bcast', True):
                        E('pool', lambda e: e.tensor_tensor(out=v3(t3), in0=I8.t[:], in1=WC.t[:, :].unsqueeze(2).to_broadcast([64, NCG, CH]), op=ALU.mult),
                          r=[I8.buf, WC.buf], w=[t3.buf])
                        E('dve', lambda e, z=z, cb=cb: e.tensor_tensor(out=GT.t[:, z, cb:cb + NCG, :], in0=pC.t[:], in1=v3(t3), op=ALU.add),
                          r=[pC.buf, t3.buf], aw=[GT.buf])
                    else:
                        for c in range(NCG):
                            E('dve', lambda e, c=c, z=z, cb=cb: e.scalar_tensor_tensor(out=GT.t[:, z, cb + c, :], in0=I8.t[:, 0, :], scalar=WC.t[:, c:c + 1],
                                                                                     in1=pC.t[:, c, :], op0=ALU.mult, op1=ALU.add),
                              r=[I8.buf, WC.buf, pC.buf], aw=[GT.buf])
                    chunk_mm(pD, slice(0, 64), lambda c: BhT.t[:, c, :], lambda c: U0T.t[:, c, :], [BhT.buf, U0T.buf, KhT.buf, VT.buf],
                             lhs2=lambda c: KhT.t[:, c, :], rhs2=lambda c: VT.t[:, c, :])
                    E('act', lambda e, z=z, cb=cb: e.activation(out=Hh.t[:, z, cb:cb + NCG, :], in_=pD.t[:], func=AF.Copy), r=[pD.buf], aw=[Hh.buf])
                    chunk_mm(pC, slice(0, 64), lambda c: ZpT.t[:, c, :], lambda c: Abr.t[:, c, :], [ZpT.buf, Abr.buf])
                    E('dve', lambda e, z=z, cb=cb: e.tensor_tensor(out=Rp.t[:, z, cb:cb + NCG, :], in0=pC.t[:], in1=ZR.t[:, :, A1], op=ALU.add),
                      r=[pC.buf, ZR.buf], aw=[Rp.buf])
                    chunk_mm(pD, slice(0, 64), lambda c: U0T.t[:, c, :], lambda c: Abr.t[:, c, :], [U0T.buf, Abr.buf, VT.buf, MK.buf],
                             lhs2=lambda c: VT.t[:, c, :], rhs2=lambda c: MK.t[:, c, A1])
                    E('act', lambda e, z=z, cb=cb: e.activation(out=Y0.t[:, z, cb:cb + NCG, :], in_=pD.t[:], func=AF.Copy), r=[pD.buf], aw=[Y0.buf])
                for z in range(2):
                    do_z(z)
                E('pool', lambda e: e.tensor_tensor(out=t3.t[:], in0=kt_[0].t[:], in1=kt_[1].t[:], op=ALU.add), r=[kt_[0].buf, kt_[1].buf], w=[t3.buf])
                E('dve', lambda e, c0=c0: e.scalar_tensor_tensor(out=rkk.t[:, c0:c0 + 512], in0=r_.t[:], scalar=col('rk', h), in1=t3.t[:], op0=ALU.mult, op1=ALU.mult),
                  r=[r_.buf, ca.buf, t3.buf], aw=[rkk.buf])
                dma(C, 'pool', sgfull.t[:, c0:c0 + 512], C.hA[26 + hp, hr:hr + 64, c0:c0 + 512], awrites=[sgfull.buf])
            for cg in range(NTG):
                do_cg(cg)
            _scan_chain(C, E, GT, Hh, Sall, NCHK, pA, pB)
            def do_out(cg):
                cb = cg * NCG
                c0 = cg * 512
                for z in range(2):
                    pz = pC if z == 0 else pD
                    chunk_mm(pz, slice(0, 64), lambda c, z=z, cb=cb: Sall.t[:, z, cb + c, :], lambda c, z=z, cb=cb: Rp.t[:, z, cb + c, :], [Sall.buf, Rp.buf])
                E('dve', lambda e, cb=cb: e.tensor_tensor(out=v3(t1), in0=pC.t[:], in1=Y0.t[:, 0, cb:cb + NCG, :], op=ALU.add), r=[pC.buf, Y0.buf], w=[t1.buf])
                E('dve', lambda e, cb=cb: e.tensor_tensor(out=v3(t2), in0=pD.t[:], in1=Y0.t[:, 1, cb:cb + NCG, :], op=ALU.add), r=[pD.buf, Y0.buf], w=[t2.buf])
                E('pool', lambda e: e.tensor_tensor(out=t1.t[:], in0=t1.t[:], in1=t2.t[:], op=ALU.add), r=[t1.buf, t2.buf], w=[t1.buf])
                E('act', lambda e: e.activation(out=vb.t[:], in_=t1.t[:], func=AF.Copy), r=[t1.buf], w=[vb.buf])
                E('act', lambda e: e.activation(out=sqk.t[:], in_=t1.t[:], func=AF.Square), r=[t1.buf], w=[sqk.buf])
                E('pe', lambda e: e.matmul(pA.t[:, 0:4, :].rearrange("p c f -> p (c f)"), on64.t[:], vb.t[:], start=True, stop=True), r=[on64.buf, vb.buf], w=[pA.buf])
                E('pe', lambda e: e.matmul(pB.t[:, 0:4, :].rearrange("p c f -> p (c f)"), on64.t[:], sqk.t[:], start=True, stop=True), r=[on64.buf, sqk.buf], w=[pB.buf])
                mean_ps = lambda: pA.t[:, 0:4, :].rearrange("p c f -> p (c f)")
                sq_ps = lambda: pB.t[:, 0:4, :].rearrange("p c f -> p (c f)")
                E('act', lambda e: e.activation(out=t2.t[:], in_=mean_ps(), func=AF.Copy, scale=1.0 / 64), r=[pA.buf], w=[t2.buf])
                E('dve', lambda e: e.tensor_tensor(out=t3.t[:], in0=t2.t[:], in1=t2.t[:], op=ALU.mult), r=[t2.buf], w=[t3.buf])
                E('dve', lambda e: e.scalar_tensor_tensor(out=t3.t[:], in0=sq_ps(), scalar=1.0 / 64, in1=t3.t[:], op0=ALU.mult, op1=ALU.subtract),
                  r=[pB.buf, t3.buf], w=[t3.buf])
                E('act', lambda e: e.activation(out=t3.t[:], in_=t3.t[:], func=AF.Sqrt, bias=64e-5, scale=1.0), r=[t3.buf], w=[t3.buf])
                E('dve', lambda e: e.reciprocal(out=t3.t[:], in_=t3.t[:]), r=[t3.buf], w=[t3.buf])
                E('pool', lambda e: e.tensor_tensor(out=t1.t[:], in0=t1.t[:], in1=t2.t[:], op=ALU.subtract), r=[t1.buf, t2.buf], w=[t1.buf])
                E('dve', lambda e: e.tensor_tensor(out=t1.t[:], in0=t1.t[:], in1=t3.t[:], op=ALU.mult), r=[t1.buf, t3.buf], w=[t1.buf])
                E('dve', lambda e, h=h: e.tensor_scalar(out=t1.t[:], in0=t1.t[:], scalar1=col('lnxw', h), scalar2=col('lnxb', h), op0=ALU.mult, op1=ALU.add),
                  r=[t1.buf, ca.buf], w=[t1.buf])
                E('pe', lambda e, c0=c0: e.matmul(pA.t[:, 4:8, :].rearrange("p c f -> p (c f)"), on64.t[:], rkk.t[:, c0:c0 + 512], start=True, stop=True),
                  r=[on64.buf, rkk.buf], w=[pA.buf])
                E('dve', lambda e, c0=c0: e.tensor_tensor(out=t2.t[:], in0=pA.t[:, 4:8, :].rearrange("p c f -> p (c f)"), in1=vfull.t[:, c0:c0 + 512], op=ALU.mult),
                  r=[pA.buf, vfull.buf], w=[t2.buf])
                E('pool', lambda e: e.tensor_tensor(out=t1.t[:], in0=t1.t[:], in1=t2.t[:], op=ALU.add), r=[t1.buf, t2.buf], w=[t1.buf])
                y_ = yst[cg % 2]
                E('pool', lambda e, c0=c0, y_=y_: e.tensor_tensor(out=y_.t[:], in0=t1.t[:], in1=sgfull.t[:, c0:c0 + 512], op=ALU.mult),
                  r=[t1.buf, sgfull.buf], w=[y_.buf])
                dma(C, 'sp', C.yT[0][hp, hr:hr + 64, c0:c0 + 512], y_.t[:], reads=[y_.buf], awrites=[C.dbuf('yT0', 0)])
            for cg in range(NTG):
                do_out(cg)
        for h in range(16):
            do_head(h)
        while C.bg:
            C.bg.pop(0)()
    P.barrier()


def _scan_chain(C, E, GT, Hh, Sall, NCHK, pA, pB):
    ztile = [Buf(), Buf()]
    for z in range(2):
        first = 0 if z == 0 else NCHK - 1
        E('pool', lambda e, z=z, first=first: e.memset(Sall.t[:, z, first, :], 0.0), w=[ztile[z]], aw=[Sall.buf])
    for i in range(NCHK - 1):
        for z in range(2):
            c = i if z == 0 else NCHK - 1 - i
            nxt = c + 1 if z == 0 else c - 1
            pz = pA if z == 0 else pB
            E('pe', lambda e, z=z, c=c, pz=pz: e.matmul(pz.t[:, 0, 0:64], GT.t[:, z, c, :], Sall.t[:, z, c, :], start=True, stop=True),
              r=[GT.buf, ztile[z]], w=[pz.buf])
            E('dve', lambda e, z=z, c=c, nxt=nxt, pz=pz: e.tensor_tensor(out=Sall.t[:, z, nxt, :], in0=pz.t[:, 0, 0:64], in1=Hh.t[:, z, c, :], op=ALU.add),
              r=[pz.buf, Hh.buf], w=[ztile[z]], aw=[Sall.buf])


def phase_C(C, l, T):
    P = C.P
    w = C.w_in[l]
    R = T // 64
    with ExitStack() as es:
        xT = sb(C, es, "xT", [128, KC, TB], BF16, nbuf=4)
        ws = WStream(C, es, KC, 512)
        q_sb = sb(C, es, "cq_sb", [128, 1, 512], F32)
        sq = sb(C, es, "csq", [128, 1, 512], BF16)
        rs = sb(C, es, "crs", [128, 512], F32)
        qst = [sb(C, es, "cqst", [128, 1, 512], BF16) for _ in range(2)]
        stage = [sb(C, es, "cstage", [128, 512], BF16) for _ in range(4)]
        pm = [ps(C, es, "pmc", [128, 512]) for _ in range(4)]
        paux = ps(C, es, "pauxc", [128, 512])
        cnt = [0]
        jobs = []
        for tb in range(T // TB):
            t0 = tb * TB

            def job_qk(slot, j, which, tb=tb, t0=t0):
                if which == 'q' and j == 0:
                    load_xT(C, xT, C.xnT, t0, TB)
                dst = C.qT if which == 'q' else C.kT
                gcol = C.col('cq' if which == 'q' else 'ck', l)
                for ct in range(4):
                    pp = [pm[(cnt[0] % 2) * 2], pm[(cnt[0] % 2) * 2 + 1]]
                    cnt[0] += 1
                    gemm_fm(C, slot, KC, xT, ct, 2, pp)
                    for tg in range(2):
                        o_ = qst[tg]
                        headnorm_fm(C, [pp[tg]], q_sb, sq, paux, rs, [gcol], o_, [0], 64, 1e-6, 512, ones=C.bones)
                        dma(C, 'sp', dst[j * 4 + ct, :, t0 + tg * 512:t0 + (tg + 1) * 512], o_.t[:, 0, :], reads=[o_.buf],
                            awrites=[C.dbuf('cqk', tb)])

            def job_v(slot, j, tb=tb, t0=t0):
                for tt in range(8):
                    k = cnt[0] % 4
                    cnt[0] += 1
                    gemm_tm(C, slot, KC, xT, tt, 512, pm[k])
                    P.emit('act', lambda e, k=k: e.activation(out=stage[k].t[:], in_=pm[k].t[:], func=AF.Copy),
                           reads=[pm[k].buf], writes=[stage[k].buf])
                    r0 = t0 + tt * 128
                    dma(C, 'sp', C.vC[r0:r0 + 128, j * 512:(j + 1) * 512], stage[k].t[:], reads=[stage[k].buf],
                        awrites=[C.dbuf('cv', tb)])

            def job_g(slot, j, tb=tb, t0=t0):
                for ct in range(4):
                    pp = [pm[(cnt[0] % 2) * 2], pm[(cnt[0] % 2) * 2 + 1]]
                    cnt[0] += 1
                    gemm_fm(C, slot, KC, xT, ct, 2, pp)
                    for tg in range(2):
                        k = (cnt[0] * 2 + tg) % 4
                        P.emit('act', lambda e, k=k, p_=pp[tg]: e.activation(out=stage[k].t[:], in_=p_.t[:], func=AF.Silu),
                               reads=[pp[tg].buf], writes=[stage[k].buf])
                        dma(C, 'sp', C.sgC[j * 4 + ct, :, t0 + tg * 512:t0 + (tg + 1) * 512], stage[k].t[:], reads=[stage[k].buf],
                            awrites=[C.dbuf('cg', tb)])
            for j in range(2):
                jobs.append((w[:, O2 + j * 512:O2 + (j + 1) * 512], KC, 512, lambda s, j=j, f=job_qk: f(s, j, 'q')))
            for j in range(2):
                jobs.append((w[:, O2 + BW + j * 512:O2 + BW + (j + 1) * 512], KC, 512, lambda s, j=j, f=job_qk: f(s, j, 'k')))
            for j in range(2):
                jobs.append((w[:, O2 + 2 * BW + j * 512:O2 + 2 * BW + (j + 1) * 512], KC, 512, lambda s, j=j, f=job_v: f(s, j)))
            for j in range(2):
                jobs.append((w[:, O2 + 3 * BW + j * 512:O2 + 3 * BW + (j + 1) * 512], KC, 512, lambda s, j=j, f=job_g: f(s, j)))
        ws.run(jobs)
    P.barrier()
    NT = T // 128
    if C.dbg.get('c_noattn'):
        return
    with ExitStack() as es:
        bufs = []
        for i in range(2):
            bufs.append(dict(
                q0=sb(C, es, "aq0", [128, T], BF16), q1=sb(C, es, "aq1", [128, T], BF16),
                k=sb(C, es, "ak", [128, T], BF16), g=sb(C, es, "ag", [128, T], BF16),
                ve=sb(C, es, "ave", [128, NT, 128], BF16), vo=sb(C, es, "avo", [128, NT, 128], BF16),
                tb=sb(C, es, "atb", [128, 2, 15, 64], F32), y=sb(C, es, "ay", [128, T], BF16)))
        sc = [sb(C, es, "asc", [128, 512], F32) for _ in range(4)]
        E = [sb(C, es, "aE", [128, 2, 4, 64], BF16) for _ in range(4)]
        rden = [sb(C, es, "arden", [128, 64], F32) for _ in range(4)]
        o1 = [sb(C, es, "ao1", [128, 64], F32) for _ in range(4)]
        pS = [ps(C, es, "apS", [128, 2, 4, 64]) for _ in range(4)]
        pOD = [ps(C, es, "apOD", [128, 4, 64]) for _ in range(4)]
        for b in bufs:
            P.emit('pool', lambda e, b=b: e.memset(b['q0'].t[64:128, :], 0.0), writes=[b['q0'].buf])
            P.emit('pool', lambda e, b=b: e.memset(b['q1'].t[0:64, :], 0.0), writes=[b['q1'].buf])
        for ct in range(8):
            b = bufs[ct % 2]
            dma(C, 'sp', b['q0'].t[0:64, :], C.qT[ct, 0:64, :], writes=[b['q0'].buf])
            dma(C, 'sp', b['q1'].t[64:128, :], C.qT[ct, 64:128, :], writes=[b['q1'].buf])
            dma(C, 'sp', b['k'].t[:], C.kT[ct], writes=[b['k'].buf])
            dma(C, 'sp', b['g'].t[:], C.sgC[ct], writes=[b['g'].buf])
            dma(C, 'sp', b['ve'].t[:], C.vC[:, ct * 128:(ct + 1) * 128].rearrange("(n p) c -> p n c", p=128), writes=[b['ve'].buf])
            dma(C, 'sp', b['vo'].t[:, 0:NT - 1, :],
                C.vC[64:T - 64, ct * 128:(ct + 1) * 128].rearrange("(n p) c -> p n c", p=128), writes=[b['vo'].buf])
            dma(C, 'sp', b['tb'].t[:], C.prm['rpbT'][l, ct], writes=[b['tb'].buf])
            for i in range(R):
                k = i % 4
                si = min(max(i - 4, 0), R - 8)
                rel0 = si - i + 7

                def mms(e, b=b, i=i, si=si, k=k):
                    ins = None
                    for h2 in range(2):
                        for kt in range(4):
                            tk = (si + 2 * kt) * 64
                            ins = e.matmul(pS[k].t[:, h2, kt, :], b['k'].t[:, tk:tk + 128],
                                           b['q%d' % h2].t[:, i * 64:(i + 1) * 64], start=True, stop=True)
                    return ins
                lvl = C.dbg.get('c_lvl', 9)
                if lvl < 2:
                    continue
                P.emit('pe', mms, reads=[b['k'].buf, b['q0'].buf, b['q1'].buf], writes=[pS[k].buf])
                if lvl == 21:
                    continue
                for h2 in range(2):
                    P.emit('dve', lambda e, b=b, k=k, rel0=rel0, h2=h2: e.scalar_tensor_tensor(
                        out=sc[k].t[:, h2 * 256:(h2 + 1) * 256].rearrange("p (t q) -> p t q", t=4), in0=pS[k].t[:, h2, :, :], scalar=0.125,
                        in1=b['tb'].t[:, h2, rel0:rel0 + 8:2, :], op0=ALU.mult, op1=ALU.add),
                        reads=[pS[k].buf, b['tb'].buf], awrites=[sc[k].buf])
                if lvl == 22:
                    continue
                P.emit('act', lambda e, k=k: e.activation(out=E[k].t[:].rearrange("p h t q -> p (h t q)"), in_=sc[k].t[:], func=AF.Exp),
                       reads=[sc[k].buf], writes=[E[k].buf])

                if lvl < 3:
                    continue

                def mmo(e, b=b, si=si, k=k):
                    ins = None
                    for h2 in range(2):
                        for kt in range(4):
                            row = si + 2 * kt
                            vt = b['ve'].t[:, row // 2, :] if row % 2 == 0 else b['vo'].t[:, row // 2, :]
                            ins = e.matmul(pOD[k].t[:, h2, :], vt, E[k].t[:, h2, kt, :], start=(kt == 0), stop=(kt == 3))
                    for h2 in range(2):
                        for kt in range(4):
                            ins = e.matmul(pOD[k].t[:, 2 + h2, :], C.ones.t[:], E[k].t[:, h2, kt, :], start=(kt == 0), stop=(kt == 3))
                    return ins
                P.emit('pe', mmo, reads=[b['ve'].buf, b['vo'].buf, E[k].buf, C.ones.buf], writes=[pOD[k].buf])
                for h2 in range(2):
                    sl = slice(h2 * 64, (h2 + 1) * 64)
                    P.emit('dve', lambda e, k=k, sl=sl, h2=h2: e.reciprocal(out=rden[k].t[sl, :], in_=pOD[k].t[sl, 2 + h2, :]),
                           reads=[pOD[k].buf], awrites=[rden[k].buf])
                    P.emit('dve', lambda e, k=k, sl=sl, h2=h2: e.tensor_tensor(out=o1[k].t[sl, :], in0=pOD[k].t[sl, h2, :], in1=rden[k].t[sl, :], op=ALU.mult),
                           reads=[pOD[k].buf, rden[k].buf], awrites=[o1[k].buf])
                P.emit('pool', lambda e, k=k, b=b, i=i: e.tensor_tensor(out=b['y'].t[:, i * 64:(i + 1) * 64], in0=o1[k].t[:],
                                                                          in1=b['g'].t[:, i * 64:(i + 1) * 64], op=ALU.mult),
                       reads=[o1[k].buf, b['g'].buf], awrites=[b['y'].buf])
            dma(C, 'sp', C.yT[2][ct], b['y'].t[:], reads=[b['y'].buf], awrites=[C.dbuf('yT2', 0)])
    P.barrier()


def phase_merge(C, l, T):
    P = C.P
    w = C.w_in[l]
    CW = 256
    with ExitStack() as es:
        xT = sb(C, es, "xT", [128, KC, TB], BF16, nbuf=4)
        yTs = [sb(C, es, "yTs", [128, 8, TB], BF16, nbuf=1) for _ in range(2)]
        wg = WStream(C, es, KC, CW)
        wb = WStream(C, es, 8, CW)
        acc = [[sb(C, es, "acc", [128, 512], F32) for _ in range(2)] for _ in range(2)]
        sig = [sb(C, es, "sig", [128, 512], F32) for _ in range(2)]
        prod = [sb(C, es, "prod", [128, 512], F32) for _ in range(2)]
        ostage = [sb(C, es, "ostage", [128, 512], BF16) for _ in range(4)]
        psg = [ps(C, es, "psg", [128, 512]) for _ in range(4)]
        psp = [ps(C, es, "psp", [128, 512]) for _ in range(4)]
        cnt = [0]
        for tb in range(T // TB):
            t0 = tb * TB
            load_xT(C, xT, C.xnT, t0, TB)
            units = [(cb, n) for cb in range(D // CW) for n in range(4)]
            load_w(C, wg.slots[0], w[:, O4 + units[0][1] * D + units[0][0] * CW:O4 + units[0][1] * D + (units[0][0] + 1) * CW], KC, CW)
            load_w(C, wb.slots[0], C.w_br[l][units[0][1]][:, units[0][0] * CW:(units[0][0] + 1) * CW], 8, CW)
            for ui, (cb, n) in enumerate(units):
                if ui + 1 < len(units):
                    cb2, n2 = units[ui + 1]
                    load_w(C, wg.slots[(ui + 1) % 2], w[:, O4 + n2 * D + cb2 * CW:O4 + n2 * D + (cb2 + 1) * CW], KC, CW)
                    load_w(C, wb.slots[(ui + 1) % 2], C.w_br[l][n2][:, cb2 * CW:(cb2 + 1) * CW], 8, CW)
                gs, bs = wg.slots[ui % 2], wb.slots[ui % 2]
                ys = yTs[ui % 2]
                dma(C, 'sp', ys.t[:, :, :], C.yT[n][:, :, t0:t0 + TB].rearrange("kc p t -> p kc t"), writes=[ys.buf])
                for ct in range(CW // 128):
                    k = cnt[0] % 2
                    cnt[0] += 1
                    pg = [psg[k * 2], psg[k * 2 + 1]]
                    pp = [psp[k * 2], psp[k * 2 + 1]]
                    gemm_fm(C, gs, KC, xT, ct, 2, pg)
                    gemm_fm(C, bs, 8, ys, ct, 2, pp)
                    for tg in range(2):
                        a_ = acc[ct][tg]
                        P.emit('act', lambda e, tg=tg, pg=pg: e.activation(out=sig[tg].t[:], in_=pg[tg].t[:], func=AF.Sigmoid),
                               reads=[pg[tg].buf], writes=[sig[tg].buf])
                        if n == 0:
                            P.emit('dve', lambda e, tg=tg, pp=pp, a_=a_: e.tensor_tensor(out=a_.t[:], in0=sig[tg].t[:], in1=pp[tg].t[:], op=ALU.mult),
                                   reads=[sig[tg].buf, pp[tg].buf], writes=[a_.buf])
                        else:
                            P.emit('dve', lambda e, tg=tg, pp=pp: e.tensor_tensor(out=prod[tg].t[:], in0=sig[tg].t[:], in1=pp[tg].t[:], op=ALU.mult),
                                   reads=[sig[tg].buf, pp[tg].buf], writes=[prod[tg].buf])
                            if n < 3:
                                P.emit('pool', lambda e, tg=tg, a_=a_: e.tensor_tensor(out=a_.t[:], in0=a_.t[:], in1=prod[tg].t[:], op=ALU.add),
                                       reads=[a_.buf, prod[tg].buf], writes=[a_.buf])
                            else:
                                o_ = ostage[(cnt[0] * 2 + tg) % 4]
                                P.emit('pool', lambda e, tg=tg, a_=a_, o_=o_: e.tensor_tensor(out=o_.t[:], in0=a_.t[:], in1=prod[tg].t[:], op=ALU.add),
                                       reads=[a_.buf, prod[tg].buf], writes=[o_.buf])
                                kc = cb * (CW // 128) + ct
                                dma(C, 'sp', C.mT[kc, :, t0 + tg * 512:t0 + (tg + 1) * 512], o_.t[:], reads=[o_.buf],
                                    awrites=[C.dbuf('mT', tb)])
    P.barrier()


def phase_out(C, l, x_src, x_dst, T):
    P = C.P
    with ExitStack() as es:
        mTs = sb(C, es, "mTs", [128, KC, TB], BF16, nbuf=4)
        ws = WStream(C, es, KC, 512)
        xin = [sb(C, es, "xin", [128, 512], F32) for _ in range(4)]
        xo = [sb(C, es, "xo", [128, 512], F32) for _ in range(4)]
        pm = [ps(C, es, "pmo", [128, 512]) for _ in range(4)]
        cnt = [0]
        jobs = []
        for tb in range(T // TB):
            t0 = tb * TB

            def job(slot, cb, tb=tb, t0=t0):
                if cb == 0:
                    load_xT(C, mTs, C.mT, t0, TB)
                for tt in range(TB // 128):
                    k = cnt[0] % 4
                    cnt[0] += 1
                    r0 = t0 + tt * 128
                    dma(C, 'sp', xin[k].t[:], x_src[r0:r0 + 128, cb * 512:(cb + 1) * 512], writes=[xin[k].buf])
                    gemm_tm(C, slot, KC, mTs, tt, 512, pm[k])
                    P.emit('dve', lambda e, k=k: e.tensor_tensor(out=xo[k].t[:], in0=pm[k].t[:], in1=xin[k].t[:], op=ALU.add),
                           reads=[pm[k].buf, xin[k].buf], writes=[xo[k].buf])
                    dma(C, 'sp', x_dst[r0:r0 + 128, cb * 512:(cb + 1) * 512], xo[k].t[:], reads=[xo[k].buf],
                        awrites=[C.dbuf('xout', tb)])
            for cb in range(D // 512):
                jobs.append((C.w_out[l][:, cb * 512:(cb + 1) * 512], KC, 512, lambda s, cb=cb, f=job: f(s, cb)))
        ws.run(jobs)
    P.barrier()


WIN_GROUPS = [(0, O1), (O1, O2), (O2, O3), (O3, O4)] + [(O4 + n * D, O4 + (n + 1) * D) for n in range(4)]
COLS = {'mq': 0, 'mk': 2, 'cq': 4, 'ck': 5, 'conv': 6, 'w0': 84, 'a0': 100, 'kk': 116, 'ka': 124, 'rk': 132,
        'lnxw': 140, 'lnxb': 148}
NCOLS = 160
NCST = 3 * 128


def build_program(T=SEQ, n_layers=DEPTH, dbg=None, NB=1):
    dbg = dbg or {}
    L = n_layers
    nc = bass.Bass("TRN2", target_bir_lowering=False)
    C = Ctx()
    C.nc = nc
    C.uid = 0
    C.T = T
    C.dbg = dbg

    def din(name, shape, dt=F32):
        return nc.dram_tensor(name, list(shape), dt, kind="ExternalInput").ap()

    def dscr(name, shape, dt):
        return nc.dram_tensor(name, list(shape), dt, kind="Internal").ap()

    x_all = din("x", [NB, T, D])
    mem_all = din("mem", [NB, MEM, D])
    nsh = dbg.get('nshard', 0)
    C.wq = 'pool'
    gathers = []
    if nsh == 0:
        C.w_in = din("w_in", [L, D, IN_W])
        C.w_kv = din("w_kv", [L, D, 2 * BW])
        C.w_br = din("w_br", [L, 4, BW, D])
        C.w_out = din("w_out", [L, D, D])
    else:
        def sharded(name, rows, cols):
            src = din(name, [L, rows // nsh, cols])
            outl = []
            for l in range(L):
                bnc = dscr("%s_b%d" % (name, l), [rows // nsh, cols], BF16)
                full = dscr("%s_f%d" % (name, l), [rows, cols], BF16)
                gathers.append((src[l], bnc, full, rows // nsh))
                outl.append(full)
            return outl
        wg = [sharded("win%d" % g, D, hi - lo) for g, (lo, hi) in enumerate(WIN_GROUPS)]
        C.w_in = [WView([(lo, hi, wg[g][l]) for g, (lo, hi) in enumerate(WIN_GROUPS)]) for l in range(L)]
        C.w_kv = sharded("wkv", D, 2 * BW)
        wbr = [sharded("wbr%d" % n, BW, D) for n in range(4)]
        C.w_br = [[wbr[n][l] for n in range(4)] for l in range(L)]
        C.w_out = sharded("wout", D, D)
    C.prm = {
        'g_bc': din("g_bc", [L, 128, D]), 'mg_bc': din("mg_bc", [L, 128, D]),
        'lng_bc': din("lng_bc", [L, 128, BW]), 'lnb_bc': din("lnb_bc", [L, 128, BW]),
        'bs_bc': din("bs_bc", [L, 128, 8, 128]), 'wsT': din("wsT", [L, 128, 8, 128]),
        'cols': din("cols", [L, 128, NCOLS]), 'cst': din("cst", [128, NCST]),
        'rpbT': din("rpbT", [L, 8, 128, 2, 15, 64]),
        'colsA': din("colsA", [L, 64, NCA]), 'cstA': din("cstA", [64, 3136]),
    }
    y_all = nc.dram_tensor("y", [NB, T, D], F32, kind="ExternalOutput").ap()
    C.xnT = dscr("xnT", [KC, 128, T], BF16)
    C.yT = [dscr("yT%d" % n, [8, 128, T], BF16) for n in range(4)]
    C.mT = dscr("mT", [KC, 128, T], BF16)
    C.qT = dscr("qT", [8, 128, T], BF16)
    C.kT = dscr("kT", [8, 128, T], BF16)
    C.sgC = dscr("sgC", [8, 128, T], BF16)
    C.vC = dscr("vC", [T, BW], BF16)
    C.hA = dscr("hA", [34, 128, T], F32)
    C.a_w_up = din("a_w_up", [L, 2, 64, BW])
    C.a_a_up = din("a_a_up", [L, 2, 64, BW])
    xbuf = [dscr("xbuf%d" % i, [T, D], F32) for i in range(2)]
    dbg_in = {k: din("in_" + k, [8, 128, T]) for k in dbg.get('yT_in', [])}
    dbg_out = {k: nc.dram_tensor("dbg_" + k, [8, 128, T], F32, kind="ExternalOutput").ap() for k in dbg.get('yT_out', [])}
    if dbg.get('mT_out'):
        dbg_out['mT'] = nc.dram_tensor("dbg_mT", [KC, 128, T], F32, kind="ExternalOutput").ap()
    dbufs = {}

    def dbuf(name, idx):
        k = (name, idx)
        if k not in dbufs:
            dbufs[k] = Buf()
        return dbufs[k]
    C.dbuf = dbuf

    with ExitStack() as es:
        P = Prog(nc, es)
        C.P = P
        C.ident = sb(C, es, "ident", [128, 128], BF16)
        C.ones = sb(C, es, "ones", [128, 128], BF16)
        C.bones = sb(C, es, "bones", [128, 128], BF16)
        C.cols = sb(C, es, "cols", [128, NCOLS], F32)
        C.col = lambda name, l, i=0: C.cols.t[:, COLS[name] + i:COLS[name] + i + 1]
        cst = C.prm['cst']
        dma(C, 'pool', C.ident.t[:], cst[:, 0:128], writes=[C.ident.buf])
        dma(C, 'pool', C.ones.t[:], cst[:, 128:256], writes=[C.ones.buf])
        dma(C, 'pool', C.bones.t[:], cst[:, 256:384], writes=[C.bones.buf])
        for src, bnc, full, rows in gathers:
            for r0 in range(0, rows, 128):
                dma(C, 'pool', bnc[r0:r0 + 128, :], src[r0:r0 + 128, :], awrites=[dbuf('bnc', id(bnc))])
            P.emit('pool', lambda e, bnc=bnc, full=full: e.collective_compute(
                "AllGather", ALU.bypass, replica_groups=[list(range(nsh))], ins=[bnc[:, :]], outs=[full[:, :]]),
                reads=[dbuf('bnc', id(bnc))], dma=True, chain=True)
        for k, ap in dbg_in.items():
            n = int(k[-1])
            for kc in range(8):
                dma(C, 'pool', C.yT[n][kc], ap[kc], awrites=[dbuf('dbgin', 0)])
        P.barrier()
        C.bg = []
        preconv = (nsh == 0) and dbg.get('preconv', True) and L > 1
        if preconv:
            w32 = (C.w_in, C.w_kv, C.w_br, C.w_out)
            w16in = [[dscr("w16_in%d_%d" % (k_, g), [D, hi - lo], BF16) for g, (lo, hi) in enumerate(WIN_GROUPS)] for k_ in range(2)]
            w16 = (None, dscr("w16_kv", [2, D, 2 * BW], BF16),
                   dscr("w16_br", [2, 4, BW, D], BF16), dscr("w16_out", [2, D, D], BF16))

            def conv_jobs(l):
                k = l % 2
                jobs_ = []
                for r0 in range(0, D, 128):
                    for g, (lo, hi) in enumerate(WIN_GROUPS):
                        jobs_.append((w16in[k][g][r0:r0 + 128, :], w32[0][l, r0:r0 + 128, lo:hi]))
                    jobs_.append((w16[1][k, r0:r0 + 128, :], w32[1][l, r0:r0 + 128, :]))
                    jobs_.append((w16[3][k, r0:r0 + 128, :], w32[3][l, r0:r0 + 128, :]))
                for n in range(4):
                    for r0 in range(0, BW, 128):
                        jobs_.append((w16[2][k, n, r0:r0 + 128, :], w32[2][l, n, r0:r0 + 128, :]))
                return [(lambda o=o, i=i: dma(C, 'pool', o, i, awrites=[dbuf('w16', 0)])) for o, i in jobs_]
        for b in range(NB):
            x_in, y_out = x_all[b], y_all[b]
            C.mem = mem_all[b]
            for l in range(L):
                if preconv:
                    if l == 0:
                        C.w_in, C.w_kv, C.w_br, C.w_out = w32
                        C.wq = 'pool'
                    else:
                        C.w_in = [WView([(lo, hi, w16in[l % 2][g]) for g, (lo, hi) in enumerate(WIN_GROUPS)])] * L
                        C.w_kv = [w16[1][l % 2]] * L
                        C.w_br = [w16[2][l % 2]] * L
                        C.w_out = [w16[3][l % 2]] * L
                        C.wq = 'act'
                    C.bg = conv_jobs(l + 1) if l + 1 < L else []
                    C.bg_per = (len(C.bg) + 127) // 128
                x_src = x_in if l == 0 else xbuf[(l - 1) % 2]
                x_dst = y_out if l == L - 1 else xbuf[l % 2]
                dma(C, 'sp', C.cols.t[:], C.prm['cols'][l], writes=[C.cols.buf])
                phase_rmsnorm(C, x_src, C.prm['g_bc'][l], C.xnT, T)
                if 'A' not in dbg.get('skip', ''):
                    phase_A(C, l, T)
                if 'B' not in dbg.get('skip', ''):
                    phase_B(C, l, T)
                if 'C' not in dbg.get('skip', ''):
                    phase_C(C, l, T)
                if 'D' not in dbg.get('skip', ''):
                    phase_D(C, l, T)
                phase_merge(C, l, T)
                phase_out(C, l, x_src, x_dst, T)
        for k, ap in dbg_out.items():
            src = C.mT if k == 'mT' else C.yT[int(k[-1])]
            for kc in range(src.shape[0]):
                dma(C, 'pool', ap[kc], src[kc], awrites=[dbuf('dbgout', 0)])
        P.barrier()
        P.build()
    C.n_ops = P.n_ops
    return nc, C


def host_params(p, L):
    f = np.float32
    rep = lambda v: np.ascontiguousarray(np.broadcast_to(np.asarray(v, f)[:, None, :], (v.shape[0], 128, v.shape[1])))
    out = {}
    out['g_bc'] = rep(p['norm_g'][:L])
    out['mg_bc'] = rep(p['m_norm_g'][:L])
    out['lng_bc'] = rep(p['b_ln_g'][:L])
    out['lnb_bc'] = rep(p['b_ln_b'][:L])
    bs = np.asarray(p['b_b_s'][:L], f)
    out['bs_bc'] = np.ascontiguousarray(np.broadcast_to(bs[:, None, :, :], (L, 128, 8, 128)))
    out['wsT'] = np.ascontiguousarray(np.transpose(np.asarray(p['b_w_s'][:L], f), (0, 3, 1, 2)))
    cols = np.zeros((L, 128, NCOLS), f)
    colv = lambda v, n: np.transpose(np.asarray(v, f).reshape(L, n, 128), (0, 2, 1))
    cols[:, :, COLS['mq']:COLS['mq'] + 2] = colv(p['m_q_norm'][:L], 2)
    cols[:, :, COLS['mk']:COLS['mk'] + 2] = colv(p['m_k_norm'][:L], 2)
    cols[:, :, COLS['cq']] = np.tile(np.asarray(p['c_q_norm'][:L], f), (1, 2))
    cols[:, :, COLS['ck']] = np.tile(np.asarray(p['c_k_norm'][:L], f), (1, 2))
    conv = np.asarray(p['a_conv'][:L], f)
    cols[:, :, COLS['conv']:COLS['conv'] + 78] = np.transpose(conv.reshape(L, 3, 26, 128), (0, 3, 2, 1)).reshape(L, 128, 78)
    cols[:, :, COLS['w0']:COLS['w0'] + 16] = np.transpose(np.asarray(p['a_w0'][:L], f).reshape(L, 16, 128), (0, 2, 1))
    cols[:, :, COLS['a0']:COLS['a0'] + 16] = np.transpose(np.asarray(p['a_a0'][:L], f).reshape(L, 16, 128), (0, 2, 1))
    cols[:, :, COLS['kk']:COLS['kk'] + 8] = colv(p['a_k_k'][:L], 8)
    cols[:, :, COLS['ka']:COLS['ka'] + 8] = colv(p['a_k_a'][:L], 8)
    cols[:, :, COLS['rk']:COLS['rk'] + 8] = colv(np.asarray(p['a_r_k'][:L]).reshape(L, BW), 8)
    cols[:, :, COLS['lnxw']:COLS['lnxw'] + 8] = colv(p['a_lnx_w'][:L], 8)
    cols[:, :, COLS['lnxb']:COLS['lnxb'] + 8] = colv(p['a_lnx_b'][:L], 8)
    out['cols'] = cols
    rpb = np.asarray(p['c_rpb'][:L], f)
    qv = np.arange(64)[None, :]
    kv = np.arange(64)[:, None]
    dcol = np.clip(kv - qv, -15, 15) + 15
    sj = np.clip(qv - 8, 0, 48)
    valid = (kv >= sj) & (kv < sj + 16)
    tbl = rpb[:, :, :, dcol]
    tbl = np.where(valid[None, None, None], tbl, f(-1e30))
    lo = np.transpose(tbl, (0, 1, 3, 2, 4))
    hi = np.full_like(lo, f(-1e30))
    hi[:, :, :, 0:14, :] = lo[:, :, :, 1:15, :]
    t2 = np.concatenate([lo, hi], axis=2)
    out['rpbT'] = np.ascontiguousarray(np.transpose(t2.reshape(L, 8, 2, 128, 15, 64), (0, 1, 3, 2, 4, 5)))
    ca = np.zeros((L, 64, NCA), f)
    hd = lambda v: np.transpose(np.asarray(v, f).reshape(L, -1, 64), (0, 2, 1))
    for wi in range(3):
        for tap in range(3):
            ca[:, :, CA['conv'] + (wi * 16) * 3 + tap:CA['conv'] + (wi * 16 + 16) * 3:3] = hd(conv[:, tap, wi * BW:(wi + 1) * BW])
    for gi in range(4):
        for tap in range(3):
            ca[:, :, CA['convl'] + gi * 3 + tap] = conv[:, tap, 3 * BW + gi * 64:3 * BW + (gi + 1) * 64]
    ca[:, :, CA['w0']:CA['w0'] + 32] = hd(np.asarray(p['a_w0'][:L], f).reshape(L, 2 * BW))
    ca[:, :, CA['a0']:CA['a0'] + 32] = hd(np.asarray(p['a_a0'][:L], f).reshape(L, 2 * BW))
    ca[:, :, CA['kk']:CA['kk'] + 16] = hd(p['a_k_k'][:L])
    ca[:, :, CA['ka']:CA['ka'] + 16] = hd(p['a_k_a'][:L])
    ca[:, :, CA['rk']:CA['rk'] + 16] = hd(np.asarray(p['a_r_k'][:L], f).reshape(L, BW))
    ca[:, :, CA['lnxw']:CA['lnxw'] + 16] = hd(p['a_lnx_w'][:L])
    ca[:, :, CA['lnxb']:CA['lnxb'] + 16] = hd(p['a_lnx_b'][:L])
    out['colsA'] = ca
    out['a_w_up'] = np.ascontiguousarray(np.asarray(p['a_w_up'][:L], f))
    out['a_a_up'] = np.ascontiguousarray(np.asarray(p['a_a_up'][:L], f))
    ka = np.zeros((64, 3136), f)
    ka[:, 0:512] = 1.0
    ka[:, 0:512:64] = 0.0
    pi = np.arange(64)[:, None]
    fi = np.arange(64)[None, :]
    m2f = np.concatenate([(pi < fi), (pi <= fi)], axis=1).astype(f)
    m2b = np.concatenate([(pi > fi), (pi >= fi)], axis=1).astype(f)
    ka[:, 512:1536] = np.tile(m2f, (1, 8))
    ka[:, 1536:2560] = np.tile(m2b, (1, 8))
    ka[:, 2560:3072] = np.tile(np.eye(64, dtype=f), (1, 8))
    ka[:, 3072:3136] = 1.0
    out['cstA'] = ka
    cst = np.zeros((128, NCST), f)
    cst[:, 0:128] = np.eye(128, dtype=f)
    cst[:, 128:256] = 1.0
    cst[0:64, 256:320] = 1.0
    cst[64:128, 320:384] = 1.0
    out['cst'] = cst
    return out


N_CORES = 4
NB_PER_CORE = 1


def kernel(**inputs):
    p = {k: np.asarray(v) for k, v in inputs.items()}
    L = DEPTH
    nc, C = build_program(T=SEQ, n_layers=L, dbg={}, NB=NB_PER_CORE)
    hp = host_params(p, L)
    shared = dict(hp)
    shared['w_in'] = np.ascontiguousarray(p['w_in'], dtype=np.float32)
    shared['w_kv'] = np.ascontiguousarray(p['m_w_kv'], dtype=np.float32)
    shared['w_br'] = np.ascontiguousarray(p['w_branch'], dtype=np.float32)
    shared['w_out'] = np.ascontiguousarray(p['w_out'], dtype=np.float32)
    in_maps = []
    for c in range(N_CORES):
        m = dict(shared)
        m['x'] = np.ascontiguousarray(p['x'][c * NB_PER_CORE:(c + 1) * NB_PER_CORE], dtype=np.float32)
        m['mem'] = np.ascontiguousarray(p['mem'][c * NB_PER_CORE:(c + 1) * NB_PER_CORE], dtype=np.float32)
        in_maps.append(m)
    res = run_bass_kernel_spmd(nc, in_maps, core_ids=list(range(N_CORES)))
    return np.concatenate([np.asarray(r['y'], dtype=np.float32) for r in res.results], axis=0)
```

```python
import numpy as np
from contextlib import ExitStack

import concourse.bass as bass
import concourse.mybir as mybir
from concourse.bass_utils import run_bass_kernel_spmd

F32 = mybir.dt.float32
BF16 = mybir.dt.bfloat16
AF = mybir.ActivationFunctionType
ALU = mybir.AluOpType

D = 4096
SEQ = 4096
DEPTH = 4
BW = 1024
MEM = 256
KC = D // 128
A_SHIFT = 3 * BW + 256
A_W = A_SHIFT + BW
O1 = A_W
O2 = O1 + 3 * BW
O3 = O2 + 4 * BW
O4 = O3 + 2 * BW
IN_W = O4 + 4 * D
TB = 1024
ENGS = ('pe', 'act', 'dve', 'pool', 'sp')


class Buf:
    __slots__ = ('w', 'r')

    def __init__(self):
        self.w = {}
        self.r = {}


class Tl:
    def __init__(self, t, nbuf=1):
        self.t = t
        self.bufs = [Buf() for _ in range(nbuf)]

    @property
    def buf(self):
        return self.bufs[0]


class Prog:
    EPOCH = 60000
    NSLOT = 8

    def __init__(self, nc, es):
        self.nc = nc
        self.es = es
        self.sems = []
        self.ops = {e: [] for e in ENGS}
        self.cnt = {e: 0 for e in ENGS}
        self.csem = {e: self._newsem() for e in ENGS}
        self.waited = {e: {} for e in ENGS}
        self.dma_n = {e: 0 for e in ENGS}
        self.dma_sems = {e: None for e in ENGS}
        self.last = {}
        self.n_ops = 0
        self.chain_sem = None
        self.chain_n = 0

    def _newsem(self):
        s = self.es.enter_context(self.nc.semaphore("sem%d" % len(self.sems)))
        self.sems.append(s)
        return len(self.sems) - 1

    def emit(self, eng, fn, reads=(), writes=(), dma=False, awrites=(), chain=False):
        deps = {}

        def add(s, v):
            if deps.get(s, 0) < v:
                deps[s] = v

        for b in reads:
            for s, v in b.w.items():
                add(s, v)
        for b in writes:
            for s, v in b.w.items():
                add(s, v)
            for s, v in b.r.items():
                add(s, v)
        for b in awrites:
            for s, v in b.r.items():
                add(s, v)
        if dma and chain:
            if self.chain_sem is None:
                self.chain_sem = self._newsem()
            if self.chain_n > 0:
                add(self.chain_sem, 16 * self.chain_n)
            self.chain_n += 1
            ev = (self.chain_sem, 16 * self.chain_n)
            inc = 16
        elif dma:
            if self.dma_sems[eng] is None:
                self.dma_sems[eng] = [self._newsem() for _ in range(self.NSLOT)]
            n = self.dma_n[eng]
            self.dma_n[eng] += 1
            slot, rnd = n % self.NSLOT, n // self.NSLOT
            sem = self.dma_sems[eng][slot]
            if rnd > 0:
                add(sem, 16 * rnd)
            ev = (sem, 16 * (rnd + 1))
            inc = 16
        else:
            if self.cnt[eng] >= self.EPOCH:
                self.csem[eng] = self._newsem()
                self.cnt[eng] = 0
            self.cnt[eng] += 1
            ev = (self.csem[eng], self.cnt[eng])
            inc = 1
        wd = self.waited[eng]
        waits = []
        for s, v in deps.items():
            if (not dma) and eng == 'pe' and s == ev[0]:
                continue
            if wd.get(s, 0) >= v:
                continue
            wd[s] = v
            waits.append((s, v))
        self.ops[eng].append((waits, fn, ev[0], inc))
        for b in reads:
            if b.r.get(ev[0], 0) < ev[1]:
                b.r[ev[0]] = ev[1]
        for b in writes:
            b.w = {ev[0]: ev[1]}
            b.r = {}
        for b in awrites:
            if b.w.get(ev[0], 0) < ev[1]:
                b.w[ev[0]] = ev[1]
        if self.last.get(ev[0], 0) < ev[1]:
            self.last[ev[0]] = ev[1]
        self.n_ops += 1
        return ev

    def barrier(self):
        for eng in ENGS:
            wd = self.waited[eng]
            waits = []
            for s, v in self.last.items():
                if wd.get(s, 0) >= v:
                    continue
                wd[s] = v
                waits.append((s, v))
            if waits:
                self.ops[eng].append((waits, None, None, 0))

    def _replay(self, eng, e):
        sems = self.sems
        for waits, fn, sem, inc in self.ops[eng]:
            for s, v in waits:
                e.wait_ge(sems[s], v)
            if fn is not None:
                ins = fn(e)
                ins.then_inc(sems[sem], inc)

    def build(self):
        with self.nc.Block() as block:
            @block.tensor
            def _(e):
                self._replay('pe', e)

            @block.scalar
            def _(e):
                self._replay('act', e)

            @block.vector
            def _(e):
                self._replay('dve', e)

            @block.gpsimd
            def _(e):
                self._replay('pool', e)

            @block.sync
            def _(e):
                self._replay('sp', e)


class Ctx:
    pass


def sb(C, es, name, shape, dt, nbuf=1):
    C.uid += 1
    return Tl(es.enter_context(C.nc.sbuf_tensor("%s_%d" % (name, C.uid), shape, dt)), nbuf)


def ps(C, es, name, shape, dt=F32):
    C.uid += 1
    return Tl(es.enter_context(C.nc.psum_tensor("%s_%d" % (name, C.uid), shape, dt)))


def dma(C, q, out, in_, reads=(), writes=(), awrites=()):
    C.P.emit(q, lambda e, out=out, in_=in_: e.dma_start(out=out, in_=in_), reads=reads, writes=writes, dma=True,
             awrites=awrites)


def load_w(C, slot, src, kc_n, cw):
    g = 0
    for q in range(0, kc_n, 8):
        n = min(8, kc_n - q)
        dma(C, C.wq, slot.t[:, q:q + n, 0:cw],
            src[q * 128:(q + n) * 128, :].rearrange("(kc p) n -> p kc n", p=128),
            writes=[slot.bufs[g]])
        g += 1


class WView:
    def __init__(self, groups):
        self.groups = groups

    def __getitem__(self, key):
        rs, cs = key
        for lo, hi, ap in self.groups:
            if lo <= cs.start and cs.stop <= hi:
                return ap[rs, cs.start - lo:cs.stop - lo]
        raise KeyError(key)


def load_xT(C, dst, src3, t0, tb, kc_n=KC):
    g = 0
    for q in range(0, kc_n, 8):
        n = min(8, kc_n - q)
        dma(C, 'sp', dst.t[:, q:q + n, 0:tb], src3[q:q + n, :, t0:t0 + tb].rearrange("kc p t -> p kc t"),
            writes=[dst.bufs[g]])
        g += 1


def phase_rmsnorm(C, x_src, gbc_src, xnT_dst, T, eps=1e-6):
    P = C.P
    with ExitStack() as es:
        gb = sb(C, es, "gb", [128, D], F32)
        xt = [sb(C, es, "xt", [128, D], F32) for _ in range(2)]
        xn = [sb(C, es, "xn", [128, D], BF16) for _ in range(2)]
        junk = sb(C, es, "junk", [128, D], BF16)
        st = [sb(C, es, "st", [128, 2], F32) for _ in range(2)]
        xs = [sb(C, es, "xs", [128, KC, 512], BF16) for _ in range(2)]
        pt = [ps(C, es, "pt", [128, 8, 128], BF16) for _ in range(4)]
        dma(C, 'sp', gb.t[:], gbc_src, writes=[gb.buf])
        ntt = T // 128
        for tt in range(ntt):
            s = tt % 2
            x_, n_, st_ = xt[s], xn[s], st[s]
            xsb = xs[(tt // 4) % 2]
            dma(C, 'sp', x_.t[:], x_src[tt * 128:(tt + 1) * 128, :], writes=[x_.buf])
            P.emit('act', lambda e, x_=x_, st_=st_: e.activation(out=junk.t[:], in_=x_.t[:], func=AF.Square,
                                                                   accum_out=st_.t[:, 0:1]),
                   reads=[x_.buf], writes=[junk.buf, st_.buf])
            P.emit('act', lambda e, st_=st_: e.activation(out=st_.t[:, 1:2], in_=st_.t[:, 0:1], func=AF.Sqrt,
                                                           bias=eps, scale=1.0 / D),
                   reads=[st_.buf], writes=[st_.buf])
            P.emit('dve', lambda e, st_=st_: e.reciprocal(out=st_.t[:, 1:2], in_=st_.t[:, 1:2]),
                   reads=[st_.buf], writes=[st_.buf])
            P.emit('dve', lambda e, x_=x_, n_=n_, st_=st_: e.scalar_tensor_tensor(
                out=n_.t[:], in0=x_.t[:], scalar=st_.t[:, 1:2], in1=gb.t[:], op0=ALU.mult, op1=ALU.mult),
                reads=[x_.buf, st_.buf, gb.buf], writes=[n_.buf])
            for q in range(4):
                p_ = pt[q]

                def tr(e, n_=n_, p_=p_, q=q):
                    for j in range(8):
                        kc = q * 8 + j
                        ins = e.transpose(out=p_.t[:, j, :], in_=n_.t[:, kc * 128:(kc + 1) * 128], identity=C.ident.t[:])
                    return ins
                P.emit('pe', tr, reads=[n_.buf, C.ident.buf], writes=[p_.buf])
                eng = 'act' if q % 2 == 0 else 'dve'
                if eng == 'act':
                    P.emit('act', lambda e, p_=p_, xsb=xsb, q=q, tt=tt: e.activation(
                        out=xsb.t[:, q * 8:(q + 1) * 8, (tt % 4) * 128:(tt % 4 + 1) * 128], in_=p_.t[:], func=AF.Copy),
                        reads=[p_.buf], writes=[xsb.buf])
                else:
                    P.emit('dve', lambda e, p_=p_, xsb=xsb, q=q, tt=tt: e.tensor_copy(
                        out=xsb.t[:, q * 8:(q + 1) * 8, (tt % 4) * 128:(tt % 4 + 1) * 128], in_=p_.t[:]),
                        reads=[p_.buf], writes=[xsb.buf])
            if tt % 4 == 3:
                t0 = (tt // 4) * 512
                for q in range(0, KC, 8):
                    dma(C, 'sp', xnT_dst[q:q + 8, :, t0:t0 + 512].rearrange("kc p t -> p kc t"),
                        xsb.t[:, q:q + 8, :], reads=[xsb.buf], awrites=[C.dbuf('xnT', t0 // TB)])
    P.barrier()


def gemm_fm(C, wslot, kc_n, xT, ct, ntg, pss, extra_reads=()):
    def mm(e):
        ins = None
        for kc in range(kc_n):
            for tg in range(ntg):
                ins = e.matmul(pss[tg].t[:], wslot.t[:, kc, ct * 128:(ct + 1) * 128],
                               xT.t[:, kc, tg * 512:(tg + 1) * 512], start=(kc == 0), stop=(kc == kc_n - 1))
        return ins
    C.P.emit('pe', mm, reads=list(wslot.bufs) + list(xT.bufs) + list(extra_reads), writes=[p.buf for p in pss[:ntg]])


def gemm_tm(C, wslot, kc_n, xT, tt, cw, pst):
    def mm(e):
        ins = None
        for kc in range(kc_n):
            ins = e.matmul(pst.t[:, 0:cw], xT.t[:, kc, tt * 128:(tt + 1) * 128], wslot.t[:, kc, 0:cw],
                           start=(kc == 0), stop=(kc == kc_n - 1))
        return ins
    C.P.emit('pe', mm, reads=list(wslot.bufs) + list(xT.bufs), writes=[pst.buf])


class WStream:
    def __init__(self, C, es, kc_max, cw_max):
        self.C = C
        self.slots = [sb(C, es, "wsl", [128, kc_max, cw_max], BF16, nbuf=(kc_max + 7) // 8) for _ in range(2)]
        self.jobs = []

    def run(self, jobs):
        C = self.C
        n = len(jobs)
        if n == 0:
            return
        load_w(C, self.slots[0], jobs[0][0], jobs[0][1], jobs[0][2])
        for i in range(n):
            if i + 1 < n:
                load_w(C, self.slots[(i + 1) % 2], jobs[i + 1][0], jobs[i + 1][1], jobs[i + 1][2])
            jobs[i][3](self.slots[i % 2])


def evac_store_fm(C, stage, pst, dst_ap, dbuf, eng):
    if eng == 'act':
        C.P.emit('act', lambda e: e.activation(out=stage.t[:], in_=pst.t[:], func=AF.Copy),
                 reads=[pst.buf], writes=[stage.buf])
    else:
        C.P.emit('dve', lambda e: e.tensor_copy(out=stage.t[:], in_=pst.t[:]), reads=[pst.buf], writes=[stage.buf])
    dma(C, 'sp', dst_ap, stage.t[:], reads=[stage.buf], awrites=[dbuf])


def phase_B(C, l, T):
    P = C.P
    w = C.w_in[l]
    with ExitStack() as es:
        xT = sb(C, es, "xT", [128, KC, TB], BF16, nbuf=4)
        ws = WStream(C, es, KC, 512)
        vg = sb(C, es, "vg", [128, 8, BW], BF16, nbuf=8)
        lng = sb(C, es, "lng", [128, BW], F32)
        lnb = sb(C, es, "lnb", [128, BW], F32)
        wsT = sb(C, es, "wsT", [128, 8, 128], BF16)
        bsb = sb(C, es, "bsb", [128, 8, 128], F32)
        sv = sb(C, es, "sv", [128, 8, TB], BF16, nbuf=8)
        st6 = [sb(C, es, "st6", [128, 12], F32) for _ in range(2)]
        mv = [sb(C, es, "mv", [128, 4], F32) for _ in range(2)]
        vtmp = [sb(C, es, "vtmp", [128, BW], F32) for _ in range(2)]
        vln = [sb(C, es, "vln", [128, BW], BF16) for _ in range(2)]
        tmp = [sb(C, es, "tmpb", [128, 512], BF16) for _ in range(4)]
        ystage = [sb(C, es, "ystage", [128, 512], BF16) for _ in range(4)]
        pm = [ps(C, es, "pm", [128, 512]) for _ in range(4)]
        psv = [ps(C, es, "psv", [128, 512]) for _ in range(2)]
        dma(C, 'sp', lng.t[:], C.prm['lng_bc'][l], writes=[lng.buf])
        dma(C, 'sp', lnb.t[:], C.prm['lnb_bc'][l], writes=[lnb.buf])
        dma(C, 'sp', bsb.t[:], C.prm['bs_bc'][l], writes=[bsb.buf])
        dma(C, 'pool', wsT.t[:], C.prm['wsT'][l], writes=[wsT.buf])
        cnt = [0]
        jobs = []
        for tb in range(T // TB):
            t0 = tb * TB

            def job_v(slot, j, tb=tb, t0=t0):
                if j == 0:
                    load_xT(C, xT, C.xnT, t0, TB)
                for tt in range(8):
                    p_ = pm[cnt[0] % 4]
                    cnt[0] += 1
                    gemm_tm(C, slot, KC, xT, tt, 512, p_)
                    P.emit('act', lambda e, p_=p_, tt=tt: e.activation(out=vg.t[:, tt, j * 512:(j + 1) * 512], in_=p_.t[:],
                                                                      func=AF.Gelu_apprx_tanh),
                           reads=[p_.buf], writes=[vg.bufs[tt]])
                if j == 1:
                    for tt in range(8):
                        s = tt % 2
                        P.emit('dve', lambda e, s=s, tt=tt: e.bn_stats(out=st6[s].t[:, 0:6], in_=vg.t[:, tt, 0:512]),
                               reads=[vg.bufs[tt]], writes=[st6[s].buf])
                        P.emit('dve', lambda e, s=s, tt=tt: e.bn_stats(out=st6[s].t[:, 6:12], in_=vg.t[:, tt, 512:1024]),
                               reads=[vg.bufs[tt], st6[s].buf], writes=[st6[s].buf])
                        P.emit('dve', lambda e, s=s: e.bn_aggr(out=mv[s].t[:, 0:2], in_=st6[s].t[:]),
                               reads=[st6[s].buf], writes=[mv[s].buf])
                        P.emit('act', lambda e, s=s: e.activation(out=mv[s].t[:, 2:3], in_=mv[s].t[:, 1:2], func=AF.Sqrt,
                                                                    bias=1e-5, scale=1.0),
                               reads=[mv[s].buf], writes=[mv[s].buf])
                        P.emit('dve', lambda e, s=s: e.reciprocal(out=mv[s].t[:, 2:3], in_=mv[s].t[:, 2:3]),
                               reads=[mv[s].buf], writes=[mv[s].buf])
                        P.emit('dve', lambda e, s=s, tt=tt: e.tensor_scalar(
                            out=vtmp[s].t[:], in0=vg.t[:, tt, :], scalar1=mv[s].t[:, 0:1], scalar2=mv[s].t[:, 2:3],
                            op0=ALU.subtract, op1=ALU.mult),
                            reads=[vg.bufs[tt], mv[s].buf], writes=[vtmp[s].buf])
                        P.emit('pool', lambda e, s=s: e.tensor_tensor(out=vtmp[s].t[:], in0=vtmp[s].t[:], in1=lng.t[:], op=ALU.mult),
                               reads=[vtmp[s].buf, lng.buf], writes=[vtmp[s].buf])
                        P.emit('pool', lambda e, s=s: e.tensor_tensor(out=vln[s].t[:], in0=vtmp[s].t[:], in1=lnb.t[:], op=ALU.add),
                               reads=[vtmp[s].buf, lnb.buf], writes=[vln[s].buf])

                        def spm(e, s=s):
                            ins = None
                            for g in range(8):
                                ins = e.matmul(psv[g // 4].t[:, (g % 4) * 128:(g % 4 + 1) * 128],
                                               vln[s].t[:, g * 128:(g + 1) * 128], wsT.t[:, g, :], start=True, stop=True)
                            return ins
                        P.emit('pe', spm, reads=[vln[s].buf, wsT.buf], writes=[psv[0].buf, psv[1].buf])
                        for hh in range(2):
                            P.emit('dve', lambda e, hh=hh, tt=tt: e.tensor_tensor(
                                out=sv.t[:, hh * 4:(hh + 1) * 4, tt * 128:(tt + 1) * 128],
                                in0=psv[hh].t[:].rearrange("p (g q) -> p g q", g=4),
                                in1=bsb.t[:, hh * 4:(hh + 1) * 4, :], op=ALU.add),
                                reads=[psv[hh].buf, bsb.buf], writes=[sv.bufs[g_] for g_ in range(hh * 4, hh * 4 + 4)])

            def job_ug(slot, j, kind, tb=tb, t0=t0):
                for ct in range(4):
                    g = j * 4 + ct
                    pp = [pm[(cnt[0] % 2) * 2], pm[(cnt[0] % 2) * 2 + 1]]
                    cnt[0] += 1
                    gemm_fm(C, slot, KC, xT, ct, 2, pp)
                    for tg in range(2):
                        t_ = tmp[(cnt[0] * 2 + tg) % 4]
                        func = AF.Gelu_apprx_tanh if kind == 'u' else AF.Silu
                        P.emit('act', lambda e, t_=t_, p_=pp[tg], func=func: e.activation(out=t_.t[:], in_=p_.t[:], func=func),
                               reads=[pp[tg].buf], writes=[t_.buf])
                        if kind == 'u':
                            P.emit('dve', lambda e, t_=t_, g=g, tg=tg: e.tensor_tensor(
                                out=sv.t[:, g, tg * 512:(tg + 1) * 512], in0=t_.t[:], in1=sv.t[:, g, tg * 512:(tg + 1) * 512],
                                op=ALU.mult), reads=[t_.buf, sv.bufs[g]], writes=[sv.bufs[g]])
                        else:
                            y_ = ystage[(cnt[0] * 2 + tg) % 4]
                            P.emit('dve', lambda e, t_=t_, y_=y_, g=g, tg=tg: e.tensor_tensor(
                                out=y_.t[:], in0=t_.t[:], in1=sv.t[:, g, tg * 512:(tg + 1) * 512], op=ALU.mult),
                                reads=[t_.buf, sv.bufs[g]], writes=[y_.buf])
                            dma(C, 'sp', C.yT[1][g, :, t0 + tg * 512:t0 + (tg + 1) * 512], y_.t[:],
                                reads=[y_.buf], awrites=[C.dbuf('yT1', tb)])

            for j in range(2):
                jobs.append((w[:, O1 + BW + j * 512:O1 + BW + (j + 1) * 512], KC, 512, lambda s, j=j, f=job_v: f(s, j)))
            for j in range(2):
                jobs.append((w[:, O1 + j * 512:O1 + (j + 1) * 512], KC, 512, lambda s, j=j, f=job_ug: f(s, j, 'u')))
            for j in range(2):
                jobs.append((w[:, O1 + 2 * BW + j * 512:O1 + 2 * BW + (j + 1) * 512], KC, 512,
                             lambda s, j=j, f=job_ug: f(s, j, 'g')))
        ws.run(jobs)
    P.barrier()


def headnorm_fm(C, pq, q_sb, sq, pss, rs, gain_cols, out_tile, out_idx, nfeat, eps, tokn, ones=None):
    P = C.P
    ndc = len(pq)
    ones = ones or C.ones
    for dc in range(ndc):
        P.emit('act', lambda e, dc=dc: e.activation(out=q_sb.t[:, dc, 0:tokn], in_=pq[dc].t[:, 0:tokn], func=AF.Copy),
               reads=[pq[dc].buf], writes=[q_sb.bufs[dc]])
        P.emit('act', lambda e, dc=dc: e.activation(out=sq.t[:, dc, 0:tokn], in_=pq[dc].t[:, 0:tokn], func=AF.Square),
               reads=[pq[dc].buf], writes=[sq.bufs[dc]])

    def mm(e):
        ins = None
        for dc in range(ndc):
            ins = e.matmul(pss.t[:, 0:tokn], ones.t[:], sq.t[:, dc, 0:tokn], start=(dc == 0), stop=(dc == ndc - 1))
        return ins
    P.emit('pe', mm, reads=[ones.buf] + [sq.bufs[dc] for dc in range(ndc)], writes=[pss.buf])
    P.emit('act', lambda e: e.activation(out=rs.t[:, 0:tokn], in_=pss.t[:, 0:tokn], func=AF.Sqrt, bias=eps, scale=1.0 / nfeat),
           reads=[pss.buf], writes=[rs.buf])
    P.emit('dve', lambda e: e.reciprocal(out=rs.t[:, 0:tokn], in_=rs.t[:, 0:tokn]), reads=[rs.buf], writes=[rs.buf])
    for dc in range(ndc):
        P.emit('dve', lambda e, dc=dc: e.scalar_tensor_tensor(
            out=out_tile.t[:, out_idx[dc], 0:tokn], in0=q_sb.t[:, dc, 0:tokn], scalar=gain_cols[dc], in1=rs.t[:, 0:tokn],
            op0=ALU.mult, op1=ALU.mult),
            reads=[q_sb.bufs[dc], rs.buf, C.cols.buf], writes=[out_tile.bufs[out_idx[dc]]])


def phase_D(C, l, T):
    P = C.P
    w = C.w_in[l]
    wkv = C.w_kv[l]
    with ExitStack() as es:
        ws = WStream(C, es, KC, 512)
        kT = sb(C, es, "kT", [128, 8, MEM], BF16, nbuf=8)
        vmem = sb(C, es, "vmem", [128, 2, BW], BF16, nbuf=2)
        q_sb = sb(C, es, "q_sb", [128, 2, 512], F32, nbuf=2)
        sq = sb(C, es, "sq", [128, 2, 512], BF16, nbuf=2)
        rs = sb(C, es, "rs", [128, 512], F32)
        qn = sb(C, es, "qn", [128, 2, 512], BF16, nbuf=2)
        E = sb(C, es, "E", [128, 2, 512], BF16, nbuf=2)
        rd = sb(C, es, "rd", [128, 512], F32)
        tmp = [sb(C, es, "tmpd", [128, 512], BF16) for _ in range(2)]
        ystage = [sb(C, es, "ystd", [128, 512], BF16) for _ in range(4)]
        pm = [ps(C, es, "pm", [128, 512]) for _ in range(4)]
        pso = [ps(C, es, "pso", [128, 512]) for _ in range(2)]
        paux = ps(C, es, "paux", [128, 512])
        with ExitStack() as es2:
            mT = sb(C, es2, "mT", [128, KC, MEM], BF16, nbuf=1)
            gb = sb(C, es2, "mgb", [128, D], F32)
            xt = sb(C, es2, "mxt", [128, D], F32)
            xn = sb(C, es2, "mxn", [128, D], BF16)
            junk = sb(C, es2, "mjunk", [128, D], BF16)
            st = sb(C, es2, "mst", [128, 2], F32)
            pt = ps(C, es2, "mpt", [128, 8, 128], BF16)
            dma(C, 'sp', gb.t[:], C.prm['mg_bc'][l], writes=[gb.buf])
            for m in range(2):
                dma(C, 'sp', xt.t[:], C.mem[m * 128:(m + 1) * 128, :], writes=[xt.buf])
                P.emit('act', lambda e: e.activation(out=junk.t[:], in_=xt.t[:], func=AF.Square, accum_out=st.t[:, 0:1]),
                       reads=[xt.buf], writes=[junk.buf, st.buf])
                P.emit('act', lambda e: e.activation(out=st.t[:, 1:2], in_=st.t[:, 0:1], func=AF.Sqrt, bias=1e-6, scale=1.0 / D),
                       reads=[st.buf], writes=[st.buf])
                P.emit('dve', lambda e: e.reciprocal(out=st.t[:, 1:2], in_=st.t[:, 1:2]), reads=[st.buf], writes=[st.buf])
                P.emit('dve', lambda e: e.scalar_tensor_tensor(out=xn.t[:], in0=xt.t[:], scalar=st.t[:, 1:2], in1=gb.t[:],
                                                               op0=ALU.mult, op1=ALU.mult),
                       reads=[xt.buf, st.buf, gb.buf], writes=[xn.buf])
                for q in range(4):
                    def tr(e, q=q):
                        ins = None
                        for j in range(8):
                            kc = q * 8 + j
                            ins = e.transpose(out=pt.t[:, j, :], in_=xn.t[:, kc * 128:(kc + 1) * 128], identity=C.ident.t[:])
                        return ins
                    P.emit('pe', tr, reads=[xn.buf, C.ident.buf], writes=[pt.buf])
                    P.emit('dve', lambda e, q=q, m=m: e.tensor_copy(out=mT.t[:, q * 8:(q + 1) * 8, m * 128:(m + 1) * 128], in_=pt.t[:]),
                           reads=[pt.buf], writes=[mT.buf])
            jobs = []

            def job_k(slot, j):
                for h2 in range(2):
                    h = j * 2 + h2
                    for dc in range(2):
                        ct = h2 * 2 + dc

                        def mm(e, ct=ct, dc=dc):
                            ins = None
                            for kc in range(KC):
                                ins = e.matmul(pm[dc].t[:, 0:MEM], slot.t[:, kc, ct * 128:(ct + 1) * 128], mT.t[:, kc, :],
                                               start=(kc == 0), stop=(kc == KC - 1))
                            return ins
                        P.emit('pe', mm, reads=list(slot.bufs) + [mT.buf], writes=[pm[dc].buf])
                    headnorm_fm(C, [pm[0], pm[1]], q_sb, sq, paux, rs,
                                [C.col('mk', l, 0), C.col('mk', l, 1)], kT, [h * 2, h * 2 + 1], 256, 1e-6, MEM)

            def job_v(slot, j):
                for m in range(2):
                    gemm_tm(C, slot, KC, mT, m, 512, pm[2 + m])
                    P.emit('act', lambda e, m=m: e.activation(out=vmem.t[:, m, j * 512:(j + 1) * 512], in_=pm[2 + m].t[:], func=AF.Copy),
                           reads=[pm[2 + m].buf], writes=[vmem.bufs[m]])
            for j in range(2):
                jobs.append((wkv[:, j * 512:(j + 1) * 512], KC, 512, lambda s, j=j: job_k(s, j)))
            for j in range(2):
                jobs.append((wkv[:, BW + j * 512:BW + (j + 1) * 512], KC, 512, lambda s, j=j: job_v(s, j)))
            ws.run(jobs)
        P.barrier()
        xT = sb(C, es, "xT", [128, KC, TB], BF16, nbuf=4)
        oD = sb(C, es, "oD", [128, 8, TB], BF16, nbuf=8)
        jobs = []
        cnt = [0]
        for tb in range(T // TB):
            t0 = tb * TB

            def job_q(slot, j, tb=tb, t0=t0):
                if j == 0:
                    load_xT(C, xT, C.xnT, t0, TB)
                for h2 in range(2):
                    h = j * 2 + h2
                    for dc in range(2):
                        gemm_fm(C, slot, KC, xT, h2 * 2 + dc, 2, [pm[dc * 2], pm[dc * 2 + 1]])
                    for tg in range(2):
                        headnorm_fm(C, [pm[tg], pm[2 + tg]], q_sb, sq, paux, rs,
                                    [C.col('mq', l, 0), C.col('mq', l, 1)], qn, [0, 1], 256, 1e-6, 512)
                        for m in range(2):
                            def mm(e, m=m, h=h):
                                ins = None
                                for dc in range(2):
                                    ins = e.matmul(pso[m].t[:], kT.t[:, h * 2 + dc, m * 128:(m + 1) * 128], qn.t[:, dc, :],
                                                   start=(dc == 0), stop=(dc == 1))
                                return ins
                            P.emit('pe', mm, reads=[kT.bufs[h * 2], kT.bufs[h * 2 + 1], qn.bufs[0], qn.bufs[1]], writes=[pso[m].buf])
                            P.emit('act', lambda e, m=m: e.activation(out=E.t[:, m, :], in_=pso[m].t[:], func=AF.Exp, scale=1.0 / 16.0),
                                   reads=[pso[m].buf], writes=[E.bufs[m]])

                        def mmd(e):
                            ins = None
                            for m in range(2):
                                ins = e.matmul(paux.t[:], C.ones.t[:], E.t[:, m, :], start=(m == 0), stop=(m == 1))
                            return ins
                        P.emit('pe', mmd, reads=[C.ones.buf, E.bufs[0], E.bufs[1]], writes=[paux.buf])
                        P.emit('dve', lambda e: e.reciprocal(out=rd.t[:], in_=paux.t[:]), reads=[paux.buf], writes=[rd.buf])
                        for dc in range(2):
                            def mmo(e, dc=dc, h=h):
                                ins = None
                                for m in range(2):
                                    c0 = h * 256 + dc * 128
                                    ins = e.matmul(pso[dc].t[:], vmem.t[:, m, c0:c0 + 128], E.t[:, m, :], start=(m == 0), stop=(m == 1))
                                return ins
                            P.emit('pe', mmo, reads=[vmem.bufs[0], vmem.bufs[1], E.bufs[0], E.bufs[1]], writes=[pso[dc].buf])
                            P.emit('dve', lambda e, dc=dc, h=h, tg=tg: e.tensor_tensor(
                                out=oD.t[:, h * 2 + dc, tg * 512:(tg + 1) * 512], in0=pso[dc].t[:], in1=rd.t[:], op=ALU.mult),
                                reads=[pso[dc].buf, rd.buf], writes=[oD.bufs[h * 2 + dc]])

            def job_g(slot, j, tb=tb, t0=t0):
                for ct in range(4):
                    g = j * 4 + ct
                    pp = [pm[(cnt[0] % 2) * 2], pm[(cnt[0] % 2) * 2 + 1]]
                    cnt[0] += 1
                    gemm_fm(C, slot, KC, xT, ct, 2, pp)
                    for tg in range(2):
                        t_ = tmp[tg]
                        y_ = ystage[(cnt[0] * 2 + tg) % 4]
                        P.emit('act', lambda e, t_=t_, p_=pp[tg]: e.activation(out=t_.t[:], in_=p_.t[:], func=AF.Silu),
                               reads=[pp[tg].buf], writes=[t_.buf])
                        P.emit('dve', lambda e, t_=t_, y_=y_, g=g, tg=tg: e.tensor_tensor(
                            out=y_.t[:], in0=t_.t[:], in1=oD.t[:, g, tg * 512:(tg + 1) * 512], op=ALU.mult),
                            reads=[t_.buf, oD.bufs[g]], writes=[y_.buf])
                        dma(C, 'sp', C.yT[3][g, :, t0 + tg * 512:t0 + (tg + 1) * 512], y_.t[:],
                            reads=[y_.buf], awrites=[C.dbuf('yT3', tb)])
            for j in range(2):
                jobs.append((w[:, O3 + j * 512:O3 + (j + 1) * 512], KC, 512, lambda s, j=j, f=job_q: f(s, j)))
            for j in range(2):
                jobs.append((w[:, O3 + BW + j * 512:O3 + BW + (j + 1) * 512], KC, 512, lambda s, j=j, f=job_g: f(s, j)))
        ws.run(jobs)
    P.barrier()


CA = {'conv': 0, 'convl': 144, 'w0': 156, 'a0': 188, 'kk': 220, 'ka': 236, 'rk': 252, 'lnxw': 268, 'lnxb': 284}
NCA = 300
CH = 64
NCG = 8


def phase_A(C, l, T):
    P = C.P
    w = C.w_in[l]
    NTG = T // 512
    NCHK = T // CH

    def E(eng, fn, r=(), w=(), aw=()):
        P.emit(eng, fn, reads=r, writes=w, awrites=aw)

    with ExitStack() as es:
        xT = sb(C, es, "xT", [128, KC, TB], BF16, nbuf=4)
        ws = WStream(C, es, KC, 512)
        stage = [sb(C, es, "astage", [128, 512], F32) for _ in range(4)]
        pm = [ps(C, es, "pma", [128, 512]) for _ in range(4)]
        cnt = [0]
        jobs = []
        for tb in range(T // TB):
            t0 = tb * TB

            def job(slot, c0, cw, tb=tb, t0=t0):
                if c0 == 0:
                    load_xT(C, xT, C.xnT, t0, TB)
                for ct in range(cw // 128):
                    tile = c0 // 128 + ct
                    pp = [pm[(cnt[0] % 2) * 2], pm[(cnt[0] % 2) * 2 + 1]]
                    cnt[0] += 1
                    gemm_fm(C, slot, KC, xT, ct, 2, pp)
                    for tg in range(2):
                        k = (cnt[0] * 2 + tg) % 4
                        func = AF.Silu if tile >= 26 else AF.Copy
                        E('act', lambda e, k=k, p_=pp[tg], func=func: e.activation(out=stage[k].t[:], in_=p_.t[:], func=func),
                          r=[pp[tg].buf], w=[stage[k].buf])
                        dma(C, 'sp', C.hA[tile, :, t0 + tg * 512:t0 + (tg + 1) * 512], stage[k].t[:], reads=[stage[k].buf],
                            awrites=[C.dbuf('hA', tb)])
            for c0, cw in [(0, 512), (512, 512), (1024, 512), (1536, 512), (2048, 512), (2560, 512), (3072, 256), (3328, 512), (3840, 512)]:
                jobs.append((w[:, c0:c0 + cw], KC, cw, lambda s, c0=c0, cw=cw, f=job: f(s, c0, cw)))
        ws.run(jobs)
    P.barrier()

    with ExitStack() as es:
        F = lambda name, shape=(64, 512), dt=F32: sb(C, es, name, list(shape), dt)
        ca = F("ca", (64, NCA))
        mskS = F("mskS")
        m2 = [F("m2f", (64, NCG, 128)), F("m2b", (64, NCG, 128))]
        I8 = F("I8", (64, NCG, 64))
        id64 = F("id64", (64, 64), BF16)
        on64 = F("on64", (64, 64), BF16)
        wup = F("wup", (64, 2, BW), BF16)
        aup = F("aup", (64, 2, BW), BF16)
        cst = C.prm['cstA']
        dma(C, 'sp', ca.t[:], C.prm['colsA'][l], writes=[ca.buf])
        dma(C, 'sp', mskS.t[:], cst[:, 0:512], writes=[mskS.buf])
        dma(C, 'sp', m2[0].t[:], cst[:, 512:1536].rearrange("p (c f) -> p c f", c=NCG), writes=[m2[0].buf])
        dma(C, 'sp', m2[1].t[:], cst[:, 1536:2560].rearrange("p (c f) -> p c f", c=NCG), writes=[m2[1].buf])
        dma(C, 'sp', I8.t[:], cst[:, 2560:3072].rearrange("p (c f) -> p c f", c=NCG), writes=[I8.buf])
        dma(C, 'pool', id64.t[:], cst[:, 2560:2624], writes=[id64.buf])
        dma(C, 'pool', on64.t[:], cst[:, 3072:3136], writes=[on64.buf])
        dma(C, 'pool', wup.t[:], C.a_w_up[l].rearrange("z r c -> r z c"), writes=[wup.buf])
        dma(C, 'pool', aup.t[:], C.a_a_up[l].rearrange("z r c -> r z c"), writes=[aup.buf])
        col = lambda name, i=0: ca.t[:, CA[name] + i:CA[name] + i + 1]
        raw = [F("raw%d" % i, (64, 514)) for i in range(4)]
        tw = [[F("tw%d%d" % (i, z), (64, 512), BF16) for z in range(2)] for i in range(2)]
        r_, k_, v_ = F("r_"), F("k_"), F("v_")
        kk_, t1, t2, t3 = F("kk_"), F("t1"), F("t2"), F("t3")
        a_, kt_ = [F("a0_"), F("a1_")], [F("kt0"), F("kt1")]
        lw_, Pc, Qc, Tt, CLb = F("lw_"), F("Pc"), F("Qc"), F("Tt"), F("CLb")
        E1, E1x, Em, Er = F("E1"), F("E1x"), F("Em"), F("Er")
        WC = F("WC", (64, NCG))
        b_ = F("b_")
        vb = F("vb", (64, 512), BF16)
        sqk = F("sqk", (64, 512), BF16)
        ZR = F("ZR", (64, NCG, 128), BF16)
        Bt, Kt, Bh, Kh = [F(n, (64, NCG, 64), BF16) for n in ("Bt", "Kt", "Bh", "Kh")]
        SP = F("SP", (64, NCG, 128), BF16)
        PT = F("PT", (64, NCG, 64), BF16)
        Abr = F("Abr", (64, NCG, 64), BF16)
        MK = F("MK", (64, NCG, 128), BF16)
        VT, ZT, BhT, KhT, X0T, U0T, ZpT = [F(n, (64, NCG, 64), BF16) for n in ("VT", "ZT", "BhT", "KhT", "X0T", "U0T", "ZpT")]
        GT = F("GT", (64, 2, NCHK, 64), BF16)
        Hh = F("Hh", (64, 2, NCHK, 64), BF16)
        Rp = F("Rp", (64, 2, NCHK, 64), BF16)
        Y0 = F("Y0", (64, 2, NCHK, 64), BF16)
        Sall = F("Sall", (64, 2, NCHK, 64), BF16)
        rkk = F("rkk", (64, T), BF16)
        vfull = F("vfull", (64, T), BF16)
        sgfull = F("sgfull", (64, T), BF16)
        yst = [F("yst%d" % i, (64, 512), BF16) for i in range(2)]
        pA = ps(C, es, "pA", [64, NCG, 128])
        pB = ps(C, es, "pB", [64, NCG, 128])
        pC = ps(C, es, "pC", [64, NCG, 64])
        pD = ps(C, es, "pD", [64, NCG, 64])
        pTr = ps(C, es, "pTr", [64, NCG, 64], BF16)
        pTr2 = ps(C, es, "pTr2", [64, NCG, 64], BF16)

        def v3(t, lo=0, hi=None):
            return t.t[:].rearrange("p (c f) -> p c f", f=CH)

        def conv(dst, src, base):
            E('dve', lambda e: e.tensor_scalar(out=dst.t[:], in0=src.t[:, 0:512], scalar1=col(*base(0)), scalar2=None, op0=ALU.mult),
              r=[src.buf, ca.buf], w=[dst.buf])
            E('dve', lambda e: e.scalar_tensor_tensor(out=dst.t[:], in0=src.t[:, 1:513], scalar=col(*base(1)), in1=dst.t[:],
                                                      op0=ALU.mult, op1=ALU.add), r=[src.buf, ca.buf, dst.buf], w=[dst.buf])
            E('dve', lambda e: e.scalar_tensor_tensor(out=dst.t[:], in0=src.t[:, 2:514], scalar=col(*base(2)), in1=dst.t[:],
                                                      op0=ALU.mult, op1=ALU.add), r=[src.buf, ca.buf, dst.buf], w=[dst.buf])

        def load_halo(dst, tile, row0, c0):
            lo, hi = max(c0 - 1, 0), min(c0 + 513, T)
            if c0 == 0:
                E('pool', lambda e: e.memset(dst.t[:, 0:1], 0.0), w=[dst.buf])
            if c0 + 512 == T:
                E('pool', lambda e: e.memset(dst.t[:, 513:514], 0.0), w=[dst.buf])
            dma(C, 'sp', dst.t[:, lo - (c0 - 1):hi - (c0 - 1)], C.hA[tile, row0:row0 + 64, lo:hi], writes=[dst.buf])

        def chunk_mm(pt, osl, lhs_fn, rhs_fn, reads, n2=1, lhs2=None, rhs2=None):
            def mm(e):
                ins = None
                for c in range(NCG):
                    ins = e.matmul(pt.t[:, c, osl], lhs_fn(c), rhs_fn(c), start=True, stop=(lhs2 is None))
                    if lhs2 is not None:
                        ins = e.matmul(pt.t[:, c, osl], lhs2(c), rhs2(c), start=False, stop=True)
                return ins
            E('pe', mm, r=reads, w=[pt.buf])

        def chunk_tr(pt, src3, reads):
            def tr(e):
                ins = None
                for c in range(NCG):
                    ins = e.transpose(out=pt.t[:, c, :], in_=src3(c), identity=id64.t[:])
                return ins
            E('pe', tr, r=list(reads) + [id64.buf], w=[pt.buf])

        A0 = slice(0, 64)
        A1 = slice(64, 128)
        def do_head(h):
            hp, hr = h // 2, (h % 2) * 64
            def do_cg(cg):
                c0 = cg * 512
                cb = cg * NCG
                for _ in range(getattr(C, 'bg_per', 1)):
                    if C.bg:
                        C.bg.pop(0)()
                for gi in range(4):
                    dma(C, 'sp', tw[gi // 2][gi % 2].t[:], C.lora16[gi, :, c0:c0 + 512], writes=[tw[gi // 2][gi % 2].buf])
                for wi, dst in enumerate((r_, k_, v_)):
                    load_halo(raw[wi], wi * 8 + hp, hr, c0)
                    conv(dst, raw[wi], lambda tap, wi=wi: ('conv', (wi * 16 + h) * 3 + tap))
                E('dve', lambda e: e.tensor_scalar(out=t1.t[:], in0=k_.t[:], scalar1=col('kk', h), scalar2=None, op0=ALU.mult),
                  r=[k_.buf, ca.buf], w=[t1.buf])
                E('act', lambda e: e.activation(out=sqk.t[:], in_=t1.t[:], func=AF.Square), r=[t1.buf], w=[sqk.buf])
                E('pe', lambda e: e.matmul(pC.t[:].rearrange("p c f -> p (c f)"), on64.t[:], sqk.t[:], start=True, stop=True),
                  r=[on64.buf, sqk.buf], w=[pC.buf])
                E('act', lambda e: e.activation(out=t2.t[:], in_=pC.t[:].rearrange("p c f -> p (c f)"), func=AF.Sqrt), r=[pC.buf], w=[t2.buf])
                E('dve', lambda e: e.tensor_scalar(out=t2.t[:], in0=t2.t[:], scalar1=1e-12, scalar2=None, op0=ALU.max), r=[t2.buf], w=[t2.buf])
                E('dve', lambda e: e.reciprocal(out=t2.t[:], in_=t2.t[:]), r=[t2.buf], w=[t2.buf])
                E('dve', lambda e: e.tensor_tensor(out=kk_.t[:], in0=t1.t[:], in1=t2.t[:], op=ALU.mult), r=[t1.buf, t2.buf], w=[kk_.buf])
                E('act', lambda e: e.activation(out=vb.t[:], in_=v_.t[:], func=AF.Copy), r=[v_.buf], w=[vb.buf])
                E('pool', lambda e, c0=c0: e.tensor_copy(out=vfull.t[:, c0:c0 + 512], in_=v_.t[:]), r=[v_.buf], aw=[vfull.buf])
                chunk_tr(pTr, lambda c: vb.t[:, c * CH:(c + 1) * CH], [vb.buf])
                E('dve', lambda e: e.tensor_copy(out=VT.t[:], in_=pTr.t[:]), r=[pTr.buf], w=[VT.buf])
                def do_z(z):
                    E('pe', lambda e, z=z: e.matmul(pC.t[:].rearrange("p c f -> p (c f)"), wup.t[:, z, h * 64:(h + 1) * 64], tw[0][z].t[:], start=True, stop=True),
                      r=[wup.buf, tw[0][z].buf], w=[pC.buf])
                    E('act', lambda e, z=z: e.activation(out=lw_.t[:], in_=pC.t[:].rearrange("p c f -> p (c f)"), func=AF.Sigmoid,
                                                         bias=col('w0', z * 16 + h), scale=1.0), r=[pC.buf, ca.buf], w=[lw_.buf])
                    E('pool', lambda e: e.tensor_scalar(out=lw_.t[:], in0=lw_.t[:], scalar1=-0.6065306597126334, scalar2=None, op0=ALU.mult),
                      r=[lw_.buf], w=[lw_.buf])
                    E('pe', lambda e, z=z: e.matmul(pD.t[:].rearrange("p c f -> p (c f)"), aup.t[:, z, h * 64:(h + 1) * 64], tw[1][z].t[:], start=True, stop=True),
                      r=[aup.buf, tw[1][z].buf], w=[pD.buf])
                    E('act', lambda e, z=z: e.activation(out=a_[z].t[:], in_=pD.t[:].rearrange("p c f -> p (c f)"), func=AF.Sigmoid,
                                                         bias=col('a0', z * 16 + h), scale=1.0), r=[pD.buf, ca.buf], w=[a_[z].buf])
                    E('dve', lambda e, z=z: e.tensor_scalar(out=t3.t[:], in0=a_[z].t[:], scalar1=-1.0, scalar2=col('ka', h), op0=ALU.add, op1=ALU.mult),
                      r=[a_[z].buf, ca.buf], w=[t3.buf])
                    E('dve', lambda e, z=z: e.scalar_tensor_tensor(out=kt_[z].t[:], in0=t3.t[:], scalar=1.0, in1=k_.t[:], op0=ALU.add, op1=ALU.mult),
                      r=[t3.buf, k_.buf], w=[kt_[z].buf])
                    E('pool', lambda e, z=z: e.tensor_tensor(out=b_.t[:], in0=kk_.t[:], in1=a_[z].t[:], op=ALU.mult), r=[kk_.buf, a_[z].buf], w=[b_.buf])
                    E('dve', lambda e: e.tensor_tensor_scan(out=Pc.t[:], data0=mskS.t[:], data1=lw_.t[:], initial=0.0, op0=ALU.mult, op1=ALU.add),
                      r=[mskS.buf, lw_.buf], w=[Pc.buf])
                    E('pool', lambda e: e.tensor_tensor(out=Qc.t[:], in0=Pc.t[:], in1=lw_.t[:], op=ALU.subtract), r=[Pc.buf, lw_.buf], w=[Qc.buf])
                    if C.dbg.get('a_bcast', True):
                        E('dve', lambda e: e.tensor_tensor(out=v3(Tt), in0=v3(Pc)[:, :, CH - 1:CH].to_broadcast([64, NCG, CH]), in1=v3(Pc), op=ALU.subtract),
                          r=[Pc.buf], w=[Tt.buf])
                    else:
                        for c in range(NCG):
                            E('dve', lambda e, c=c: e.tensor_scalar(out=Tt.t[:, c * CH:(c + 1) * CH], in0=Pc.t[:, c * CH:(c + 1) * CH], scalar1=-1.0,
                                                                    scalar2=Pc.t[:, c * CH + CH - 1:c * CH + CH], op0=ALU.mult, op1=ALU.add),
                              r=[Pc.buf], aw=[Tt.buf])
                    E('act', lambda e: e.activation(out=WC.t[:], in_=v3(Pc)[:, :, CH - 1], func=AF.Exp), r=[Pc.buf], w=[WC.buf])
                    if z == 0:
                        cl, clx, cr = Pc, Qc, Tt
                    else:
                        E('pool', lambda e: e.tensor_tensor(out=CLb.t[:], in0=Tt.t[:], in1=lw_.t[:], op=ALU.add), r=[Tt.buf, lw_.buf], w=[CLb.buf])
                        cl, clx, cr = CLb, Tt, Qc
                    E('act', lambda e, cl=cl: e.activation(out=E1.t[:], in_=cl.t[:], func=AF.Exp), r=[cl.buf], w=[E1.buf])
                    E('act', lambda e, clx=clx: e.activation(out=E1x.t[:], in_=clx.t[:], func=AF.Exp), r=[clx.buf], w=[E1x.buf])
                    E('act', lambda e, cl=cl: e.activation(out=Em.t[:], in_=cl.t[:], func=AF.Exp, scale=-1.0), r=[cl.buf], w=[Em.buf])
                    E('act', lambda e, cr=cr: e.activation(out=Er.t[:], in_=cr.t[:], func=AF.Exp), r=[cr.buf], w=[Er.buf])
                    E('dve', lambda e: e.scalar_tensor_tensor(out=ZR.t[:, :, A0], in0=v3(kk_), scalar=-1.0, in1=v3(E1x), op0=ALU.mult, op1=ALU.mult),
                      r=[kk_.buf, E1x.buf], aw=[ZR.buf])
                    E('pool', lambda e: e.tensor_tensor(out=ZR.t[:, :, A1], in0=v3(r_), in1=v3(E1), op=ALU.mult), r=[r_.buf, E1.buf], aw=[ZR.buf])
                    E('dve', lambda e: e.tensor_tensor(out=Bt.t[:], in0=v3(b_), in1=v3(Em), op=ALU.mult), r=[b_.buf, Em.buf], w=[Bt.buf])
                    E('pool', lambda e, z=z: e.tensor_tensor(out=Kt.t[:], in0=v3(kt_[z]), in1=v3(Em), op=ALU.mult), r=[kt_[z].buf, Em.buf], w=[Kt.buf])
                    E('dve', lambda e: e.tensor_tensor(out=Bh.t[:], in0=v3(b_), in1=v3(Er), op=ALU.mult), r=[b_.buf, Er.buf], w=[Bh.buf])
                    E('pool', lambda e, z=z: e.tensor_tensor(out=Kh.t[:], in0=v3(kt_[z]), in1=v3(Er), op=ALU.mult), r=[kt_[z].buf, Er.buf], w=[Kh.buf])
                    mz = m2[z]
                    chunk_mm(pA, slice(0, 128), lambda c: Bt.t[:, c, :], lambda c: ZR.t[:, c, :], [Bt.buf, ZR.buf])
                    E('dve', lambda e, mz=mz: e.tensor_tensor(out=SP.t[:, :, A1], in0=pA.t[:, :, A0], in1=mz.t[:, :, A0], op=ALU.mult),
                      r=[pA.buf, mz.buf], aw=[SP.buf])
                    E('dve', lambda e, mz=mz: e.tensor_tensor(out=Abr.t[:], in0=pA.t[:, :, A1], in1=mz.t[:, :, A1], op=ALU.mult),
                      r=[pA.buf, mz.buf], w=[Abr.buf])
                    E('pool', lambda e: e.tensor_tensor(out=SP.t[:, :, A0], in0=SP.t[:, :, A1], in1=I8.t[:], op=ALU.add), r=[SP.buf, I8.buf], aw=[SP.buf])
                    chunk_mm(pB, slice(0, 128), lambda c: Kt.t[:, c, :], lambda c: ZR.t[:, c, :], [Kt.buf, ZR.buf])
                    E('dve', lambda e, mz=mz: e.tensor_tensor(out=MK.t[:], in0=pB.t[:], in1=mz.t[:], op=ALU.mult), r=[pB.buf, mz.buf], w=[MK.buf])
                    chunk_mm(pC, slice(0, 64), lambda c: ZR.t[:, c, A0], lambda c: Bt.t[:, c, :], [Bt.buf, ZR.buf])
                    mT_ = m2[1 - z]
                    E('dve', lambda e, mT_=mT_: e.tensor_tensor(out=PT.t[:], in0=pC.t[:], in1=mT_.t[:, :, A0], op=ALU.mult), r=[pC.buf, mT_.buf], w=[PT.buf])
                    for kstep in range(5):
                        chunk_mm(pA, slice(0, 64), lambda c: PT.t[:, c, :], lambda c: SP.t[:, c, A1], [PT.buf, SP.buf])
                        chunk_mm(pD, slice(0, 64), lambda c: SP.t[:, c, A1], lambda c: PT.t[:, c, :], [PT.buf, SP.buf])
                        E('act', lambda e: e.activation(out=SP.t[:, :, A1], in_=pA.t[:, :, A0], func=AF.Copy), r=[pA.buf], w=[SP.buf])
                        E('dve', lambda e: e.tensor_copy(out=PT.t[:], in_=pD.t[:]), r=[pD.buf], w=[PT.buf])
                        chunk_mm(pC, slice(0, 64), lambda c: PT.t[:, c, :], lambda c: SP.t[:, c, A0], [PT.buf, SP.buf])
                        E('dve', lambda e: e.tensor_tensor(out=SP.t[:, :, A0], in0=pC.t[:], in1=SP.t[:, :, A0], op=ALU.add), r=[pC.buf, SP.buf], w=[SP.buf])
                    Tm = lambda c: SP.t[:, c, A0]
                    chunk_tr(pTr, lambda c: ZR.t[:, c, A0], [ZR.buf])
                    E('dve', lambda e: e.tensor_copy(out=ZT.t[:], in_=pTr.t[:]), r=[pTr.buf], w=[ZT.buf])
                    chunk_tr(pTr2, lambda c: Bh.t[:, c, :], [Bh.buf])
                    E('act', lambda e: e.activation(out=BhT.t[:], in_=pTr2.t[:], func=AF.Copy), r=[pTr2.buf], w=[BhT.buf])
                    chunk_tr(pTr, lambda c: Kh.t[:, c, :], [Kh.buf])
                    E('dve', lambda e: e.tensor_copy(out=KhT.t[:], in_=pTr.t[:]), r=[pTr.buf], w=[KhT.buf])
                    chunk_mm(pD, slice(0, 64), lambda c: MK.t[:, c, A0], lambda c: VT.t[:, c, :], [MK.buf, VT.buf])
                    E('act', lambda e: e.activation(out=X0T.t[:], in_=pD.t[:], func=AF.Copy), r=[pD.buf], w=[X0T.buf])
                    chunk_mm(pC, slice(0, 64), Tm, lambda c: X0T.t[:, c, :], [SP.buf, X0T.buf])
                    E('dve', lambda e: e.tensor_copy(out=U0T.t[:], in_=pC.t[:]), r=[pC.buf], w=[U0T.buf])
                    chunk_mm(pD, slice(0, 64), Tm, lambda c: ZT.t[:, c, :], [SP.buf, ZT.buf])
                    E('act', lambda e: e.activation(out=ZpT.t[:], in_=pD.t[:], func=AF.Copy), r=[pD.buf], w=[ZpT.buf])
                    chunk_mm(pC, slice(0, 64), lambda c: ZpT.t[:, c, :], lambda c: BhT.t[:, c, :], [ZpT.buf, BhT.buf])
                    if C.dbg.get('a_bcast', True):
                        E('pool', lambda e: e.tensor_tensor(out=v3(t3), in0=I8.t[:], in1=WC.t[:, :].unsqueeze(2).to_broadcast([64, NCG, CH]), op=ALU.mult),
                          r=[I8.buf, WC.buf], w=[t3.buf])
                        E('dve', lambda e, z=z, cb=cb: e.tensor_tensor(out=GT.t[:, z, cb:cb + NCG, :], in0=pC.t[:], in1=v3(t3), op=ALU.add),
                          r=[pC.buf, t3.buf], aw=[GT.buf])
                    else:
                        for c in range(NCG):
                            E('dve', lambda e, c=c, z=z, cb=cb: e.scalar_tensor_tensor(out=GT.t[:, z, cb + c, :], in0=I8.t[:, 0, :], scalar=WC.t[:, c:c + 1],
                                                                                     in1=pC.t[:, c, :], op0=ALU.mult, op1=ALU.add),
                              r=[I8.buf, WC.buf, pC.buf], aw=[GT.buf])
                    chunk_mm(pD, slice(0, 64), lambda c: BhT.t[:, c, :], lambda c: U0T.t[:, c, :], [BhT.buf, U0T.buf, KhT.buf, VT.buf],
                             lhs2=lambda c: KhT.t[:, c, :], rhs2=lambda c: VT.t[:, c, :])
                    E('act', lambda e, z=z, cb=cb: e.activation(out=Hh.t[:, z, cb:cb + NCG, :], in_=pD.t[:], func=AF.Copy), r=[pD.buf], aw=[Hh.buf])
                    chunk_mm(pC, slice(0, 64), lambda c: ZpT.t[:, c, :], lambda c: Abr.t[:, c, :], [ZpT.buf, Abr.buf])
                    E('dve', lambda e, z=z, cb=cb: e.tensor_tensor(out=Rp.t[:, z, cb:cb + NCG, :], in0=pC.t[:], in1=ZR.t[:, :, A1], op=ALU.add),
                      r=[pC.buf, ZR.buf], aw=[Rp.buf])
                    chunk_mm(pD, slice(0, 64), lambda c: U0T.t[:, c, :], lambda c: Abr.t[:, c, :], [U0T.buf, Abr.buf, VT.buf, MK.buf],
                             lhs2=lambda c: VT.t[:, c, :], rhs2=lambda c: MK.t[:, c, A1])
                    E('act', lambda e, z=z, cb=cb: e.activation(out=Y0.t[:, z, cb:cb + NCG, :], in_=pD.t[:], func=AF.Copy), r=[pD.buf], aw=[Y0.buf])
                for z in range(2):
                    do_z(z)
                E('pool', lambda e: e.tensor_tensor(out=t3.t[:], in0=kt_[0].t[:], in1=kt_[1].t[:], op=ALU.add), r=[kt_[0].buf, kt_[1].buf], w=[t3.buf])
                E('dve', lambda e, c0=c0: e.scalar_tensor_tensor(out=rkk.t[:, c0:c0 + 512], in0=r_.t[:], scalar=col('rk', h), in1=t3.t[:], op0=ALU.mult, op1=ALU.mult),
                  r=[r_.buf, ca.buf, t3.buf], aw=[rkk.buf])
                dma(C, 'pool', sgfull.t[:, c0:c0 + 512], C.hA[26 + hp, hr:hr + 64, c0:c0 + 512], awrites=[sgfull.buf])
            for cg in range(NTG):
                do_cg(cg)
            _scan_chain(C, E, GT, Hh, Sall, NCHK, pA, pB)
            def do_out(cg):
                cb = cg * NCG
                c0 = cg * 512
                for z in range(2):
                    pz = pC if z == 0 else pD
                    chunk_mm(pz, slice(0, 64), lambda c, z=z, cb=cb: Sall.t[:, z, cb + c, :], lambda c, z=z, cb=cb: Rp.t[:, z, cb + c, :], [Sall.buf, Rp.buf])
                E('dve', lambda e, cb=cb: e.tensor_tensor(out=v3(t1), in0=pC.t[:], in1=Y0.t[:, 0, cb:cb + NCG, :], op=ALU.add), r=[pC.buf, Y0.buf], w=[t1.buf])
                E('dve', lambda e, cb=cb: e.tensor_tensor(out=v3(t2), in0=pD.t[:], in1=Y0.t[:, 1, cb:cb + NCG, :], op=ALU.add), r=[pD.buf, Y0.buf], w=[t2.buf])
                E('pool', lambda e: e.tensor_tensor(out=t1.t[:], in0=t1.t[:], in1=t2.t[:], op=ALU.add), r=[t1.buf, t2.buf], w=[t1.buf])
                E('act', lambda e: e.activation(out=vb.t[:], in_=t1.t[:], func=AF.Copy), r=[t1.buf], w=[vb.buf])
                E('act', lambda e: e.activation(out=sqk.t[:], in_=t1.t[:], func=AF.Square), r=[t1.buf], w=[sqk.buf])
                E('pe', lambda e: e.matmul(pA.t[:, 0:4, :].rearrange("p c f -> p (c f)"), on64.t[:], vb.t[:], start=True, stop=True), r=[on64.buf, vb.buf], w=[pA.buf])
                E('pe', lambda e: e.matmul(pB.t[:, 0:4, :].rearrange("p c f -> p (c f)"), on64.t[:], sqk.t[:], start=True, stop=True), r=[on64.buf, sqk.buf], w=[pB.buf])
                mean_ps = lambda: pA.t[:, 0:4, :].rearrange("p c f -> p (c f)")
                sq_ps = lambda: pB.t[:, 0:4, :].rearrange("p c f -> p (c f)")
                E('act', lambda e: e.activation(out=t2.t[:], in_=mean_ps(), func=AF.Copy, scale=1.0 / 64), r=[pA.buf], w=[t2.buf])
                E('dve', lambda e: e.tensor_tensor(out=t3.t[:], in0=t2.t[:], in1=t2.t[:], op=ALU.mult), r=[t2.buf], w=[t3.buf])
                E('dve', lambda e: e.scalar_tensor_tensor(out=t3.t[:], in0=sq_ps(), scalar=1.0 / 64, in1=t3.t[:], op0=ALU.mult, op1=ALU.subtract),
                  r=[pB.buf, t3.buf], w=[t3.buf])
                E('act', lambda e: e.activation(out=t3.t[:], in_=t3.t[:], func=AF.Sqrt, bias=64e-5, scale=1.0), r=[t3.buf], w=[t3.buf])
                E('dve', lambda e: e.reciprocal(out=t3.t[:], in_=t3.t[:]), r=[t3.buf], w=[t3.buf])
                E('pool', lambda e: e.tensor_tensor(out=t1.t[:], in0=t1.t[:], in1=t2.t[:], op=ALU.subtract), r=[t1.buf, t2.buf], w=[t1.buf])
                E('dve', lambda e: e.tensor_tensor(out=t1.t[:], in0=t1.t[:], in1=t3.t[:], op=ALU.mult), r=[t1.buf, t3.buf], w=[t1.buf])
                E('dve', lambda e, h=h: e.tensor_scalar(out=t1.t[:], in0=t1.t[:], scalar1=col('lnxw', h), scalar2=col('lnxb', h), op0=ALU.mult, op1=ALU.add),
                  r=[t1.buf, ca.buf], w=[t1.buf])
                E('pe', lambda e, c0=c0: e.matmul(pA.t[:, 4:8, :].rearrange("p c f -> p (c f)"), on64.t[:], rkk.t[:, c0:c0 + 512], start=True, stop=True),
                  r=[on64.buf, rkk.buf], w=[pA.buf])
                E('dve', lambda e, c0=c0: e.tensor_tensor(out=t2.t[:], in0=pA.t[:, 4:8, :].rearrange("p c f -> p (c f)"), in1=vfull.t[:, c0:c0 + 512], op=ALU.mult),
                  r=[pA.buf, vfull.buf], w=[t2.buf])
                E('pool', lambda e: e.tensor_tensor(out=t1.t[:], in0=t1.t[:], in1=t2.t[:], op=ALU.add), r=[t1.buf, t2.buf], w=[t1.buf])
                y_ = yst[cg % 2]
                E('pool', lambda e, c0=c0, y_=y_: e.tensor_tensor(out=y_.t[:], in0=t1.t[:], in1=sgfull.t[:, c0:c0 + 512], op=ALU.mult),
                  r=[t1.buf, sgfull.buf], w=[y_.buf])
                dma(C, 'sp', C.yT[0][hp, hr:hr + 64, c0:c0 + 512], y_.t[:], reads=[y_.buf], awrites=[C.dbuf('yT0', 0)])
            for cg in range(NTG):
                do_out(cg)
        def do_lora(cg):
            c0 = cg * 512
            for gi, (tile, row0) in enumerate([(24, 0), (24, 64), (25, 0), (25, 64)]):
                load_halo(raw[gi], tile, row0, c0)
                conv(t1, raw[gi], lambda tap, gi=gi: ('convl', gi * 3 + tap))
                kind, z = gi // 2, gi % 2
                func = AF.Tanh if kind == 0 else AF.Copy
                E('act', lambda e, kind=kind, z=z, func=func: e.activation(out=tw[kind][z].t[:], in_=t1.t[:], func=func),
                  r=[t1.buf], w=[tw[kind][z].buf])
                dma(C, 'sp', C.lora16[gi, :, c0:c0 + 512], tw[kind][z].t[:], reads=[tw[kind][z].buf], awrites=[C.dbuf('lora16', 0)])
        for cg in range(NTG):
            do_lora(cg)
        P.barrier()
        for h in range(16):
            do_head(h)
        while C.bg:
            C.bg.pop(0)()
    P.barrier()


def _scan_chain(C, E, GT, Hh, Sall, NCHK, pA, pB):
    ztile = [Buf(), Buf()]
    for z in range(2):
        first = 0 if z == 0 else NCHK - 1
        E('pool', lambda e, z=z, first=first: e.memset(Sall.t[:, z, first, :], 0.0), w=[ztile[z]], aw=[Sall.buf])
    for i in range(NCHK - 1):
        for z in range(2):
            c = i if z == 0 else NCHK - 1 - i
            nxt = c + 1 if z == 0 else c - 1
            pz = pA if z == 0 else pB
            E('pe', lambda e, z=z, c=c, pz=pz: e.matmul(pz.t[:, 0, 0:64], GT.t[:, z, c, :], Sall.t[:, z, c, :], start=True, stop=True),
              r=[GT.buf, ztile[z]], w=[pz.buf])
            E('dve', lambda e, z=z, c=c, nxt=nxt, pz=pz: e.tensor_tensor(out=Sall.t[:, z, nxt, :], in0=pz.t[:, 0, 0:64], in1=Hh.t[:, z, c, :], op=ALU.add),
              r=[pz.buf, Hh.buf], w=[ztile[z]], aw=[Sall.buf])


def phase_C(C, l, T):
    P = C.P
    w = C.w_in[l]
    R = T // 64
    with ExitStack() as es:
        xT = sb(C, es, "xT", [128, KC, TB], BF16, nbuf=4)
        ws = WStream(C, es, KC, 512)
        q_sb = sb(C, es, "cq_sb", [128, 1, 512], F32)
        sq = sb(C, es, "csq", [128, 1, 512], BF16)
        rs = sb(C, es, "crs", [128, 512], F32)
        qst = [sb(C, es, "cqst", [128, 1, 512], BF16) for _ in range(2)]
        stage = [sb(C, es, "cstage", [128, 512], BF16) for _ in range(4)]
        pm = [ps(C, es, "pmc", [128, 512]) for _ in range(4)]
        paux = ps(C, es, "pauxc", [128, 512])
        cnt = [0]
        jobs = []
        for tb in range(T // TB):
            t0 = tb * TB

            def job_qk(slot, j, which, tb=tb, t0=t0):
                if which == 'q' and j == 0:
                    load_xT(C, xT, C.xnT, t0, TB)
                dst = C.qT if which == 'q' else C.kT
                gcol = C.col('cq' if which == 'q' else 'ck', l)
                for ct in range(4):
                    pp = [pm[(cnt[0] % 2) * 2], pm[(cnt[0] % 2) * 2 + 1]]
                    cnt[0] += 1
                    gemm_fm(C, slot, KC, xT, ct, 2, pp)
                    for tg in range(2):
                        o_ = qst[tg]
                        headnorm_fm(C, [pp[tg]], q_sb, sq, paux, rs, [gcol], o_, [0], 64, 1e-6, 512, ones=C.bones)
                        dma(C, 'sp', dst[j * 4 + ct, :, t0 + tg * 512:t0 + (tg + 1) * 512], o_.t[:, 0, :], reads=[o_.buf],
                            awrites=[C.dbuf('cqk', tb)])

            def job_v(slot, j, tb=tb, t0=t0):
                for tt in range(8):
                    k = cnt[0] % 4
                    cnt[0] += 1
                    gemm_tm(C, slot, KC, xT, tt, 512, pm[k])
                    P.emit('act', lambda e, k=k: e.activation(out=stage[k].t[:], in_=pm[k].t[:], func=AF.Copy),
                           reads=[pm[k].buf], writes=[stage[k].buf])
                    r0 = t0 + tt * 128
                    dma(C, 'sp', C.vC[r0:r0 + 128, j * 512:(j + 1) * 512], stage[k].t[:], reads=[stage[k].buf],
                        awrites=[C.dbuf('cv', tb)])

            def job_g(slot, j, tb=tb, t0=t0):
                for ct in range(4):
                    pp = [pm[(cnt[0] % 2) * 2], pm[(cnt[0] % 2) * 2 + 1]]
                    cnt[0] += 1
                    gemm_fm(C, slot, KC, xT, ct, 2, pp)
                    for tg in range(2):
                        k = (cnt[0] * 2 + tg) % 4
                        P.emit('act', lambda e, k=k, p_=pp[tg]: e.activation(out=stage[k].t[:], in_=p_.t[:], func=AF.Silu),
                               reads=[pp[tg].buf], writes=[stage[k].buf])
                        dma(C, 'sp', C.sgC[j * 4 + ct, :, t0 + tg * 512:t0 + (tg + 1) * 512], stage[k].t[:], reads=[stage[k].buf],
                            awrites=[C.dbuf('cg', tb)])
            for j in range(2):
                jobs.append((w[:, O2 + j * 512:O2 + (j + 1) * 512], KC, 512, lambda s, j=j, f=job_qk: f(s, j, 'q')))
            for j in range(2):
                jobs.append((w[:, O2 + BW + j * 512:O2 + BW + (j + 1) * 512], KC, 512, lambda s, j=j, f=job_qk: f(s, j, 'k')))
            for j in range(2):
                jobs.append((w[:, O2 + 2 * BW + j * 512:O2 + 2 * BW + (j + 1) * 512], KC, 512, lambda s, j=j, f=job_v: f(s, j)))
            for j in range(2):
                jobs.append((w[:, O2 + 3 * BW + j * 512:O2 + 3 * BW + (j + 1) * 512], KC, 512, lambda s, j=j, f=job_g: f(s, j)))
        ws.run(jobs)
    P.barrier()
    NT = T // 128
    if C.dbg.get('c_noattn'):
        return
    with ExitStack() as es:
        bufs = []
        for i in range(2):
            bufs.append(dict(
                q0=sb(C, es, "aq0", [128, T], BF16), q1=sb(C, es, "aq1", [128, T], BF16),
                k=sb(C, es, "ak", [128, T], BF16), g=sb(C, es, "ag", [128, T], BF16),
                ve=sb(C, es, "ave", [128, NT, 128], BF16), vo=sb(C, es, "avo", [128, NT, 128], BF16),
                tb=sb(C, es, "atb", [128, 2, 15, 64], F32), y=sb(C, es, "ay", [128, T], BF16)))
        sc = [sb(C, es, "asc", [128, 512], F32) for _ in range(4)]
        E = [sb(C, es, "aE", [128, 2, 4, 64], BF16) for _ in range(4)]
        rden = [sb(C, es, "arden", [128, 64], F32) for _ in range(4)]
        o1 = [sb(C, es, "ao1", [128, 64], F32) for _ in range(4)]
        pS = [ps(C, es, "apS", [128, 2, 4, 64]) for _ in range(4)]
        pOD = [ps(C, es, "apOD", [128, 4, 64]) for _ in range(4)]
        for b in bufs:
            P.emit('pool', lambda e, b=b: e.memset(b['q0'].t[64:128, :], 0.0), writes=[b['q0'].buf])
            P.emit('pool', lambda e, b=b: e.memset(b['q1'].t[0:64, :], 0.0), writes=[b['q1'].buf])
        for ct in range(8):
            b = bufs[ct % 2]
            dma(C, 'sp', b['q0'].t[0:64, :], C.qT[ct, 0:64, :], writes=[b['q0'].buf])
            dma(C, 'sp', b['q1'].t[64:128, :], C.qT[ct, 64:128, :], writes=[b['q1'].buf])
            dma(C, 'sp', b['k'].t[:], C.kT[ct], writes=[b['k'].buf])
            dma(C, 'sp', b['g'].t[:], C.sgC[ct], writes=[b['g'].buf])
            dma(C, 'sp', b['ve'].t[:], C.vC[:, ct * 128:(ct + 1) * 128].rearrange("(n p) c -> p n c", p=128), writes=[b['ve'].buf])
            dma(C, 'sp', b['vo'].t[:, 0:NT - 1, :],
                C.vC[64:T - 64, ct * 128:(ct + 1) * 128].rearrange("(n p) c -> p n c", p=128), writes=[b['vo'].buf])
            dma(C, 'sp', b['tb'].t[:], C.prm['rpbT'][l, ct], writes=[b['tb'].buf])
            for i in range(R):
                k = i % 4
                si = min(max(i - 4, 0), R - 8)
                rel0 = si - i + 7

                def mms(e, b=b, i=i, si=si, k=k):
                    ins = None
                    for h2 in range(2):
                        for kt in range(4):
                            tk = (si + 2 * kt) * 64
                            ins = e.matmul(pS[k].t[:, h2, kt, :], b['k'].t[:, tk:tk + 128],
                                           b['q%d' % h2].t[:, i * 64:(i + 1) * 64], start=True, stop=True)
                    return ins
                lvl = C.dbg.get('c_lvl', 9)
                if lvl < 2:
                    continue
                P.emit('pe', mms, reads=[b['k'].buf, b['q0'].buf, b['q1'].buf], writes=[pS[k].buf])
                if lvl == 21:
                    continue
                for h2 in range(2):
                    P.emit('dve', lambda e, b=b, k=k, rel0=rel0, h2=h2: e.scalar_tensor_tensor(
                        out=sc[k].t[:, h2 * 256:(h2 + 1) * 256].rearrange("p (t q) -> p t q", t=4), in0=pS[k].t[:, h2, :, :], scalar=0.125,
                        in1=b['tb'].t[:, h2, rel0:rel0 + 8:2, :], op0=ALU.mult, op1=ALU.add),
                        reads=[pS[k].buf, b['tb'].buf], awrites=[sc[k].buf])
                if lvl == 22:
                    continue
                P.emit('act', lambda e, k=k: e.activation(out=E[k].t[:].rearrange("p h t q -> p (h t q)"), in_=sc[k].t[:], func=AF.Exp),
                       reads=[sc[k].buf], writes=[E[k].buf])

                if lvl < 3:
                    continue

                def mmo(e, b=b, si=si, k=k):
                    ins = None
                    for h2 in range(2):
                        for kt in range(4):
                            row = si + 2 * kt
                            vt = b['ve'].t[:, row // 2, :] if row % 2 == 0 else b['vo'].t[:, row // 2, :]
                            ins = e.matmul(pOD[k].t[:, h2, :], vt, E[k].t[:, h2, kt, :], start=(kt == 0), stop=(kt == 3))
                    for h2 in range(2):
                        for kt in range(4):
                            ins = e.matmul(pOD[k].t[:, 2 + h2, :], C.ones.t[:], E[k].t[:, h2, kt, :], start=(kt == 0), stop=(kt == 3))
                    return ins
                P.emit('pe', mmo, reads=[b['ve'].buf, b['vo'].buf, E[k].buf, C.ones.buf], writes=[pOD[k].buf])
                for h2 in range(2):
                    sl = slice(h2 * 64, (h2 + 1) * 64)
                    P.emit('dve', lambda e, k=k, sl=sl, h2=h2: e.reciprocal(out=rden[k].t[sl, :], in_=pOD[k].t[sl, 2 + h2, :]),
                           reads=[pOD[k].buf], awrites=[rden[k].buf])
                    P.emit('dve', lambda e, k=k, sl=sl, h2=h2: e.tensor_tensor(out=o1[k].t[sl, :], in0=pOD[k].t[sl, h2, :], in1=rden[k].t[sl, :], op=ALU.mult),
                           reads=[pOD[k].buf, rden[k].buf], awrites=[o1[k].buf])
                P.emit('pool', lambda e, k=k, b=b, i=i: e.tensor_tensor(out=b['y'].t[:, i * 64:(i + 1) * 64], in0=o1[k].t[:],
                                                                          in1=b['g'].t[:, i * 64:(i + 1) * 64], op=ALU.mult),
                       reads=[o1[k].buf, b['g'].buf], awrites=[b['y'].buf])
            dma(C, 'sp', C.yT[2][ct], b['y'].t[:], reads=[b['y'].buf], awrites=[C.dbuf('yT2', 0)])
    P.barrier()


def phase_merge(C, l, T):
    P = C.P
    w = C.w_in[l]
    CW = 256
    with ExitStack() as es:
        xT = sb(C, es, "xT", [128, KC, TB], BF16, nbuf=4)
        yTs = [sb(C, es, "yTs", [128, 8, TB], BF16, nbuf=1) for _ in range(2)]
        wg = WStream(C, es, KC, CW)
        wb = WStream(C, es, 8, CW)
        acc = [[sb(C, es, "acc", [128, 512], F32) for _ in range(2)] for _ in range(2)]
        sig = [sb(C, es, "sig", [128, 512], F32) for _ in range(2)]
        prod = [sb(C, es, "prod", [128, 512], F32) for _ in range(2)]
        ostage = [sb(C, es, "ostage", [128, 512], BF16) for _ in range(4)]
        psg = [ps(C, es, "psg", [128, 512]) for _ in range(4)]
        psp = [ps(C, es, "psp", [128, 512]) for _ in range(4)]
        cnt = [0]
        for tb in range(T // TB):
            t0 = tb * TB
            load_xT(C, xT, C.xnT, t0, TB)
            units = [(cb, n) for cb in range(D // CW) for n in range(4)]
            load_w(C, wg.slots[0], w[:, O4 + units[0][1] * D + units[0][0] * CW:O4 + units[0][1] * D + (units[0][0] + 1) * CW], KC, CW)
            load_w(C, wb.slots[0], C.w_br[l][units[0][1]][:, units[0][0] * CW:(units[0][0] + 1) * CW], 8, CW)
            for ui, (cb, n) in enumerate(units):
                if ui + 1 < len(units):
                    cb2, n2 = units[ui + 1]
                    load_w(C, wg.slots[(ui + 1) % 2], w[:, O4 + n2 * D + cb2 * CW:O4 + n2 * D + (cb2 + 1) * CW], KC, CW)
                    load_w(C, wb.slots[(ui + 1) % 2], C.w_br[l][n2][:, cb2 * CW:(cb2 + 1) * CW], 8, CW)
                gs, bs = wg.slots[ui % 2], wb.slots[ui % 2]
                ys = yTs[ui % 2]
                dma(C, 'sp', ys.t[:, :, :], C.yT[n][:, :, t0:t0 + TB].rearrange("kc p t -> p kc t"), writes=[ys.buf])
                for ct in range(CW // 128):
                    k = cnt[0] % 2
                    cnt[0] += 1
                    pg = [psg[k * 2], psg[k * 2 + 1]]
                    pp = [psp[k * 2], psp[k * 2 + 1]]
                    gemm_fm(C, gs, KC, xT, ct, 2, pg)
                    gemm_fm(C, bs, 8, ys, ct, 2, pp)
                    for tg in range(2):
                        a_ = acc[ct][tg]
                        P.emit('act', lambda e, tg=tg, pg=pg: e.activation(out=sig[tg].t[:], in_=pg[tg].t[:], func=AF.Sigmoid),
                               reads=[pg[tg].buf], writes=[sig[tg].buf])
                        if n == 0:
                            P.emit('dve', lambda e, tg=tg, pp=pp, a_=a_: e.tensor_tensor(out=a_.t[:], in0=sig[tg].t[:], in1=pp[tg].t[:], op=ALU.mult),
                                   reads=[sig[tg].buf, pp[tg].buf], writes=[a_.buf])
                        else:
                            P.emit('dve', lambda e, tg=tg, pp=pp: e.tensor_tensor(out=prod[tg].t[:], in0=sig[tg].t[:], in1=pp[tg].t[:], op=ALU.mult),
                                   reads=[sig[tg].buf, pp[tg].buf], writes=[prod[tg].buf])
                            if n < 3:
                                P.emit('pool', lambda e, tg=tg, a_=a_: e.tensor_tensor(out=a_.t[:], in0=a_.t[:], in1=prod[tg].t[:], op=ALU.add),
                                       reads=[a_.buf, prod[tg].buf], writes=[a_.buf])
                            else:
                                o_ = ostage[(cnt[0] * 2 + tg) % 4]
                                P.emit('pool', lambda e, tg=tg, a_=a_, o_=o_: e.tensor_tensor(out=o_.t[:], in0=a_.t[:], in1=prod[tg].t[:], op=ALU.add),
                                       reads=[a_.buf, prod[tg].buf], writes=[o_.buf])
                                kc = cb * (CW // 128) + ct
                                dma(C, 'sp', C.mT[kc, :, t0 + tg * 512:t0 + (tg + 1) * 512], o_.t[:], reads=[o_.buf],
                                    awrites=[C.dbuf('mT', tb)])
    P.barrier()


def phase_out(C, l, x_src, x_dst, T):
    P = C.P
    with ExitStack() as es:
        mTs = sb(C, es, "mTs", [128, KC, TB], BF16, nbuf=4)
        ws = WStream(C, es, KC, 512)
        xin = [sb(C, es, "xin", [128, 512], F32) for _ in range(4)]
        xo = [sb(C, es, "xo", [128, 512], F32) for _ in range(4)]
        pm = [ps(C, es, "pmo", [128, 512]) for _ in range(4)]
        cnt = [0]
        jobs = []
        for tb in range(T // TB):
            t0 = tb * TB

            def job(slot, cb, tb=tb, t0=t0):
                if cb == 0:
                    load_xT(C, mTs, C.mT, t0, TB)
                for tt in range(TB // 128):
                    k = cnt[0] % 4
                    cnt[0] += 1
                    r0 = t0 + tt * 128
                    dma(C, 'sp', xin[k].t[:], x_src[r0:r0 + 128, cb * 512:(cb + 1) * 512], writes=[xin[k].buf])
                    gemm_tm(C, slot, KC, mTs, tt, 512, pm[k])
                    P.emit('dve', lambda e, k=k: e.tensor_tensor(out=xo[k].t[:], in0=pm[k].t[:], in1=xin[k].t[:], op=ALU.add),
                           reads=[pm[k].buf, xin[k].buf], writes=[xo[k].buf])
                    dma(C, 'sp', x_dst[r0:r0 + 128, cb * 512:(cb + 1) * 512], xo[k].t[:], reads=[xo[k].buf],
                        awrites=[C.dbuf('xout', tb)])
            for cb in range(D // 512):
                jobs.append((C.w_out[l][:, cb * 512:(cb + 1) * 512], KC, 512, lambda s, cb=cb, f=job: f(s, cb)))
        ws.run(jobs)
    P.barrier()


WIN_GROUPS = [(0, O1), (O1, O2), (O2, O3), (O3, O4)] + [(O4 + n * D, O4 + (n + 1) * D) for n in range(4)]
COLS = {'mq': 0, 'mk': 2, 'cq': 4, 'ck': 5, 'conv': 6, 'w0': 84, 'a0': 100, 'kk': 116, 'ka': 124, 'rk': 132,
        'lnxw': 140, 'lnxb': 148}
NCOLS = 160
NCST = 3 * 128


def build_program(T=SEQ, n_layers=DEPTH, dbg=None, NB=1):
    dbg = dbg or {}
    L = n_layers
    nc = bass.Bass("TRN2", target_bir_lowering=False)
    C = Ctx()
    C.nc = nc
    C.uid = 0
    C.T = T
    C.dbg = dbg

    def din(name, shape, dt=F32):
        return nc.dram_tensor(name, list(shape), dt, kind="ExternalInput").ap()

    def dscr(name, shape, dt):
        return nc.dram_tensor(name, list(shape), dt, kind="Internal").ap()

    x_all = din("x", [NB, T, D])
    mem_all = din("mem", [NB, MEM, D])
    nsh = dbg.get('nshard', 0)
    C.wq = 'pool'
    gathers = []
    if nsh == 0:
        C.w_in = din("w_in", [L, D, IN_W])
        C.w_kv = din("w_kv", [L, D, 2 * BW])
        C.w_br = din("w_br", [L, 4, BW, D])
        C.w_out = din("w_out", [L, D, D])
    else:
        def sharded(name, rows, cols):
            src = din(name, [L, rows // nsh, cols])
            outl = []
            for l in range(L):
                bnc = dscr("%s_b%d" % (name, l), [rows // nsh, cols], BF16)
                full = dscr("%s_f%d" % (name, l), [rows, cols], BF16)
                gathers.append((src[l], bnc, full, rows // nsh))
                outl.append(full)
            return outl
        wg = [sharded("win%d" % g, D, hi - lo) for g, (lo, hi) in enumerate(WIN_GROUPS)]
        C.w_in = [WView([(lo, hi, wg[g][l]) for g, (lo, hi) in enumerate(WIN_GROUPS)]) for l in range(L)]
        C.w_kv = sharded("wkv", D, 2 * BW)
        wbr = [sharded("wbr%d" % n, BW, D) for n in range(4)]
        C.w_br = [[wbr[n][l] for n in range(4)] for l in range(L)]
        C.w_out = sharded("wout", D, D)
    C.prm = {
        'g_bc': din("g_bc", [L, 128, D]), 'mg_bc': din("mg_bc", [L, 128, D]),
        'lng_bc': din("lng_bc", [L, 128, BW]), 'lnb_bc': din("lnb_bc", [L, 128, BW]),
        'bs_bc': din("bs_bc", [L, 128, 8, 128]), 'wsT': din("wsT", [L, 128, 8, 128]),
        'cols': din("cols", [L, 128, NCOLS]), 'cst': din("cst", [128, NCST]),
        'rpbT': din("rpbT", [L, 8, 128, 2, 15, 64]),
        'colsA': din("colsA", [L, 64, NCA]), 'cstA': din("cstA", [64, 3136]),
    }
    y_all = nc.dram_tensor("y", [NB, T, D], F32, kind="ExternalOutput").ap()
    C.xnT = dscr("xnT", [KC, 128, T], BF16)
    C.yT = [dscr("yT%d" % n, [8, 128, T], BF16) for n in range(4)]
    C.mT = dscr("mT", [KC, 128, T], BF16)
    C.qT = dscr("qT", [8, 128, T], BF16)
    C.kT = dscr("kT", [8, 128, T], BF16)
    C.sgC = dscr("sgC", [8, 128, T], BF16)
    C.vC = dscr("vC", [T, BW], BF16)
    C.hA = dscr("hA", [34, 128, T], F32)
    C.lora16 = dscr("lora16", [4, 64, T], BF16)
    C.a_w_up = din("a_w_up", [L, 2, 64, BW])
    C.a_a_up = din("a_a_up", [L, 2, 64, BW])
    xbuf = [dscr("xbuf%d" % i, [T, D], F32) for i in range(2)]
    dbg_in = {k: din("in_" + k, [8, 128, T]) for k in dbg.get('yT_in', [])}
    dbg_out = {k: nc.dram_tensor("dbg_" + k, [8, 128, T], F32, kind="ExternalOutput").ap() for k in dbg.get('yT_out', [])}
    if dbg.get('mT_out'):
        dbg_out['mT'] = nc.dram_tensor("dbg_mT", [KC, 128, T], F32, kind="ExternalOutput").ap()
    dbufs = {}

    def dbuf(name, idx):
        k = (name, idx)
        if k not in dbufs:
            dbufs[k] = Buf()
        return dbufs[k]
    C.dbuf = dbuf

    with ExitStack() as es:
        P = Prog(nc, es)
        C.P = P
        C.ident = sb(C, es, "ident", [128, 128], BF16)
        C.ones = sb(C, es, "ones", [128, 128], BF16)
        C.bones = sb(C, es, "bones", [128, 128], BF16)
        C.cols = sb(C, es, "cols", [128, NCOLS], F32)
        C.col = lambda name, l, i=0: C.cols.t[:, COLS[name] + i:COLS[name] + i + 1]
        cst = C.prm['cst']
        dma(C, 'pool', C.ident.t[:], cst[:, 0:128], writes=[C.ident.buf])
        dma(C, 'pool', C.ones.t[:], cst[:, 128:256], writes=[C.ones.buf])
        dma(C, 'pool', C.bones.t[:], cst[:, 256:384], writes=[C.bones.buf])
        for src, bnc, full, rows in gathers:
            for r0 in range(0, rows, 128):
                dma(C, 'pool', bnc[r0:r0 + 128, :], src[r0:r0 + 128, :], awrites=[dbuf('bnc', id(bnc))])
            P.emit('pool', lambda e, bnc=bnc, full=full: e.collective_compute(
                "AllGather", ALU.bypass, replica_groups=[list(range(nsh))], ins=[bnc[:, :]], outs=[full[:, :]]),
                reads=[dbuf('bnc', id(bnc))], dma=True, chain=True)
        for k, ap in dbg_in.items():
            n = int(k[-1])
            for kc in range(8):
                dma(C, 'pool', C.yT[n][kc], ap[kc], awrites=[dbuf('dbgin', 0)])
        P.barrier()
        C.bg = []
        preconv = (nsh == 0) and dbg.get('preconv', True) and L > 1
        if preconv:
            w32 = (C.w_in, C.w_kv, C.w_br, C.w_out)
            w16in = [[dscr("w16_in%d_%d" % (k_, g), [D, hi - lo], BF16) for g, (lo, hi) in enumerate(WIN_GROUPS)] for k_ in range(2)]
            w16 = (None, dscr("w16_kv", [2, D, 2 * BW], BF16),
                   dscr("w16_br", [2, 4, BW, D], BF16), dscr("w16_out", [2, D, D], BF16))

            def conv_jobs(l):
                k = l % 2
                jobs_ = []
                for r0 in range(0, D, 128):
                    for g, (lo, hi) in enumerate(WIN_GROUPS):
                        jobs_.append((w16in[k][g][r0:r0 + 128, :], w32[0][l, r0:r0 + 128, lo:hi]))
                    jobs_.append((w16[1][k, r0:r0 + 128, :], w32[1][l, r0:r0 + 128, :]))
                    jobs_.append((w16[3][k, r0:r0 + 128, :], w32[3][l, r0:r0 + 128, :]))
                for n in range(4):
                    for r0 in range(0, BW, 128):
                        jobs_.append((w16[2][k, n, r0:r0 + 128, :], w32[2][l, n, r0:r0 + 128, :]))
                return [(lambda o=o, i=i: dma(C, 'pool', o, i, awrites=[dbuf('w16', 0)])) for o, i in jobs_]
        for b in range(NB):
            x_in, y_out = x_all[b], y_all[b]
            C.mem = mem_all[b]
            for l in range(L):
                if preconv:
                    if l == 0:
                        C.w_in, C.w_kv, C.w_br, C.w_out = w32
                        C.wq = 'pool'
                    else:
                        C.w_in = [WView([(lo, hi, w16in[l % 2][g]) for g, (lo, hi) in enumerate(WIN_GROUPS)])] * L
                        C.w_kv = [w16[1][l % 2]] * L
                        C.w_br = [w16[2][l % 2]] * L
                        C.w_out = [w16[3][l % 2]] * L
                        C.wq = 'act'
                    C.bg = conv_jobs(l + 1) if l + 1 < L else []
                    C.bg_per = (len(C.bg) + 127) // 128
                x_src = x_in if l == 0 else xbuf[(l - 1) % 2]
                x_dst = y_out if l == L - 1 else xbuf[l % 2]
                dma(C, 'sp', C.cols.t[:], C.prm['cols'][l], writes=[C.cols.buf])
                phase_rmsnorm(C, x_src, C.prm['g_bc'][l], C.xnT, T)
                if 'A' not in dbg.get('skip', ''):
                    phase_A(C, l, T)
                if 'B' not in dbg.get('skip', ''):
                    phase_B(C, l, T)
                if 'C' not in dbg.get('skip', ''):
                    phase_C(C, l, T)
                if 'D' not in dbg.get('skip', ''):
                    phase_D(C, l, T)
                phase_merge(C, l, T)
                phase_out(C, l, x_src, x_dst, T)
        for k, ap in dbg_out.items():
            src = C.mT if k == 'mT' else C.yT[int(k[-1])]
            for kc in range(src.shape[0]):
                dma(C, 'pool', ap[kc], src[kc], awrites=[dbuf('dbgout', 0)])
        P.barrier()
        P.build()
    C.n_ops = P.n_ops
    return nc, C


def host_params(p, L):
    f = np.float32
    rep = lambda v: np.ascontiguousarray(np.broadcast_to(np.asarray(v, f)[:, None, :], (v.shape[0], 128, v.shape[1])))
    out = {}
    out['g_bc'] = rep(p['norm_g'][:L])
    out['mg_bc'] = rep(p['m_norm_g'][:L])
    out['lng_bc'] = rep(p['b_ln_g'][:L])
    out['lnb_bc'] = rep(p['b_ln_b'][:L])
    bs = np.asarray(p['b_b_s'][:L], f)
    out['bs_bc'] = np.ascontiguousarray(np.broadcast_to(bs[:, None, :, :], (L, 128, 8, 128)))
    out['wsT'] = np.ascontiguousarray(np.transpose(np.asarray(p['b_w_s'][:L], f), (0, 3, 1, 2)))
    cols = np.zeros((L, 128, NCOLS), f)
    colv = lambda v, n: np.transpose(np.asarray(v, f).reshape(L, n, 128), (0, 2, 1))
    cols[:, :, COLS['mq']:COLS['mq'] + 2] = colv(p['m_q_norm'][:L], 2)
    cols[:, :, COLS['mk']:COLS['mk'] + 2] = colv(p['m_k_norm'][:L], 2)
    cols[:, :, COLS['cq']] = np.tile(np.asarray(p['c_q_norm'][:L], f), (1, 2))
    cols[:, :, COLS['ck']] = np.tile(np.asarray(p['c_k_norm'][:L], f), (1, 2))
    conv = np.asarray(p['a_conv'][:L], f)
    cols[:, :, COLS['conv']:COLS['conv'] + 78] = np.transpose(conv.reshape(L, 3, 26, 128), (0, 3, 2, 1)).reshape(L, 128, 78)
    cols[:, :, COLS['w0']:COLS['w0'] + 16] = np.transpose(np.asarray(p['a_w0'][:L], f).reshape(L, 16, 128), (0, 2, 1))
    cols[:, :, COLS['a0']:COLS['a0'] + 16] = np.transpose(np.asarray(p['a_a0'][:L], f).reshape(L, 16, 128), (0, 2, 1))
    cols[:, :, COLS['kk']:COLS['kk'] + 8] = colv(p['a_k_k'][:L], 8)
    cols[:, :, COLS['ka']:COLS['ka'] + 8] = colv(p['a_k_a'][:L], 8)
    cols[:, :, COLS['rk']:COLS['rk'] + 8] = colv(np.asarray(p['a_r_k'][:L]).reshape(L, BW), 8)
    cols[:, :, COLS['lnxw']:COLS['lnxw'] + 8] = colv(p['a_lnx_w'][:L], 8)
    cols[:, :, COLS['lnxb']:COLS['lnxb'] + 8] = colv(p['a_lnx_b'][:L], 8)
    out['cols'] = cols
    rpb = np.asarray(p['c_rpb'][:L], f)
    qv = np.arange(64)[None, :]
    kv = np.arange(64)[:, None]
    dcol = np.clip(kv - qv, -15, 15) + 15
    sj = np.clip(qv - 8, 0, 48)
    valid = (kv >= sj) & (kv < sj + 16)
    tbl = rpb[:, :, :, dcol]
    tbl = np.where(valid[None, None, None], tbl, f(-1e30))
    lo = np.transpose(tbl, (0, 1, 3, 2, 4))
    hi = np.full_like(lo, f(-1e30))
    hi[:, :, :, 0:14, :] = lo[:, :, :, 1:15, :]
    t2 = np.concatenate([lo, hi], axis=2)
    out['rpbT'] = np.ascontiguousarray(np.transpose(t2.reshape(L, 8, 2, 128, 15, 64), (0, 1, 3, 2, 4, 5)))
    ca = np.zeros((L, 64, NCA), f)
    hd = lambda v: np.transpose(np.asarray(v, f).reshape(L, -1, 64), (0, 2, 1))
    for wi in range(3):
        for tap in range(3):
            ca[:, :, CA['conv'] + (wi * 16) * 3 + tap:CA['conv'] + (wi * 16 + 16) * 3:3] = hd(conv[:, tap, wi * BW:(wi + 1) * BW])
    for gi in range(4):
        for tap in range(3):
            ca[:, :, CA['convl'] + gi * 3 + tap] = conv[:, tap, 3 * BW + gi * 64:3 * BW + (gi + 1) * 64]
    ca[:, :, CA['w0']:CA['w0'] + 32] = hd(np.asarray(p['a_w0'][:L], f).reshape(L, 2 * BW))
    ca[:, :, CA['a0']:CA['a0'] + 32] = hd(np.asarray(p['a_a0'][:L], f).reshape(L, 2 * BW))
    ca[:, :, CA['kk']:CA['kk'] + 16] = hd(p['a_k_k'][:L])
    ca[:, :, CA['ka']:CA['ka'] + 16] = hd(p['a_k_a'][:L])
    ca[:, :, CA['rk']:CA['rk'] + 16] = hd(np.asarray(p['a_r_k'][:L], f).reshape(L, BW))
    ca[:, :, CA['lnxw']:CA['lnxw'] + 16] = hd(p['a_lnx_w'][:L])
    ca[:, :, CA['lnxb']:CA['lnxb'] + 16] = hd(p['a_lnx_b'][:L])
    out['colsA'] = ca
    out['a_w_up'] = np.ascontiguousarray(np.asarray(p['a_w_up'][:L], f))
    out['a_a_up'] = np.ascontiguousarray(np.asarray(p['a_a_up'][:L], f))
    ka = np.zeros((64, 3136), f)
    ka[:, 0:512] = 1.0
    ka[:, 0:512:64] = 0.0
    pi = np.arange(64)[:, None]
    fi = np.arange(64)[None, :]
    m2f = np.concatenate([(pi < fi), (pi <= fi)], axis=1).astype(f)
    m2b = np.concatenate([(pi > fi), (pi >= fi)], axis=1).astype(f)
    ka[:, 512:1536] = np.tile(m2f, (1, 8))
    ka[:, 1536:2560] = np.tile(m2b, (1, 8))
    ka[:, 2560:3072] = np.tile(np.eye(64, dtype=f), (1, 8))
    ka[:, 3072:3136] = 1.0
    out['cstA'] = ka
    cst = np.zeros((128, NCST), f)
    cst[:, 0:128] = np.eye(128, dtype=f)
    cst[:, 128:256] = 1.0
    cst[0:64, 256:320] = 1.0
    cst[64:128, 320:384] = 1.0
    out['cst'] = cst
    return out


N_CORES = 4
NB_PER_CORE = 1


def kernel(**inputs):
    p = {k: np.asarray(v) for k, v in inputs.items()}
    L = DEPTH
    nc, C = build_program(T=SEQ, n_layers=L, dbg={}, NB=NB_PER_CORE)
    hp = host_params(p, L)
    shared = dict(hp)
    shared['w_in'] = np.ascontiguousarray(p['w_in'], dtype=np.float32)
    shared['w_kv'] = np.ascontiguousarray(p['m_w_kv'], dtype=np.float32)
    shared['w_br'] = np.ascontiguousarray(p['w_branch'], dtype=np.float32)
    shared['w_out'] = np.ascontiguousarray(p['w_out'], dtype=np.float32)
    in_maps = []
    for c in range(N_CORES):
        m = dict(shared)
        m['x'] = np.ascontiguousarray(p['x'][c * NB_PER_CORE:(c + 1) * NB_PER_CORE], dtype=np.float32)
        m['mem'] = np.ascontiguousarray(p['mem'][c * NB_PER_CORE:(c + 1) * NB_PER_CORE], dtype=np.float32)
        in_maps.append(m)
    res = run_bass_kernel_spmd(nc, in_maps, core_ids=list(range(N_CORES)))
    return np.concatenate([np.asarray(r['y'], dtype=np.float32) for r in res.results], axis=0)
```
